# Optimizing a Trainium2 kernel written in Bass

```python
import jax, jax.numpy as jnp
from jax import lax
import numpy as np

D_MODEL = 2048
BATCH = 1
SEQ = 8192
DEPTH = 4

CTX_LEN = 256
GRID_W = 64
EPS = 1e-6

POOL_WINDOWS = (2, 4, 8, 16)
POOL_WIDTH = D_MODEL // 2
POOL_GROUP = POOL_WIDTH // len(POOL_WINDOWS)
CONV_WIDTH = D_MODEL // 2
CONV_HEADS = 8
CONV_K = 3
EVEN_IN = POOL_WIDTH + 3 * CONV_WIDTH

MLA_HEADS = 16
QK_NOPE = 128
QK_ROPE = 64
V_DIM = 128
Q_LORA = 512
KV_LORA = 512
ROPE_THETA = 10000.0
Q_BLOCK = 128

D_FF = ((8 * D_MODEL // 3 + 255) // 256) * 256

N_EVEN = (DEPTH + 1) // 2
N_ODD = DEPTH // 2

kernel_name = "hybrid_pool_conv_mla_dit_prefix"


def rmsnorm(x, g):
    xf = x.astype(jnp.float32)
    y = xf * lax.rsqrt(jnp.mean(xf * xf, axis=-1, keepdims=True) + EPS)
    return (y * g.astype(jnp.float32)).astype(x.dtype)


def modulate(h, shift, scale):
    return h * (1 + scale) + shift


def ada_mod(cond, w, b):
    return jnp.split(jax.nn.silu(cond) @ w + b, 6, axis=-1)


def swiglu(h, w_gate, w_up, w_down):
    return (jax.nn.silu(h @ w_gate) * (h @ w_up)) @ w_down


def centred_pool_minus_self(u):
    n = u.shape[1]
    uf = u.astype(jnp.float32)
    cs = jnp.concatenate([jnp.zeros_like(uf[:, :1]), jnp.cumsum(uf, axis=1)], axis=1)
    t = jnp.arange(n)
    outs = []
    for g, w in enumerate(POOL_WINDOWS):
        lo = jnp.clip(t - w // 2, 0, n)
        hi = jnp.clip(t + (w - w // 2), 0, n)
        csg = cs[..., g * POOL_GROUP:(g + 1) * POOL_GROUP]
        s = jnp.take(csg, hi, axis=1) - jnp.take(csg, lo, axis=1)
        cnt = (hi - lo).astype(jnp.float32)[None, :, None]
        outs.append(s / cnt)
    pooled = jnp.concatenate(outs, axis=-1)
    return (pooled - uf).astype(u.dtype)


def pool_conv_mixer(h, w_in, pool_w, pool_scale, conv_w, w_out):
    bsz, n, _ = h.shape
    z = h @ w_in
    u_pool, gate_b, gate_c, v = jnp.split(
        z, [POOL_WIDTH, POOL_WIDTH + CONV_WIDTH, POOL_WIDTH + 2 * CONV_WIDTH], axis=-1)
    p = centred_pool_minus_self(u_pool).reshape(bsz, n, len(POOL_WINDOWS), POOL_GROUP)
    y_a = jnp.einsum('bngc,gcd->bngd', p, pool_w).reshape(bsz, n, POOL_WIDTH) * pool_scale
    u = gate_c * v
    pad = CONV_K // 2
    up = jnp.pad(u, ((0, 0), (pad, pad), (0, 0)))
    conv = sum(up[:, k:k + n] * conv_w[k] for k in range(CONV_K))
    y_b = gate_b * conv
    return jnp.concatenate([y_a, y_b], axis=-1) @ w_out


def axial_tables(n, dtype):
    rows = n // GRID_W
    row = jnp.repeat(jnp.arange(rows), GRID_W).astype(jnp.float32)
    col = jnp.tile(jnp.arange(GRID_W), rows).astype(jnp.float32)
    nf = QK_ROPE // 4
    inv = jnp.power(jnp.float32(ROPE_THETA), -jnp.arange(nf, dtype=jnp.float32) / nf)
    ar = row[:, None] * inv
    ac = col[:, None] * inv
    return (jnp.cos(ar).astype(dtype), jnp.sin(ar).astype(dtype),
            jnp.cos(ac).astype(dtype), jnp.sin(ac).astype(dtype))


def rope_half(t, cos, sin):
    half = t.shape[-1] // 2
    t1, t2 = t[..., :half], t[..., half:]
    return jnp.concatenate([t1 * cos - t2 * sin, t2 * cos + t1 * sin], axis=-1)


def axial_rope(t, cos_r, sin_r, cos_c, sin_c):
    h = QK_ROPE // 2
    return jnp.concatenate([rope_half(t[..., :h], cos_r, sin_r),
                            rope_half(t[..., h:], cos_c, sin_c)], axis=-1)


def mla_query(h, w_dq, q_norm_g, w_uq):
    bsz, n, _ = h.shape
    q = (rmsnorm(h @ w_dq, q_norm_g) @ w_uq).reshape(bsz, n, MLA_HEADS, QK_NOPE + QK_ROPE)
    return q[..., :QK_NOPE], q[..., QK_NOPE:]


def mla_keyvalue(h, w_dkv, kv_norm_g, w_ukv):
    bsz, n, _ = h.shape
    kv_a = h @ w_dkv
    c_kv = rmsnorm(kv_a[..., :KV_LORA], kv_norm_g)
    k_rope = kv_a[..., KV_LORA:]
    kv = (c_kv @ w_ukv).reshape(bsz, n, MLA_HEADS, QK_NOPE + V_DIM)
    return kv[..., :QK_NOPE], k_rope, kv[..., QK_NOPE:]


def mla_attend(q_nope, q_rope, k_nope, k_rope, v):
    scale = (QK_NOPE + QK_ROPE) ** -0.5
    s = (jnp.einsum('bqhd,bkhd->bhqk', q_nope, k_nope)
         + jnp.einsum('bqhr,bkr->bhqk', q_rope, k_rope))
    p = jax.nn.softmax(s.astype(jnp.float32) * scale, axis=-1).astype(v.dtype)
    return jnp.einsum('bhqk,bkhd->bqhd', p, v)


def blocked_attend(q_nope, q_rope, k_nope, k_rope, v):
    bsz, n, nh, _ = q_nope.shape
    nb = n // Q_BLOCK

    def to_blocks(t):
        return jnp.moveaxis(t.reshape(bsz, nb, Q_BLOCK, *t.shape[2:]), 1, 0)

    out = lax.map(lambda qs: mla_attend(qs[0], qs[1], k_nope, k_rope, v),
                  (to_blocks(q_nope), to_blocks(q_rope)))
    return jnp.moveaxis(out, 0, 1).reshape(bsz, n, nh, V_DIM)


def mla_mixer(h, hc, need_ctx_out, w_dq, q_norm_g, w_uq, w_dkv, kv_norm_g, w_ukv, w_o):
    bsz, n, _ = h.shape
    cr, sr, cc, sc = axial_tables(n, h.dtype)
    qn, qr = mla_query(h, w_dq, q_norm_g, w_uq)
    qr = axial_rope(qr, cr[:, None], sr[:, None], cc[:, None], sc[:, None])
    kn, kr, v = mla_keyvalue(h, w_dkv, kv_norm_g, w_ukv)
    kr = axial_rope(kr, cr, sr, cc, sc)
    kn_c, kr_c, v_c = mla_keyvalue(hc, w_dkv, kv_norm_g, w_ukv)
    kn_all = jnp.concatenate([kn, kn_c], axis=1)
    kr_all = jnp.concatenate([kr, kr_c], axis=1)
    v_all = jnp.concatenate([v, v_c], axis=1)
    y = blocked_attend(qn, qr, kn_all, kr_all, v_all).reshape(bsz, n, MLA_HEADS * V_DIM) @ w_o
    yc = None
    if need_ctx_out:
        qn_c, qr_c = mla_query(hc, w_dq, q_norm_g, w_uq)
        lc = hc.shape[1]
        yc = mla_attend(qn_c, qr_c, kn_c, kr_c, v_c).reshape(bsz, lc, MLA_HEADS * V_DIM) @ w_o
    return y, yc


def setup_inputs(seed: int = 0) -> dict:
    key = jax.random.key(seed)
    ks = iter(jax.random.split(key, 32))

    def nrm(shape, fan_in, mult=1.0):
        return jax.random.normal(next(ks), shape, jnp.float32) * (mult * fan_in ** -0.5)

    def gain(shape):
        return 1.0 + 0.02 * jax.random.normal(next(ks), shape, jnp.float32)

    x = jax.random.normal(next(ks), (BATCH, SEQ, D_MODEL), jnp.float32)
    c = jax.random.normal(next(ks), (BATCH, D_MODEL), jnp.float32)
    ctx = jax.random.normal(next(ks), (BATCH, CTX_LEN, D_MODEL), jnp.float32)
    c_ctx = jax.random.normal(next(ks), (D_MODEL,), jnp.float32)
    return {
        'x': x, 'c': c, 'ctx': ctx, 'c_ctx': c_ctx,
        'ada_w': nrm((DEPTH, D_MODEL, 6 * D_MODEL), D_MODEL, 0.5),
        'ada_b': 0.02 * jax.random.normal(next(ks), (DEPTH, 6 * D_MODEL), jnp.float32),
        'norm1_g': gain((DEPTH, D_MODEL)),
        'norm2_g': gain((DEPTH, D_MODEL)),
        'even_w_in': nrm((N_EVEN, D_MODEL, EVEN_IN), D_MODEL),
        'pool_w': nrm((N_EVEN, len(POOL_WINDOWS), POOL_GROUP, POOL_GROUP), POOL_GROUP),
        'pool_scale': gain((N_EVEN, POOL_WIDTH)),
        'conv_w': nrm((N_EVEN, CONV_K, CONV_WIDTH), CONV_K),
        'even_w_out': nrm((N_EVEN, POOL_WIDTH + CONV_WIDTH, D_MODEL), POOL_WIDTH + CONV_WIDTH),
        'mla_w_dq': nrm((N_ODD, D_MODEL, Q_LORA), D_MODEL),
        'mla_q_norm_g': gain((N_ODD, Q_LORA)),
        'mla_w_uq': nrm((N_ODD, Q_LORA, MLA_HEADS * (QK_NOPE + QK_ROPE)), Q_LORA),
        'mla_w_dkv': nrm((N_ODD, D_MODEL, KV_LORA + QK_ROPE), D_MODEL),
        'mla_kv_norm_g': gain((N_ODD, KV_LORA)),
        'mla_w_ukv': nrm((N_ODD, KV_LORA, MLA_HEADS * (QK_NOPE + V_DIM)), KV_LORA),
        'mla_w_o': nrm((N_ODD, MLA_HEADS * V_DIM, D_MODEL), MLA_HEADS * V_DIM),
        'ffn_w_gate': nrm((DEPTH, D_MODEL, D_FF), D_MODEL),
        'ffn_w_up': nrm((DEPTH, D_MODEL, D_FF), D_MODEL),
        'ffn_w_down': nrm((DEPTH, D_FF, D_MODEL), D_FF),
        'final_norm_g': gain((D_MODEL,)),
    }


def reference(x, c, ctx, c_ctx, ada_w, ada_b, norm1_g, norm2_g, even_w_in, pool_w, pool_scale,
              conv_w, even_w_out, mla_w_dq, mla_q_norm_g, mla_w_uq, mla_w_dkv, mla_kv_norm_g,
              mla_w_ukv, mla_w_o, ffn_w_gate, ffn_w_up, ffn_w_down, final_norm_g):
    x_lat, x_ctx = x, ctx
    for layer in range(DEPTH):
        last = layer == DEPTH - 1
        odd = layer % 2 == 1
        i = layer // 2
        sh1, sc1, g1, sh2, sc2, g2 = [p[:, None, :] for p in ada_mod(c, ada_w[layer], ada_b[layer])]
        h = modulate(rmsnorm(x_lat, norm1_g[layer]), sh1, sc1)
        hc = None
        if odd or not last:
            csh1, csc1, cg1, csh2, csc2, cg2 = ada_mod(c_ctx, ada_w[layer], ada_b[layer])
            hc = modulate(rmsnorm(x_ctx, norm1_g[layer]), csh1, csc1)
        if odd:
            y, yc = mla_mixer(h, hc, not last, mla_w_dq[i], mla_q_norm_g[i], mla_w_uq[i],
                              mla_w_dkv[i], mla_kv_norm_g[i], mla_w_ukv[i], mla_w_o[i])
        else:
            y = pool_conv_mixer(h, even_w_in[i], pool_w[i], pool_scale[i], conv_w[i], even_w_out[i])
            yc = None
            if not last:
                yc = pool_conv_mixer(hc, even_w_in[i], pool_w[i], pool_scale[i], conv_w[i], even_w_out[i])
        x_lat = x_lat + g1 * y
        h2 = modulate(rmsnorm(x_lat, norm2_g[layer]), sh2, sc2)
        x_lat = x_lat + g2 * swiglu(h2, ffn_w_gate[layer], ffn_w_up[layer], ffn_w_down[layer])
        if not last:
            x_ctx = x_ctx + cg1 * yc
            h2c = modulate(rmsnorm(x_ctx, norm2_g[layer]), csh2, csc2)
            x_ctx = x_ctx + cg2 * swiglu(h2c, ffn_w_gate[layer], ffn_w_up[layer], ffn_w_down[layer])
    return rmsnorm(x_lat, final_norm_g)
```

```python
import numpy as np
import ml_dtypes
from contextlib import ExitStack
import concourse.bass as bass
import concourse.mybir as mybir
from concourse.bass_utils import run_bass_kernel_spmd

F32 = mybir.dt.float32
BF16 = mybir.dt.bfloat16
AF = mybir.ActivationFunctionType
ALU = mybir.AluOpType
NPBF16 = ml_dtypes.bfloat16

NCORES = 8
D = 2048
KC = 16
SEQ = 8192
CTX = 256
HALO = 16
OWN = SEQ // NCORES
WIN = OWN + 2 * HALO
T = WIN + CTX
BLOCKS = [(0, 352, 0), (352, 352, 0), (704, 352, 0), (1056, 256, 1)]
DFF = 5632
FC = DFF // 128
NKEYS = SEQ + CTX
NKC = NKEYS // 128
EPS = 1e-6
NH = 16
GRID_W = 64
ENGS = ("sp", "act", "dve", "pool", "pe")
KD = 6


class Op:
    __slots__ = ("eng", "fn", "deps", "flag", "sem", "val", "dma")


class Sched:
    def __init__(self):
        self.ops = {e: [] for e in ENGS}
        self.lastw = {}
        self.rd = {}
        self.dmal = {e: [] for e in ENGS}
        self.lastreal = {e: None for e in ENGS}

    def add(self, eng, fn, reads=(), writes=(), dma=False):
        op = Op()
        op.eng, op.fn, op.dma, op.flag = eng, fn, dma, dma
        op.sem = None
        op.val = 0
        deps = []
        for k in reads:
            w = self.lastw.get(k)
            if w is not None:
                deps.append(w)
        for k in writes:
            w = self.lastw.get(k)
            if w is not None:
                deps.append(w)
            deps.extend(self.rd.get(k, ()))
        f = []
        for d in deps:
            if d.eng == "pe" and eng == "pe" and not d.dma and not dma:
                continue
            d.flag = True
            f.append(d)
        op.deps = f
        for k in writes:
            self.lastw[k] = op
            self.rd[k] = []
        for k in reads:
            lst = self.rd.setdefault(k, [])
            if not dma:
                for i, o in enumerate(lst):
                    if o.eng == eng and not o.dma:
                        lst[i] = op
                        break
                else:
                    lst.append(op)
            else:
                lst.append(op)
        self.ops[eng].append(op)
        if dma:
            self.dmal[eng].append(op)
        elif fn is not None:
            self.lastreal[eng] = op
        return op

    def barrier(self):
        deps = []
        for e in ENGS:
            if self.lastreal[e] is not None:
                deps.append(self.lastreal[e])
            deps.extend(self.dmal[e][-KD:])
        for d in deps:
            d.flag = True
        for e in ENGS:
            op = Op()
            op.eng, op.fn, op.dma, op.flag, op.sem, op.val = e, None, False, False, None, 0
            op.deps = [d for d in deps if not (d.eng == e and not d.dma)]
            self.ops[e].append(op)
        self.lastw = {}
        self.rd = {}

    def finalize(self, nc, stack):
        self.sem = {e: stack.enter_context(nc.semaphore("s_" + e)) for e in ENGS}
        self.dsem = {e: [stack.enter_context(nc.semaphore("d_%s_%d" % (e, i))) for i in range(KD)] for e in ENGS}
        for e in ENGS:
            cnt = 0
            dl = self.dmal[e]
            nd = 0
            for op in self.ops[e]:
                if op.dma:
                    assert dl[nd] is op
                    op.sem = self.dsem[e][nd % KD]
                    op.val = 16 * (nd // KD + 1)
                    if nd >= KD:
                        op.deps.append(dl[nd - KD])
                    nd += 1
                elif op.flag:
                    cnt += 1
                    op.sem = self.sem[e]
                    op.val = cnt

    def emit(self, ename, eng):
        waited = {}
        for op in self.ops[ename]:
            need = {}
            for d in op.deps:
                key = d.sem
                if d.val > need.get(key, (None, 0))[1]:
                    need[key] = (d.sem, d.val)
            for key, (sem, val) in need.items():
                if waited.get(key, 0) < val:
                    eng.wait_ge(sem, val)
                    waited[key] = val
            ins = op.fn(eng) if op.fn is not None else None
            if op.flag:
                ins.then_inc(op.sem, 16 if op.dma else 1)


class Prog:
    def __init__(self):
        self.nc = bass.Bass("TRN2", target_bir_lowering=False)
        self.S = Sched()
        self.stack = ExitStack()
        self.AE = 51200
        self.arena = self.stack.enter_context(self.nc.sbuf_tensor("arena", [128, self.AE], F32))
        self.ps = [self.stack.enter_context(self.nc.psum_tensor("ps%d" % i, [128, 512], F32)) for i in range(8)]
        self.off = 0
        self.uid = 0
        self.inputs = {}
        self.psrot = 0

    def din(self, name, shape, dt=F32):
        self.inputs[name] = (tuple(shape), dt)
        return self.nc.dram_tensor(name, list(shape), dt, kind="ExternalInput").ap()

    def dbg(self, name, view, key, shape, dt=F32):
        d = self.nc.dram_tensor(name, [128] + list(shape), dt, kind="ExternalOutput").ap()
        self.S.add("sp", (lambda e: e.dma_start(out=d, in_=view)), reads=(key,), writes=("dbg_" + name,), dma=True)

    def dout(self, name, shape, dt=F32):
        return self.nc.dram_tensor(name, list(shape), dt, kind="ExternalOutput").ap()

    def dint(self, name, shape, dt=F32):
        return self.nc.dram_tensor(name, list(shape), dt).ap()

    def mark(self):
        return self.off

    def release(self, m):
        self.S.barrier()
        self.off = m

    def alloc(self, dt, *free):
        n = 1
        for f in free:
            n *= f
        nbytes = n * (2 if dt is BF16 else 4)
        nbytes = (nbytes + 63) // 64 * 64
        o = self.off
        self.off += nbytes
        assert self.off <= self.AE * 4, "SBUF arena overflow %d" % self.off
        a = self.arena[:, o // 4:(o + nbytes) // 4]
        if dt is BF16:
            a = a.bitcast(BF16)
        a = a[:, 0:n]
        if len(free) == 2:
            a = a.rearrange("p (a b) -> p a b", a=free[0])
        elif len(free) == 3:
            a = a.rearrange("p (a b c) -> p a b c", a=free[0], b=free[1])
        self.uid += 1
        return a, "b%d" % self.uid

    def barrier(self):
        self.S.barrier()

    def psbank(self, banks):
        b = banks[self.psrot % len(banks)]
        self.psrot += 1
        return b


def A_(eng, fn, reads=(), writes=(), dma=False, P=None):
    return P.S.add(eng, fn, reads, writes, dma)


def cast_op(S, eng, out_ap, in_ap, reads, writes):
    if eng == "act":
        S.add("act", (lambda e: e.activation(out=out_ap, in_=in_ap, func=AF.Identity)), reads=reads, writes=writes)
    else:
        S.add(eng, (lambda e: e.tensor_copy(out=out_ap, in_=in_ap)), reads=reads, writes=writes)


def linear(P, in_v, in_key, kcn, panels, blocks, epi, pre_chunk=None, post_chunk=None, post_panel=None,
           banks=(0, 1, 2, 3), nbuf=2, tag="w", cast=("dve", "act")):
    S = P.S
    maxc = max(sum(s[1] for s in p) for p in panels)
    m0 = P.mark()
    st = [P.alloc(F32, kcn, maxc) for _ in range(nbuf)]
    wb = [P.alloc(BF16, kcn, maxc) for _ in range(nbuf)]

    def load(pi):
        sv, sk = st[pi % nbuf]
        bv, bk = wb[pi % nbuf]
        c0 = 0
        for (wap, ncol) in panels[pi]:
            src = wap.rearrange("(kc p) n -> p kc n", p=128)
            dst = sv[:, :, c0:c0 + ncol]
            S.add("sp", (lambda e, d=dst, s=src: e.dma_start(out=d, in_=s)), reads=(), writes=(sk,), dma=True)
            c0 += ncol
        cast_op(S, cast[pi % len(cast)], bv[:, :, 0:c0], sv[:, :, 0:c0], (sk,), (bk,))

    load(0)
    for pi in range(len(panels)):
        if pi + 1 < len(panels):
            load(pi + 1)
        bv, bk = wb[pi % nbuf]
        c0 = 0
        ci = 0
        for (wap, ncol) in panels[pi]:
            for cc in range(0, ncol, 128):
                M = min(128, ncol - cc)
                if pre_chunk:
                    pre_chunk(pi, ci)
                for bi, (o, n, v) in enumerate(blocks):
                    b = P.psbank(banks)
                    pk = ("ps", b)
                    psap = P.ps[b][0:M, 0:n]

                    def mm(e, psap=psap, bv=bv, c=c0 + cc, M=M, o=o, n=n):
                        ins = None
                        for kc in range(kcn):
                            ins = e.matmul(psap, lhsT=bv[:, kc, c:c + M], rhs=in_v[:, kc, o:o + n],
                                           start=(kc == 0), stop=(kc == kcn - 1))
                        return ins
                    S.add("pe", mm, reads=(bk, in_key), writes=(pk,))
                    epi(pi, ci, bi, psap, pk, M)
                if post_chunk:
                    post_chunk(pi, ci)
                ci += 1
            c0 += ncol
        if post_panel:
            post_panel(pi)
    P.release(m0)


def split_cols(w, c0, c1, step):
    return [[(w[:, c:min(c + step, c1)], min(c + step, c1) - c)] for c in range(c0, c1, step)]


def load_small(P, dst, key, src):
    P.S.add("sp", (lambda e: e.dma_start(out=dst, in_=src)), writes=(key,), dma=True)


def rms_stats(P, src3, src_key, nchunk, o, n, sq, sqk, rstd, rk, ones, bank, inv_n):
    S = P.S
    S.add("act", (lambda e: e.activation(out=sq[:, 0:nchunk, 0:n], in_=src3[:, 0:nchunk, o:o + n], func=AF.Square)),
          reads=(src_key,), writes=(sqk,))
    pk = ("ps", bank)
    psap = P.ps[bank][:, 0:n]

    def mm(e):
        ins = None
        for c in range(nchunk):
            ins = e.matmul(psap, lhsT=ones, rhs=sq[:, c, 0:n], start=(c == 0), stop=(c == nchunk - 1))
        return ins
    S.add("pe", mm, reads=(sqk, "const"), writes=(pk,))
    S.add("dve", (lambda e: e.tensor_scalar(out=rstd[:, 0:n], in0=psap, scalar1=inv_n, scalar2=EPS,
                                            op0=ALU.mult, op1=ALU.add)), reads=(pk,), writes=(rk,))
    S.add("act", (lambda e: e.sqrt(out=rstd[:, 0:n], in_=rstd[:, 0:n])), reads=(rk,), writes=(rk,))
    S.add("dve", (lambda e: e.reciprocal(out=rstd[:, 0:n], in_=rstd[:, 0:n])), reads=(rk,), writes=(rk,))


def norm_phase(P, C, xd, Amod, Bmod, h, hk, blocks, mask3=None):
    S = P.S
    m0 = P.mark()
    xs = [P.alloc(F32, KC, 352) for _ in range(2)]
    sq, sqk = P.alloc(BF16, KC, 352)
    rs = [P.alloc(F32, 352) for _ in range(2)]
    tmp = [P.alloc(F32, 352) for _ in range(4)]
    xr = xd.rearrange("(c p) t -> p c t", p=128)
    ti = 0
    for bi, (o, n, v) in enumerate(blocks):
        xv, xk = xs[bi % 2]
        rv, rk = rs[bi % 2]
        S.add("sp", (lambda e, xv=xv, o=o, n=n: e.dma_start(out=xv[:, :, 0:n], in_=xr[:, :, o:o + n])),
              reads=("xd",), writes=(xk,), dma=True)
        rms_stats(P, xv, xk, KC, 0, n, sq, sqk, rv, rk, C["ones"], 7, 1.0 / D)
        for c in range(KC):
            tv, tk = tmp[ti % 4]
            ti += 1
            S.add("dve", (lambda e, tv=tv, xv=xv, c=c, n=n, v=v, rv=rv: e.scalar_tensor_tensor(
                out=tv[:, 0:n], in0=xv[:, c, 0:n], scalar=Amod[:, c, v:v + 1], in1=rv[:, 0:n],
                op0=ALU.mult, op1=ALU.mult)), reads=(xk, rk, "mod"), writes=(tk,))
            if Bmod is not None:
                S.add("act", (lambda e, tv=tv, c=c, o=o, n=n, v=v: e.activation(
                    out=h[:, c, o:o + n], in_=tv[:, 0:n], func=AF.Identity, bias=Bmod[:, c, v:v + 1], scale=1.0)),
                    reads=(tk, "mod"), writes=(hk,))
            else:
                S.add("act", (lambda e, tv=tv, c=c, o=o, n=n: e.activation(
                    out=h[:, c, o:o + n], in_=tv[:, 0:n], func=AF.Identity)), reads=(tk,), writes=(hk,))
    if mask3 is not None:
        for (a, b, ma) in ((0, HALO, 0), (WIN - HALO, WIN, HALO)):
            S.add("dve", (lambda e, a=a, b=b, ma=ma: e.tensor_tensor(
                out=h[:, :, a:b], in0=h[:, :, a:b], in1=mask3[:, :, ma:ma + HALO], op=ALU.mult)),
                reads=(hk, "const"), writes=(hk,))
    P.release(m0)


class Resid:
    def __init__(self, P, xd, gmod, blocks):
        self.P, self.xd, self.g, self.blocks = P, xd, gmod, blocks
        self.xt = [P.alloc(F32, T) for _ in range(2)]
        self.n = 0
        self.lo = min(b[0] for b in blocks)
        self.hi = max(b[0] + b[1] for b in blocks)

    def pre(self, pi, ci):
        self.cur = self.xt[self.n % 2]
        self.c = self.n
        self.n += 1
        xv, xk = self.cur
        c = self.c
        self.P.S.add("sp", (lambda e: e.dma_start(out=xv[:, self.lo:self.hi],
                                                  in_=self.xd[c * 128:(c + 1) * 128, self.lo:self.hi])),
                     reads=("xd",), writes=(xk,), dma=True)

    def epi(self, pi, ci, bi, ps, pk, M):
        xv, xk = self.cur
        o, n, v = self.blocks[bi]
        c = self.c
        self.P.S.add("dve", (lambda e: e.scalar_tensor_tensor(
            out=xv[:, o:o + n], in0=ps, scalar=self.g[:, c, v:v + 1], in1=xv[:, o:o + n],
            op0=ALU.mult, op1=ALU.add)), reads=(pk, xk, "mod"), writes=(xk,))

    def post(self, pi, ci):
        xv, xk = self.cur
        c = self.c
        self.P.S.add("sp", (lambda e: e.dma_start(out=self.xd[c * 128:(c + 1) * 128, self.lo:self.hi],
                                                  in_=xv[:, self.lo:self.hi])),
                     reads=(xk,), writes=("xd",), dma=True)


def ffn_phase(P, C, xd, w_gate, w_up, w_down, ud, Amod, Bmod, gmod, blocks):
    S = P.S
    m0 = P.mark()
    h, hk = P.alloc(BF16, KC, T)
    norm_phase(P, C, xd, Amod, Bmod, h, hk, blocks)
    sg = [P.alloc(F32, T) for _ in range(2)]
    ub = [P.alloc(BF16, T) for _ in range(3)]
    lo = min(b[0] for b in blocks)
    hi = max(b[0] + b[1] for b in blocks)
    panels = [[(w_gate[:, f * 128:(f + 1) * 128], 128), (w_up[:, f * 128:(f + 1) * 128], 128)] for f in range(FC)]

    def epi(pi, ci, bi, ps, pk, M):
        o, n, v = blocks[bi]
        sv, sk = sg[pi % 2]
        uv, uk = ub[pi % 3]
        if ci == 0:
            S.add("act", (lambda e: e.activation(out=sv[:, o:o + n], in_=ps, func=AF.Silu)), reads=(pk,), writes=(sk,))
        else:
            S.add("dve", (lambda e: e.tensor_tensor(out=uv[:, o:o + n], in0=sv[:, o:o + n], in1=ps, op=ALU.mult)),
                  reads=(pk, sk), writes=(uk,))

    def post_panel(pi):
        uv, uk = ub[pi % 3]
        S.add("sp", (lambda e: e.dma_start(out=ud[pi * 128:(pi + 1) * 128, lo:hi], in_=uv[:, lo:hi])),
              reads=(uk,), writes=("ud",), dma=True)
    linear(P, h, hk, KC, panels, blocks, epi, post_panel=post_panel, banks=(0, 1, 2, 3, 4, 5))
    P.release(m0)
    P.barrier()
    m0 = P.mark()
    u, ukk = P.alloc(BF16, FC, T)
    ur = ud.rearrange("(c p) t -> p c t", p=128)
    for q in range(4):
        S.add("sp", (lambda e, q=q: e.dma_start(out=u[:, q * 11:(q + 1) * 11, lo:hi], in_=ur[:, q * 11:(q + 1) * 11, lo:hi])),
              reads=("ud",), writes=(ukk,), dma=True)
    R = Resid(P, xd, gmod, blocks)
    linear(P, u, ukk, FC, split_cols(w_down, 0, D, 128), blocks, R.epi, pre_chunk=R.pre, post_chunk=R.post,
           banks=(0, 1, 2, 3, 4, 5))
    P.release(m0)
    P.barrier()


LATP = 16
TP = (WIN + 2 * LATP) + (CTX + 2 * LATP)
SEQS = ((LATP, WIN, 0), (WIN + 3 * LATP, CTX, WIN))


def even_mixer(P, C, xd, h, hk, w_in, pool_w, pool_scale, conv_w, w_out, gmod, blocks, invc_d):
    S = P.S
    m0 = P.mark()
    yab, yk = P.alloc(BF16, KC, T)
    pwb, pwbk = P.alloc(BF16, 4, 2, 256)
    psc, psck = P.alloc(F32, 8)
    cw, cwk = P.alloc(F32, 3, 8)
    mA = P.mark()
    pw32, pw32k = P.alloc(F32, 4, 2, 256)
    load_small(P, pw32, pw32k, pool_w.rearrange("g (kc p) n -> p g kc n", p=128))
    S.add("pool", (lambda e: e.tensor_copy(out=pwb, in_=pw32)), reads=(pw32k,), writes=(pwbk,))
    load_small(P, psc, psck, pool_scale)
    load_small(P, cw, cwk, conv_w)
    P.barrier()
    P.release(mA)
    invg = [P.alloc(F32, T) for _ in range(2)]
    U, Uk = P.alloc(F32, 2, TP)
    LA, LAk = P.alloc(F32, 2, TP)
    LB, LBk = P.alloc(F32, 2, TP)
    pb, pbk = P.alloc(BF16, 2, T)
    for (buf, k) in ((U, Uk), (LA, LAk), (LB, LBk)):
        S.add("pool", (lambda e, buf=buf: e.memset(buf, 0.0)), writes=(k,))
    for g in range(4):
        iv, ik = invg[g % 2]
        if g < 2:
            load_small(P, iv, ik, invc_d[:, g, :])

    def epi_pool(pi, ci, bi, ps, pk, M):
        o, n, v = blocks[bi]
        po = (LATP + o) if v == 0 else (WIN + 3 * LATP + o - WIN)
        S.add("act", (lambda e: e.activation(out=U[:, ci, po:po + n], in_=ps, func=AF.Identity)),
              reads=(pk,), writes=(Uk,))

    def post_pool(g):
        iv, ik = invg[g % 2]
        nlev = g + 1
        src, srck = U, Uk
        dsts = [(LA, LAk), (LB, LBk)]
        sh = [(1, 0), (1, 1), (2, 2), (4, 4)]
        for l in range(nlev):
            dst, dstk = dsts[l % 2]
            a, b = sh[l]
            for (so, sn, to) in SEQS:
                lo_, hi_ = so - 8, so + sn + 8
                S.add("dve", (lambda e, dst=dst, src=src, lo_=lo_, hi_=hi_, a=a, b=b: e.tensor_tensor(
                    out=dst[:, :, lo_:hi_], in0=src[:, :, lo_ - a:hi_ - a], in1=src[:, :, lo_ + b:hi_ + b], op=ALU.add)),
                    reads=(srck,), writes=(dstk,))
            src, srck = dst, dstk
        for (so, sn, to) in SEQS:
            for j in range(2):
                S.add("dve", (lambda e, src=src, so=so, sn=sn, to=to, j=j: e.tensor_tensor(
                    out=src[:, j, so:so + sn], in0=src[:, j, so:so + sn], in1=iv[:, to:to + sn], op=ALU.mult)),
                    reads=(srck, ik), writes=(srck,))
            S.add("dve", (lambda e, src=src, so=so, sn=sn, to=to: e.tensor_tensor(
                out=pb[:, :, to:to + sn], in0=src[:, :, so:so + sn], in1=U[:, :, so:so + sn], op=ALU.subtract)),
                reads=(srck, Uk), writes=(pbk,))
        if g + 2 < 4:
            load_small(P, iv, ik, invc_d[:, g + 2, :])
        for nn in range(2):
            for bi, (o, n, v) in enumerate(blocks):
                b = P.psbank((4, 5))
                pk = ("ps", b)
                psap = P.ps[b][:, 0:n]

                def mm(e, psap=psap, nn=nn, o=o, n=n):
                    ins = None
                    for kc in range(2):
                        ins = e.matmul(psap, lhsT=pwb[:, g, kc, nn * 128:(nn + 1) * 128], rhs=pb[:, kc, o:o + n],
                                       start=(kc == 0), stop=(kc == 1))
                    return ins
                S.add("pe", mm, reads=(pwbk, pbk), writes=(pk,))
                S.add("act", (lambda e, psap=psap, nn=nn, o=o, n=n: e.activation(
                    out=yab[:, 2 * g + nn, o:o + n], in_=psap, func=AF.Identity, scale=psc[:, 2 * g + nn:2 * g + nn + 1])),
                    reads=(pk, psck), writes=(yk,))

    linear(P, h, hk, KC, split_cols(w_in, 0, 1024, 256), blocks, epi_pool, post_panel=post_pool, banks=(0, 1, 2, 3))
    if "yab" in DEBUG["dump"]:
        P.dbg("dbg_U", U, Uk, [2, TP])
        P.dbg("dbg_LB", LB, LBk, [2, TP])
        P.dbg("dbg_pb", pb, pbk, [2, T], BF16)
    P.barrier()
    P.release(mA)

    TPC = T + 4
    CSEQ = ((1, WIN, 0), (WIN + 3, CTX, WIN))
    gbs, gbk = P.alloc(F32, T)
    gcf, gck = P.alloc(F32, T)
    uc, uck = P.alloc(F32, TPC)
    cv, cvk = P.alloc(F32, T)
    S.add("pool", (lambda e: e.memset(uc, 0.0)), writes=(uck,))
    panels = [[(w_in[:, 1024 + c * 128:1024 + (c + 1) * 128], 128), (w_in[:, 2048 + c * 128:2048 + (c + 1) * 128], 128),
               (w_in[:, 3072 + c * 128:3072 + (c + 1) * 128], 128)] for c in range(8)]

    def epi_conv(pi, ci, bi, ps, pk, M):
        o, n, v = blocks[bi]
        if ci == 0:
            S.add("act", (lambda e: e.activation(out=gbs[:, o:o + n], in_=ps, func=AF.Identity)), reads=(pk,), writes=(gbk,))
        elif ci == 1:
            S.add("act", (lambda e: e.activation(out=gcf[:, o:o + n], in_=ps, func=AF.Identity)), reads=(pk,), writes=(gck,))
        else:
            po = (1 + o) if v == 0 else (WIN + 3 + o - WIN)
            S.add("dve", (lambda e: e.tensor_tensor(out=uc[:, po:po + n], in0=gcf[:, o:o + n], in1=ps, op=ALU.mult)),
                  reads=(pk, gck), writes=(uck,))

    def post_conv(c):
        for (so, sn, to) in CSEQ:
            S.add("dve", (lambda e, so=so, sn=sn, to=to: e.tensor_scalar(
                out=cv[:, to:to + sn], in0=uc[:, so - 1:so - 1 + sn], scalar1=cw[:, 0, c:c + 1], scalar2=None, op0=ALU.mult)),
                reads=(uck, cwk), writes=(cvk,))
            for k in (1, 2):
                S.add("dve", (lambda e, so=so, sn=sn, to=to, k=k: e.scalar_tensor_tensor(
                    out=cv[:, to:to + sn], in0=uc[:, so - 1 + k:so - 1 + k + sn], scalar=cw[:, k, c:c + 1],
                    in1=cv[:, to:to + sn], op0=ALU.mult, op1=ALU.add)), reads=(uck, cwk, cvk), writes=(cvk,))
        S.add("dve", (lambda e: e.tensor_tensor(out=yab[:, 8 + c, :], in0=gbs, in1=cv, op=ALU.mult)),
              reads=(gbk, cvk), writes=(yk,))

    linear(P, h, hk, KC, panels, blocks, epi_conv, post_panel=post_conv, banks=(0, 1, 2, 3, 4, 5))
    if "yab" in DEBUG["dump"]:
        P.dbg("dbg_yab", yab, yk, [KC, T], BF16)
        P.dbg("dbg_gbs", gbs, gbk, [T])
        P.dbg("dbg_uc", uc, uck, [TPC])
        P.dbg("dbg_cv", cv, cvk, [T])
    if DEBUG["stop"] == "conv":
        raise StopBuild(P)
    P.barrier()
    P.release(mA)
    R = Resid(P, xd, gmod, blocks)
    linear(P, yab, yk, KC, split_cols(w_out, 0, D, 256), blocks, R.epi, pre_chunk=R.pre, post_chunk=R.post,
           banks=(0, 1, 2, 3, 4, 5))
    P.release(m0)
    P.barrier()


def odd_pre(P, C, h, hk, w_dq, qg_d, w_dkv, kvg_d, rope_cos_d, rope_sin_d, kv_lat, kv_ctx, cq_out, blocks):
    S = P.S
    m0 = P.mark()
    qg, qgk = P.alloc(F32, 4)
    kvg, kvgk = P.alloc(F32, 4)
    load_small(P, qg, qgk, qg_d)
    load_small(P, kvg, kvgk, kvg_d)
    cos, cosk = P.alloc(F32, T)
    sin, sink = P.alloc(F32, T)
    load_small(P, cos[0:64, :], cosk, rope_cos_d)
    load_small(P, sin[0:64, :], sink, rope_sin_d)
    c32, c32k = P.alloc(F32, 4, T)
    kr32, kr32k = P.alloc(F32, T)
    krb, krbk = P.alloc(BF16, T)
    kro, krok = P.alloc(BF16, T)
    cn, cnk = P.alloc(BF16, 4, T)
    sq, sqk = P.alloc(BF16, 4, 352)
    rs = [P.alloc(F32, 352) for _ in range(2)]
    t1 = [P.alloc(F32, 352) for _ in range(2)]
    t2 = [P.alloc(F32, 352) for _ in range(2)]

    def epi(pi, ci, bi, ps, pk, M):
        o, n, v = blocks[bi]
        ch = pi * 2 + ci
        if M == 128:
            S.add("act", (lambda e: e.activation(out=c32[:, ch, o:o + n], in_=ps, func=AF.Identity)),
                  reads=(pk,), writes=(c32k,))
        else:
            S.add("act", (lambda e: e.activation(out=kr32[0:64, o:o + n], in_=ps, func=AF.Identity)),
                  reads=(pk,), writes=(kr32k,))

    def normalize(g):
        for bi, (o, n, v) in enumerate(blocks):
            rv, rk = rs[bi % 2]
            rms_stats(P, c32, c32k, 4, o, n, sq, sqk, rv, rk, C["ones"], 7, 1.0 / 512)
            for c in range(4):
                S.add("dve", (lambda e, c=c, o=o, n=n, rv=rv: e.scalar_tensor_tensor(
                    out=cn[:, c, o:o + n], in0=c32[:, c, o:o + n], scalar=g[:, c:c + 1], in1=rv[:, 0:n],
                    op0=ALU.mult, op1=ALU.mult)), reads=(c32k, rk, qgk, kvgk), writes=(cnk,))

    lo = min(b[0] for b in blocks)
    hi = max(b[0] + b[1] for b in blocks)
    linear(P, h, hk, KC, split_cols(w_dq, 0, 512, 256), blocks, epi, banks=(0, 1, 2, 3))
    normalize(qg)
    S.add("sp", (lambda e: e.dma_start(out=cq_out.rearrange("(c p) t -> p c t", p=128)[:, :, lo:hi], in_=cn[:, :, lo:hi])),
          reads=(cnk,), writes=("cq_out",), dma=True)
    linear(P, h, hk, KC, split_cols(w_dkv, 0, 576, 256), blocks, epi, banks=(0, 1, 2, 3))
    normalize(kvg)
    S.add("sp", (lambda e: e.dma_start(out=kv_lat[0:512, :].rearrange("(c p) t -> p c t", p=128), in_=cn[:, :, HALO:HALO + OWN])),
          reads=(cnk,), writes=("kv_out",), dma=True)
    if kv_ctx is not None:
        S.add("sp", (lambda e: e.dma_start(out=kv_ctx[0:512, :].rearrange("(c p) t -> p c t", p=128), in_=cn[:, :, WIN:WIN + CTX])),
              reads=(cnk,), writes=("kv_out",), dma=True)
    S.add("act", (lambda e: e.activation(out=krb[0:64, lo:hi], in_=kr32[0:64, lo:hi], func=AF.Identity)),
          reads=(kr32k,), writes=(krbk,))
    for bi, (o, n, v) in enumerate(blocks):
        b = P.psbank((4, 5))
        pk = ("ps", b)
        psap = P.ps[b][0:64, 0:n]
        S.add("pe", (lambda e, psap=psap, o=o, n=n: e.matmul(psap, lhsT=C["rot"], rhs=krb[0:64, o:o + n], start=True, stop=True)),
              reads=(krbk, "const"), writes=(pk,))
        av, ak = t1[bi % 2]
        bv, bk = t2[bi % 2]
        S.add("dve", (lambda e, av=av, o=o, n=n: e.tensor_tensor(out=av[0:64, 0:n], in0=kr32[0:64, o:o + n],
                                                               in1=cos[0:64, o:o + n], op=ALU.mult)),
              reads=(kr32k, cosk), writes=(ak,))
        S.add("dve", (lambda e, bv=bv, psap=psap, o=o, n=n: e.tensor_tensor(out=bv[0:64, 0:n], in0=psap,
                                                                          in1=sin[0:64, o:o + n], op=ALU.mult)),
              reads=(pk, sink), writes=(bk,))
        S.add("dve", (lambda e, av=av, bv=bv, o=o, n=n: e.tensor_tensor(out=kro[0:64, o:o + n], in0=av[0:64, 0:n],
                                                                      in1=bv[0:64, 0:n], op=ALU.add)),
              reads=(ak, bk), writes=(krok,))
    S.add("sp", (lambda e: e.dma_start(out=kv_lat[512:576, :], in_=kro[0:64, HALO:HALO + OWN])),
          reads=(krok,), writes=("kv_out",), dma=True)
    if kv_ctx is not None:
        S.add("sp", (lambda e: e.dma_start(out=kv_ctx[512:576, :], in_=kro[0:64, WIN:WIN + CTX])),
              reads=(krok,), writes=("kv_out",), dma=True)
    P.release(m0)
    P.barrier()


def attention(P, C, xd, od, kv_all, cq_in, w_uq, w_ukv, w_o, rope_cos_d, rope_sin_d, gmod, blocks):
    S = P.S
    scale = float((128 + 64) ** -0.5)
    m0 = P.mark()
    ckv, ckvk = P.alloc(BF16, 4, NKEYS)
    kr, krk = P.alloc(BF16, NKEYS)
    cq, cqk = P.alloc(BF16, 4, T)
    cos, cosk = P.alloc(F32, T)
    sin, sink = P.alloc(F32, T)
    load_small(P, cos[0:64, :], cosk, rope_cos_d)
    load_small(P, sin[0:64, :], sink, rope_sin_d)
    kvr = kv_all[0:512, :].rearrange("(c p) t -> p c t", p=128)
    for c in range(4):
        S.add("sp", (lambda e, c=c: e.dma_start(out=ckv[:, c, :], in_=kvr[:, c, :])), writes=(ckvk,), dma=True)
    S.add("sp", (lambda e: e.dma_start(out=kr[0:64, :], in_=kv_all[512:576, :])), writes=(krk,), dma=True)
    S.add("sp", (lambda e: e.dma_start(out=cq, in_=cq_in.rearrange("(c p) t -> p c t", p=128))), writes=(cqk,), dma=True)
    wq32 = [P.alloc(F32, 4, 192) for _ in range(2)]
    wqb = [P.alloc(BF16, 4, 192) for _ in range(2)]
    wk32 = [P.alloc(F32, 4, 256) for _ in range(2)]
    wkb = [P.alloc(BF16, 4, 256) for _ in range(2)]
    qn, qnk = P.alloc(BF16, T)
    qr32, qr32k = P.alloc(F32, T)
    qrb, qrbk = P.alloc(BF16, T)
    qro, qrok = P.alloc(BF16, T)
    kT, kTk = P.alloc(BF16, NKEYS)
    vh, vhk = P.alloc(BF16, NKC, 128)
    pT = [P.alloc(BF16, 352) for _ in range(4)]
    t1 = [P.alloc(F32, 352) for _ in range(2)]
    t2 = [P.alloc(F32, 352) for _ in range(2)]
    rsum = [P.alloc(F32, 352) for _ in range(2)]
    sacc = [P.alloc(F32, 352) for _ in range(2)]
    ones32, ones32k = P.alloc(F32, 128)
    S.add("pool", (lambda e: e.memset(ones32, 1.0)), writes=(ones32k,))
    obuf = [P.alloc(BF16, T) for _ in range(2)]
    wqr = w_uq.rearrange("(kc p) n -> p kc n", p=128)
    wkr = w_ukv.rearrange("(kc p) n -> p kc n", p=128)
    lo = min(b[0] for b in blocks)
    hi = max(b[0] + b[1] for b in blocks)
    GB = (0, 1, 2, 3)
    evac = [0]

    def loadw(hd):
        a, ak = wq32[hd % 2]
        ab, abk = wqb[hd % 2]
        b, bk = wk32[hd % 2]
        bb, bbk = wkb[hd % 2]
        S.add("sp", (lambda e: e.dma_start(out=a, in_=wqr[:, :, hd * 192:(hd + 1) * 192])), writes=(ak,), dma=True)
        S.add("sp", (lambda e: e.dma_start(out=b, in_=wkr[:, :, hd * 256:(hd + 1) * 256])), writes=(bk,), dma=True)
        S.add("pool", (lambda e: e.tensor_copy(out=ab, in_=a)), reads=(ak,), writes=(abk,))
        S.add("pool", (lambda e: e.tensor_copy(out=bb, in_=b)), reads=(bk,), writes=(bbk,))

    def evacuate(out_ap, ps_ap, pk, wkey):
        evac[0] += 1
        if evac[0] % 2:
            S.add("act", (lambda e: e.activation(out=out_ap, in_=ps_ap, func=AF.Identity)), reads=(pk,), writes=(wkey,))
        else:
            S.add("dve", (lambda e: e.tensor_copy(out=out_ap, in_=ps_ap)), reads=(pk,), writes=(wkey,))

    def group4(psap, lhs_fn, rhs_fn, reads, pk):
        def mm(e):
            ins = None
            for kc in range(4):
                ins = e.matmul(psap, lhsT=lhs_fn(kc), rhs=rhs_fn(kc), start=(kc == 0), stop=(kc == 3))
            return ins
        S.add("pe", mm, reads=reads, writes=(pk,))

    def q_block(hd, bi, o, n, v, wq, wqk_):
        b = P.psbank(GB)
        pk = ("ps", b)
        psap = P.ps[b][:, 0:n]
        group4(psap, (lambda kc: wq[:, kc, 0:128]), (lambda kc: cq[:, kc, o:o + n]), (wqk_, cqk), pk)
        evacuate(qn[:, o:o + n], psap, pk, qnk)
        qs = DEBUG.get("q_steps", 9)
        if qs < 2:
            return
        b = P.psbank(GB)
        pk2 = ("ps", b)
        psr = P.ps[b][0:64, 0:n]
        group4(psr, (lambda kc: wq[:, kc, 128:192]), (lambda kc: cq[:, kc, o:o + n]), (wqk_, cqk), pk2)
        S.add("act", (lambda e: e.activation(out=qr32[0:64, o:o + n], in_=psr, func=AF.Identity)), reads=(pk2,), writes=(qr32k,))
        S.add("dve", (lambda e: e.tensor_copy(out=qrb[0:64, o:o + n], in_=qr32[0:64, o:o + n])), reads=(qr32k,), writes=(qrbk,))
        if qs < 3:
            return
        b = P.psbank(GB)
        pk3 = ("ps", b)
        psq = P.ps[b][0:64, 0:n]
        S.add("pe", (lambda e: e.matmul(psq, lhsT=C["rot"], rhs=qrb[0:64, o:o + n], start=True, stop=True)),
              reads=(qrbk, "const"), writes=(pk3,))
        if qs < 4:
            return
        av, ak = t1[bi % 2]
        bv, bk = t2[bi % 2]
        S.add("dve", (lambda e: e.tensor_tensor(out=av[0:64, 0:n], in0=qr32[0:64, o:o + n], in1=cos[0:64, o:o + n], op=ALU.mult)),
              reads=(qr32k, cosk), writes=(ak,))
        S.add("dve", (lambda e: e.tensor_tensor(out=bv[0:64, 0:n], in0=psq, in1=sin[0:64, o:o + n], op=ALU.mult)),
              reads=(pk3, sink), writes=(bk,))
        S.add("dve", (lambda e: e.tensor_tensor(out=qro[0:64, o:o + n], in0=av[0:64, 0:n], in1=bv[0:64, 0:n], op=ALU.add)),
              reads=(ak, bk), writes=(qrok,))

    def k_block(kb, n, wk, wkk_):
        b = P.psbank(GB)
        pk = ("ps", b)
        psap = P.ps[b][:, 0:n]
        group4(psap, (lambda kc: wk[:, kc, 0:128]), (lambda kc: ckv[:, kc, kb:kb + n]), (wkk_, ckvk), pk)
        evacuate(kT[:, kb:kb + n], psap, pk, kTk)

    def v_block(kc0, nk, wk, wkk_):
        b = P.psbank(GB)
        pk = ("ps", b)

        def mmv(e):
            ins = None
            for j in range(nk):
                for kc in range(4):
                    ins = e.matmul(P.ps[b][:, j * 128:(j + 1) * 128], lhsT=ckv[:, kc, (kc0 + j) * 128:(kc0 + j + 1) * 128],
                                   rhs=wk[:, kc, 128:256], start=(kc == 0), stop=(kc == 3))
            return ins
        S.add("pe", mmv, reads=(wkk_, ckvk), writes=(pk,))
        evacuate(vh[:, kc0:kc0 + nk, :], P.ps[b][:, 0:nk * 128].rearrange("p (a b) -> p a b", a=nk), pk, vhk)

    def att_block(hd, bi, o, n, v, ov, ovk):
        kcs = list(range(NKC)) if v == 0 else list(range(SEQ // 128, NKC))
        ob = 4 + (bi % 2)
        sb = 6 + (bi % 2)
        opk, spk = ("ps", ob), ("ps", sb)
        ops_ap, sps_ap = P.ps[ob][:, 0:n], P.ps[sb][:, 0:n]

        def qk(kc):
            b = P.psbank(GB)
            pk = ("ps", b)
            psap = P.ps[b][:, 0:n]

            def mm(e):
                e.matmul(psap, lhsT=kT[:, kc * 128:(kc + 1) * 128], rhs=qn[:, o:o + n], start=True, stop=False)
                return e.matmul(psap, lhsT=kr[0:64, kc * 128:(kc + 1) * 128], rhs=qro[0:64, o:o + n], start=False, stop=True)
            S.add("pe", mm, reads=(kTk, krk, qnk, qrok), writes=(pk,))
            pv, pvk = pT[kc % 4]
            S.add("act", (lambda e: e.activation(out=pv[:, 0:n], in_=psap, func=AF.Exp, scale=scale)), reads=(pk,), writes=(pvk,))
            return (kc, pv, pvk)

        av, avk = sacc[bi % 2]

        def pvs(item, first, last):
            kc, pv, pvk = item
            S.add("pe", (lambda e: e.matmul(ops_ap, lhsT=vh[:, kc, :], rhs=pv[:, 0:n], start=first, stop=last)),
                  reads=(pvk, vhk), writes=(opk,))
            if first:
                S.add("dve", (lambda e: e.tensor_copy(out=av[:, 0:n], in_=pv[:, 0:n])), reads=(pvk,), writes=(avk,))
            else:
                S.add("dve", (lambda e: e.tensor_tensor(out=av[:, 0:n], in0=av[:, 0:n], in1=pv[:, 0:n], op=ALU.add)),
                      reads=(pvk, avk), writes=(avk,))
            if last:
                S.add("pe", (lambda e: e.matmul(sps_ap, lhsT=ones32, rhs=av[:, 0:n], start=True, stop=True)),
                      reads=(avk, ones32k), writes=(spk,))

        pend = []
        done = 0
        for kc in kcs:
            pend.append(qk(kc))
            if len(pend) > 2:
                pvs(pend.pop(0), done == 0, False)
                done += 1
        while pend:
            it = pend.pop(0)
            pvs(it, done == 0, len(pend) == 0)
            done += 1
        rv, rk = rsum[bi % 2]
        S.add("dve", (lambda e: e.reciprocal(out=rv[:, 0:n], in_=sps_ap)), reads=(spk,), writes=(rk,))
        S.add("dve", (lambda e: e.tensor_tensor(out=ov[:, o:o + n], in0=ops_ap, in1=rv[:, 0:n], op=ALU.mult)),
              reads=(opk, rk), writes=(ovk,))

    parts = DEBUG.get("att_parts", "qkva")

    def do_head(hd):
        if hd + 1 < NH:
            loadw(hd + 1)
        wq, wqk_ = wqb[hd % 2]
        wk, wkk_ = wkb[hd % 2]
        if "q" in parts:
            for bi, (o, n, v) in enumerate(blocks):
                q_block(hd, bi, o, n, v, wq, wqk_)
        if "k" in parts:
            for kb in range(0, NKEYS, 512):
                k_block(kb, min(512, NKEYS - kb), wk, wkk_)
        if "v" in parts:
            for kc0 in range(0, NKC, 4):
                v_block(kc0, min(4, NKC - kc0), wk, wkk_)
        ov, ovk = obuf[hd % 2]
        if "a" in parts:
            for bi, (o, n, v) in enumerate(blocks):
                att_block(hd, bi, o, n, v, ov, ovk)
            S.add("sp", (lambda e: e.dma_start(out=od[hd * 128:(hd + 1) * 128, lo:hi], in_=ov[:, lo:hi])),
                  reads=(ovk,), writes=("od",), dma=True)

    loadw(0)
    for hd in range(DEBUG.get("att_heads", NH)):
        do_head(hd)
    if "att_dump" in DEBUG["dump"]:
        P.dbg("dbg_qn", qn, qnk, [T], BF16)
        P.dbg("dbg_qro", qro, qrok, [T], BF16)
        P.dbg("dbg_kT", kT, kTk, [NKEYS], BF16)
        P.dbg("dbg_vh", vh, vhk, [NKC, 128], BF16)
        P.dbg("dbg_ov", obuf[0][0], obuf[0][1], [T], BF16)
    if DEBUG["stop"] == "attcore":
        raise StopBuild(P)
    P.barrier()
    P.release(m0)
    m0 = P.mark()
    oT, oTk = P.alloc(BF16, KC, T)
    odr = od.rearrange("(c p) t -> p c t", p=128)
    for q in range(4):
        S.add("sp", (lambda e, q=q: e.dma_start(out=oT[:, q * 4:(q + 1) * 4, lo:hi], in_=odr[:, q * 4:(q + 1) * 4, lo:hi])),
              reads=("od",), writes=(oTk,), dma=True)
    R = Resid(P, xd, gmod, blocks)
    linear(P, oT, oTk, KC, split_cols(w_o, 0, D, 256), blocks, R.epi, pre_chunk=R.pre, post_chunk=R.post,
           banks=(0, 1, 2, 3, 4, 5))
    P.release(m0)
    P.barrier()


def load_consts(P):
    S = P.S
    C = {}
    ones, _ = P.alloc(BF16, 128)
    rot, _ = P.alloc(BF16, 64)
    o_d = P.din("c_ones", [128, 128], BF16)
    r_d = P.din("c_rot", [64, 64], BF16)
    S.add("sp", (lambda e: e.dma_start(out=ones, in_=o_d)), writes=("const",), dma=True)
    S.add("sp", (lambda e: e.dma_start(out=rot[0:64, :], in_=r_d)), writes=("const",), dma=True)
    C["ones"] = ones
    C["rot"] = rot[0:64, :]
    return C


def load_mod(P, C, layers, mod_ap=None):
    S = P.S
    nl = len(layers)
    mod_d = mod_ap if mod_ap is not None else P.din("mod", [128, nl, 6, KC, 2])
    ng_d = P.din("ng", [128, nl, 2, KC])
    mod, _ = P.alloc(F32, nl * 6, KC, 2)
    ng, _ = P.alloc(F32, nl * 2, KC)
    S.add("sp", (lambda e: e.dma_start(out=mod, in_=mod_d.rearrange("p l s c v -> p (l s) c v"))), writes=("mod",), dma=True)
    S.add("sp", (lambda e: e.dma_start(out=ng, in_=ng_d.rearrange("p l s c -> p (l s) c"))), writes=("mod",), dma=True)
    out = {}
    for li, L in enumerate(layers):
        d = {}
        for w, (si, ni) in enumerate(((1, 0), (4, 1))):
            a, _ = P.alloc(F32, KC, 2)
            for v in range(2):
                S.add("dve", (lambda e, a=a, v=v, li=li, si=si, ni=ni: e.scalar_tensor_tensor(
                    out=a[:, :, v], in0=mod[:, li * 6 + si, :, v], scalar=1.0, in1=ng[:, li * 2 + ni, :],
                    op0=ALU.add, op1=ALU.mult)), reads=("mod",), writes=("mod",))
            d["A%d" % (w + 1)] = a
        d["B1"] = mod[:, li * 6 + 0]
        d["G1"] = mod[:, li * 6 + 2]
        d["B2"] = mod[:, li * 6 + 3]
        d["G2"] = mod[:, li * 6 + 5]
        out[L] = d
    return out


class StopBuild(Exception):
    pass


DEBUG = {"stop": None, "dump": ()}


def build_stage(stage):
    try:
        return _build_stage(stage)
    except StopBuild as e:
        P = e.args[0]
        P.barrier()
        return P


def _build_stage(stage):
    P = Prog()
    S = P.S
    C = load_consts(P)
    layers = {"B": [0, 1], "C": [1, 2, 3], "D": [3]}[stage]
    M = load_mod(P, C, layers)
    x_in = P.din("x_in", [D, T])
    if stage == "D":
        xd = P.dint("x_state", [D, T])
    else:
        xd = P.dout("x_out", [D, T])
    ud = P.dint("u_spill", [DFF, T], BF16)
    od = P.dint("o_spill", [D, T], BF16)
    S.add("sp", (lambda e: e.dma_start(out=xd, in_=x_in)), writes=("xd",), dma=True)
    rope_cos = P.din("rope_cos", [64, T])
    rope_sin = P.din("rope_sin", [64, T])
    W = {}

    def win(name, shape):
        W[name] = P.din(name, shape)
        return W[name]

    def ffn_w(L):
        return (win("ffn_w_gate_%d" % L, [D, DFF]), win("ffn_w_up_%d" % L, [D, DFF]), win("ffn_w_down_%d" % L, [DFF, D]))

    def even_layer(L):
        i = L // 2
        mask_d = P.din("mask3", [128, KC, 2 * HALO])
        invc_d = P.din("invc", [128, 4, T])
        w_in = win("even_w_in_%d" % i, [D, 4096])
        pool_w = win("pool_w_%d" % i, [4, 256, 256])
        pool_scale = win("pool_scale_%d" % i, [128, 8])
        conv_w = win("conv_w_%d" % i, [128, 3, 8])
        w_out = win("even_w_out_%d" % i, [D, D])
        m0 = P.mark()
        mask3, _ = P.alloc(F32, KC, 2 * HALO)
        S.add("sp", (lambda e: e.dma_start(out=mask3, in_=mask_d)), writes=("const",), dma=True)
        h, hk = P.alloc(BF16, KC, T)
        norm_phase(P, C, xd, M[L]["A1"], M[L]["B1"], h, hk, BLOCKS, mask3=mask3)
        if "h" in DEBUG["dump"]:
            P.dbg("dbg_h", h, hk, [KC, T], BF16)
        if DEBUG["stop"] == "norm%d" % L:
            raise StopBuild(P)
        even_mixer(P, C, xd, h, hk, w_in, pool_w, pool_scale, conv_w, w_out, M[L]["G1"], BLOCKS, invc_d)
        P.release(m0)
        if DEBUG["stop"] == "mixer%d" % L:
            raise StopBuild(P)
        wg, wu, wd = ffn_w(L)
        ffn_phase(P, C, xd, wg, wu, wd, ud, M[L]["A2"], M[L]["B2"], M[L]["G2"], BLOCKS)
        if DEBUG["stop"] == "ffn%d" % L:
            raise StopBuild(P)

    def odd_layer_pre(L):
        i = L // 2
        w_dq = win("mla_w_dq_%d" % i, [D, 512])
        qg = win("mla_q_norm_g_%d" % i, [128, 4])
        w_dkv = win("mla_w_dkv_%d" % i, [D, 576])
        kvg = win("mla_kv_norm_g_%d" % i, [128, 4])
        kv_out = P.dout("kv_out", [576, OWN + CTX], BF16)
        cq_out = P.dout("cq_out", [512, T], BF16)
        m0 = P.mark()
        h, hk = P.alloc(BF16, KC, T)
        norm_phase(P, C, xd, M[L]["A1"], M[L]["B1"], h, hk, BLOCKS)
        odd_pre(P, C, h, hk, w_dq, qg, w_dkv, kvg, rope_cos, rope_sin, kv_out[:, 0:OWN], kv_out[:, OWN:OWN + CTX], cq_out, BLOCKS)
        P.release(m0)

    def odd_layer_post(L, blocks):
        i = L // 2
        kv_all = P.din("kv_all", [576, NKEYS], BF16)
        cq_in = P.din("cq_in", [512, T], BF16)
        w_uq = win("mla_w_uq_%d" % i, [512, 3072])
        w_ukv = win("mla_w_ukv_%d" % i, [512, 4096])
        w_o = win("mla_w_o_%d" % i, [D, D])
        attention(P, C, xd, od, kv_all, cq_in, w_uq, w_ukv, w_o, rope_cos, rope_sin, M[L]["G1"], blocks)
        if DEBUG["stop"] == "att%d" % L:
            raise StopBuild(P)
        wg, wu, wd = ffn_w(L)
        ffn_phase(P, C, xd, wg, wu, wd, ud, M[L]["A2"], M[L]["B2"], M[L]["G2"], blocks)
        if DEBUG["stop"] == "ffn%d" % L:
            raise StopBuild(P)

    P.barrier()
    if stage == "B":
        even_layer(0)
        odd_layer_pre(1)
    elif stage == "C":
        odd_layer_post(1, BLOCKS)
        even_layer(2)
        odd_layer_pre(3)
    else:
        odd_layer_post(3, BLOCKS[:3])
        fg_d = P.din("fgain", [128, KC, 2])
        y = P.dout("y", [D, OWN])
        m0 = P.mark()
        fg, _ = P.alloc(F32, KC, 2)
        S.add("sp", (lambda e: e.dma_start(out=fg, in_=fg_d)), writes=("mod",), dma=True)
        hf, hfk = P.alloc(F32, KC, T)
        norm_phase(P, C, xd, fg, None, hf, hfk, BLOCKS[:3])
        yr = y.rearrange("(c p) t -> p c t", p=128)
        for q in range(4):
            S.add("sp", (lambda e, q=q: e.dma_start(out=yr[:, q * 4:(q + 1) * 4, :], in_=hf[:, q * 4:(q + 1) * 4, HALO:HALO + OWN])),
                  reads=(hfk,), writes=("y",), dma=True)
        P.release(m0)
    P.barrier()
    return P


def build_fused():
    P = Prog()
    S = P.S
    C = load_consts(P)
    NS = NCORES
    NJ = 96
    ada_w = P.din("ada_w", [4 * D, 12288])
    ada_b = P.din("ada_b", [128, 4 * NJ])
    cvec = P.din("cvec", [128, KC, 2])
    mod_d = P.dint("mod_d", [128, 4 * NJ, 2])
    m0 = P.mark()
    cs, csk = P.alloc(F32, KC, 2)
    ss, ssk = P.alloc(F32, KC, 2)
    bb, bbk = P.alloc(F32, 4 * NJ)
    res, resk = P.alloc(F32, 4 * NJ, 2)
    load_small(P, cs, csk, cvec)
    load_small(P, bb, bbk, ada_b)
    S.add("act", (lambda e: e.activation(out=ss, in_=cs, func=AF.Silu)), reads=(csk,), writes=(ssk,))
    st = [P.alloc(F32, KC, 128) for _ in range(3)]
    items = [(l, j) for l in range(4) for j in range(NJ)]

    def aload(idx):
        l, j = items[idx]
        sv, sk = st[idx % 3]
        S.add("sp", (lambda e: e.dma_start(out=sv, in_=ada_w[l * D:(l + 1) * D, j * 128:(j + 1) * 128].rearrange("(kc p) n -> p kc n", p=128))),
              writes=(sk,), dma=True)
    aload(0)
    aload(1)
    for idx in range(len(items)):
        if idx + 2 < len(items):
            aload(idx + 2)
        sv, sk = st[idx % 3]
        b = idx % 4
        pk = ("ps", b)
        psap = P.ps[b][:, 0:2]

        def mm(e, sv=sv, psap=psap):
            ins = None
            for kc in range(KC):
                ins = e.matmul(psap, lhsT=sv[:, kc, :], rhs=ss[:, kc, :], start=(kc == 0), stop=(kc == KC - 1))
            return ins
        S.add("pe", mm, reads=(sk, ssk), writes=(pk,))
        S.add("act", (lambda e, psap=psap, idx=idx: e.activation(out=res[:, idx, :], in_=psap, func=AF.Identity,
                                                                 bias=bb[:, idx:idx + 1], scale=1.0)),
              reads=(pk, bbk), writes=(resk,))
    S.add("sp", (lambda e: e.dma_start(out=mod_d, in_=res)), reads=(resk,), writes=("mod_d",), dma=True)
    P.release(m0)
    M = load_mod(P, C, [0, 1, 2, 3], mod_ap=mod_d.rearrange("p (l s c) v -> p l s c v", l=4, s=6))
    x_in = P.din("x_in", [NS, D, T])
    xds = [P.dint("x_state%d" % sl, [D, T]) for sl in range(NS)]
    for sl in range(NS):
        S.add("sp", (lambda e, sl=sl: e.dma_start(out=xds[sl], in_=x_in[sl])), writes=("xd",), dma=True)
    ud = P.dint("u_spill", [DFF, T], BF16)
    od = P.dint("o_spill", [D, T], BF16)
    kvs = {1: P.dint("kv_all_l1", [576, NKEYS], BF16), 3: P.dint("kv_all_l3", [576, NKEYS], BF16)}
    cqd = [P.dint("cq_%d" % sl, [512, T], BF16) for sl in range(NS)]
    rope_cos = P.din("rope_cos", [NS, 64, T])
    rope_sin = P.din("rope_sin", [NS, 64, T])
    mask_d = P.din("mask3", [NS, 128, KC, 2 * HALO])
    invc_d = P.din("invc", [NS, 128, 4, T])
    W = {}

    def win(name, shape):
        if name not in W:
            W[name] = P.din(name, shape)
        return W[name]

    def ffn_w(L):
        return (win("ffn_w_gate_%d" % L, [D, DFF]), win("ffn_w_up_%d" % L, [D, DFF]), win("ffn_w_down_%d" % L, [DFF, D]))

    def blocks_of(sl):
        return BLOCKS if sl == 0 else BLOCKS[:3]

    def even_layer(L, sl):
        i = L // 2
        blocks = blocks_of(sl)
        w_in = win("even_w_in_%d" % i, [D, 4096])
        pool_w = win("pool_w_%d" % i, [4, 256, 256])
        pool_scale = win("pool_scale_%d" % i, [128, 8])
        conv_w = win("conv_w_%d" % i, [128, 3, 8])
        w_out = win("even_w_out_%d" % i, [D, D])
        m0 = P.mark()
        mask3, _ = P.alloc(F32, KC, 2 * HALO)
        S.add("sp", (lambda e: e.dma_start(out=mask3, in_=mask_d[sl])), writes=("const",), dma=True)
        h, hk = P.alloc(BF16, KC, T)
        norm_phase(P, C, xds[sl], M[L]["A1"], M[L]["B1"], h, hk, blocks, mask3=mask3)
        even_mixer(P, C, xds[sl], h, hk, w_in, pool_w, pool_scale, conv_w, w_out, M[L]["G1"], blocks, invc_d[sl])
        P.release(m0)
        wg, wu, wd = ffn_w(L)
        ffn_phase(P, C, xds[sl], wg, wu, wd, ud, M[L]["A2"], M[L]["B2"], M[L]["G2"], blocks)

    def odd_layer_pre(L, sl):
        i = L // 2
        blocks = blocks_of(sl)
        w_dq = win("mla_w_dq_%d" % i, [D, 512])
        qg = win("mla_q_norm_g_%d" % i, [128, 4])
        w_dkv = win("mla_w_dkv_%d" % i, [D, 576])
        kvg = win("mla_kv_norm_g_%d" % i, [128, 4])
        m0 = P.mark()
        h, hk = P.alloc(BF16, KC, T)
        norm_phase(P, C, xds[sl], M[L]["A1"], M[L]["B1"], h, hk, blocks)
        kv_all = kvs[L]
        odd_pre(P, C, h, hk, w_dq, qg, w_dkv, kvg, rope_cos[sl], rope_sin[sl], kv_all[:, sl * OWN:(sl + 1) * OWN],
                kv_all[:, SEQ:SEQ + CTX] if sl == 0 else None, cqd[sl], blocks)
        P.release(m0)

    def odd_layer_post(L, sl, blocks):
        i = L // 2
        w_uq = win("mla_w_uq_%d" % i, [512, 3072])
        w_ukv = win("mla_w_ukv_%d" % i, [512, 4096])
        w_o = win("mla_w_o_%d" % i, [D, D])
        attention(P, C, xds[sl], od, kvs[L], cqd[sl], w_uq, w_ukv, w_o, rope_cos[sl], rope_sin[sl], M[L]["G1"], blocks)
        wg, wu, wd = ffn_w(L)
        ffn_phase(P, C, xds[sl], wg, wu, wd, ud, M[L]["A2"], M[L]["B2"], M[L]["G2"], blocks)

    P.barrier()
    nsl = DEBUG.get("fused_slots", NS)
    for sl in range(nsl):
        even_layer(0, sl)
        odd_layer_pre(1, sl)
    for sl in range(nsl):
        odd_layer_post(1, sl, blocks_of(sl))
        even_layer(2, sl)
        odd_layer_pre(3, sl)
    odd_layer_post(3, 0, BLOCKS[:3])
    fg_d = P.din("fgain", [128, KC, 2])
    y = P.dout("y", [D, OWN])
    m0 = P.mark()
    fg, _ = P.alloc(F32, KC, 2)
    S.add("sp", (lambda e: e.dma_start(out=fg, in_=fg_d)), writes=("mod",), dma=True)
    hf, hfk = P.alloc(F32, KC, T)
    norm_phase(P, C, xds[0], fg, None, hf, hfk, BLOCKS[:3])
    yr = y.rearrange("(c p) t -> p c t", p=128)
    for q in range(4):
        S.add("sp", (lambda e, q=q: e.dma_start(out=yr[:, q * 4:(q + 1) * 4, :], in_=hf[:, q * 4:(q + 1) * 4, HALO:HALO + OWN])),
              reads=(hfk,), writes=("y",), dma=True)
    P.release(m0)
    P.barrier()
    return P


def finish(P):
    nc, S = P.nc, P.S
    with ExitStack() as st:
        S.finalize(nc, st)
        with nc.Block() as block:
            @block.sync
            def _(e):
                S.emit("sp", e)

            @block.scalar
            def _(e):
                S.emit("act", e)

            @block.vector
            def _(e):
                S.emit("dve", e)

            @block.gpsimd
            def _(e):
                S.emit("pool", e)

            @block.tensor
            def _(e):
                S.emit("pe", e)
    P.stack.close()
    return nc


def build_ada():
    P = Prog()
    S = P.S
    NJ = 12288 // NCORES // 128
    w = P.din("ada_w", [4 * D, NJ * 128])
    b_d = P.din("ada_b", [128, 4 * NJ])
    cv = P.din("cvec", [128, KC, 2])
    out_d = P.dout("mod_out", [128, 4 * NJ, 2])
    cs, csk = P.alloc(F32, KC, 2)
    ss, ssk = P.alloc(F32, KC, 2)
    bb, bbk = P.alloc(F32, 4 * NJ)
    res, resk = P.alloc(F32, 4 * NJ, 2)
    load_small(P, cs, csk, cv)
    load_small(P, bb, bbk, b_d)
    S.add("act", (lambda e: e.activation(out=ss, in_=cs, func=AF.Silu)), reads=(csk,), writes=(ssk,))
    st = [P.alloc(F32, KC, 128) for _ in range(3)]
    n = 0
    pend = []

    def load(l, j):
        sv, sk = st[(l * NJ + j) % 3]
        S.add("sp", (lambda e: e.dma_start(out=sv, in_=w[l * D:(l + 1) * D, j * 128:(j + 1) * 128].rearrange("(kc p) n -> p kc n", p=128))),
              writes=(sk,), dma=True)
    items = [(l, j) for l in range(4) for j in range(NJ)]
    load(*items[0])
    load(*items[1])
    for idx, (l, j) in enumerate(items):
        if idx + 2 < len(items):
            load(*items[idx + 2])
        sv, sk = st[idx % 3]
        b = idx % 4
        pk = ("ps", b)
        psap = P.ps[b][:, 0:2]

        def mm(e, sv=sv, psap=psap):
            ins = None
            for kc in range(KC):
                ins = e.matmul(psap, lhsT=sv[:, kc, :], rhs=ss[:, kc, :], start=(kc == 0), stop=(kc == KC - 1))
            return ins
        S.add("pe", mm, reads=(sk, ssk), writes=(pk,))
        S.add("act", (lambda e, psap=psap, idx=idx: e.activation(out=res[:, idx, :], in_=psap, func=AF.Identity,
                                                                 bias=bb[:, idx:idx + 1], scale=1.0)),
              reads=(pk, bbk), writes=(resk,))
    S.add("sp", (lambda e: e.dma_start(out=out_d, in_=res)), reads=(resk,), writes=("out",), dma=True)
    P.barrier()
    return P


_CACHE = {}


def _prog(name):
    if name not in _CACHE:
        P = build_ada() if name == "A" else (build_fused() if name == "F" else build_stage(name))
        _CACHE[name] = finish(P)
    return _CACHE[name]


def _fm(v):
    return np.ascontiguousarray(np.asarray(v).reshape(-1, 128).T)


def _consts(core):
    s = core * OWN - HALO
    pos = np.arange(s, s + WIN)
    valid = (pos >= 0) & (pos < SEQ)
    mask = np.concatenate([valid[:HALO], valid[-HALO:]]).astype(np.float32)
    mask3 = np.ascontiguousarray(np.broadcast_to(mask[None, None, :], (128, KC, 2 * HALO))).astype(np.float32)
    invc = np.ones((4, T), np.float32)
    for g, w in enumerate((2, 4, 8, 16)):
        lo = np.clip(pos - w // 2, 0, SEQ)
        hi = np.clip(pos + (w - w // 2), 0, SEQ)
        cnt = np.maximum(hi - lo, 1).astype(np.float32)
        invc[g, :WIN] = np.float32(1.0) / cnt
        t = np.arange(CTX)
        lo = np.clip(t - w // 2, 0, CTX)
        hi = np.clip(t + (w - w // 2), 0, CTX)
        invc[g, WIN:] = np.float32(1.0) / (hi - lo).astype(np.float32)
    invc = np.ascontiguousarray(np.broadcast_to(invc[None], (128, 4, T))).astype(np.float32)
    p = np.clip(pos, 0, SEQ - 1)
    row = (p // GRID_W).astype(np.float32)
    col = (p % GRID_W).astype(np.float32)
    nf = 16
    inv = np.power(np.float32(10000.0), -np.arange(nf, dtype=np.float32) / np.float32(nf)).astype(np.float32)
    ar = (row[:, None] * inv).astype(np.float32)
    ac = (col[:, None] * inv).astype(np.float32)
    cos = np.ones((64, T), np.float32)
    sin = np.zeros((64, T), np.float32)
    cos[0:16, :WIN] = np.cos(ar).T
    cos[16:32, :WIN] = np.cos(ar).T
    cos[32:48, :WIN] = np.cos(ac).T
    cos[48:64, :WIN] = np.cos(ac).T
    sin[0:16, :WIN] = np.sin(ar).T
    sin[16:32, :WIN] = np.sin(ar).T
    sin[32:48, :WIN] = np.sin(ac).T
    sin[48:64, :WIN] = np.sin(ac).T
    return mask3, invc, cos, sin


def _rot_lhsT():
    Pm = np.zeros((64, 64), np.float32)
    for base in (0, 32):
        for i in range(16):
            Pm[base + i, base + 16 + i] = -1.0
            Pm[base + 16 + i, base + i] = 1.0
    return np.ascontiguousarray(Pm.T).astype(NPBF16)


FUSED = True


def _kernel_fused(x, c, ctx, c_ctx, ada_w, ada_b, norm1_g, norm2_g, even_w_in, pool_w, pool_scale, conv_w, even_w_out,
                  mla_w_dq, mla_q_norm_g, mla_w_uq, mla_w_dkv, mla_kv_norm_g, mla_w_ukv, mla_w_o,
                  ffn_w_gate, ffn_w_up, ffn_w_down, final_norm_g):
    f32 = lambda a: np.ascontiguousarray(np.asarray(a, dtype=np.float32))
    cores = list(range(NCORES))
    NJ = 96
    cvec = np.stack([_fm(c[0]), _fm(c_ctx)], axis=-1).astype(np.float32)
    ada_w2 = f32(np.asarray(ada_w)).reshape(4 * D, 12288)
    ada_b2 = np.ascontiguousarray(np.asarray(ada_b, dtype=np.float32).reshape(4, NJ, 128).transpose(2, 0, 1).reshape(128, 4 * NJ))
    ngt = np.stack([np.stack([_fm(norm1_g[l]), _fm(norm2_g[l])], 0) for l in range(4)], 0)
    ngt = np.ascontiguousarray(ngt.transpose(2, 0, 1, 3)).astype(np.float32)
    ones = np.ones((128, 128), NPBF16)
    rot = _rot_lhsT()
    consts = [_consts(k) for k in cores]
    xp = np.zeros((SEQ + 2 * HALO, D), np.float32)
    xp[HALO:HALO + SEQ] = x[0]
    wins = [np.ascontiguousarray(np.concatenate([xp[k * OWN:k * OWN + WIN], ctx[0]], axis=0).T) for k in cores]
    Wd = {}
    for nm, arr in (("even_w_in", even_w_in), ("pool_w", pool_w), ("pool_scale", pool_scale), ("conv_w", conv_w),
                    ("even_w_out", even_w_out), ("mla_w_dq", mla_w_dq), ("mla_q_norm_g", mla_q_norm_g), ("mla_w_uq", mla_w_uq),
                    ("mla_w_dkv", mla_w_dkv), ("mla_kv_norm_g", mla_kv_norm_g), ("mla_w_ukv", mla_w_ukv), ("mla_w_o", mla_w_o),
                    ("ffn_w_gate", ffn_w_gate), ("ffn_w_up", ffn_w_up), ("ffn_w_down", ffn_w_down)):
        arr = np.asarray(arr)
        for i in range(arr.shape[0]):
            a = f32(arr[i])
            if nm in ("pool_scale", "mla_q_norm_g", "mla_kv_norm_g"):
                a = _fm(a)
            elif nm == "conv_w":
                a = np.ascontiguousarray(a.reshape(3, 8, 128).transpose(2, 0, 1))
            Wd["%s_%d" % (nm, i)] = a
    fgain = np.ascontiguousarray(np.broadcast_to(_fm(final_norm_g)[:, :, None], (128, KC, 2))).astype(np.float32)
    pf = _prog("F")
    ins = []
    for k in cores:
        order = [(k + sl) % NCORES for sl in range(NCORES)]
        d = {"c_ones": ones, "c_rot": rot, "ada_w": ada_w2, "ada_b": ada_b2, "cvec": cvec, "ng": ngt,
             "x_in": np.stack([wins[w] for w in order], 0),
             "rope_cos": np.stack([consts[w][2] for w in order], 0), "rope_sin": np.stack([consts[w][3] for w in order], 0),
             "mask3": np.stack([consts[w][0] for w in order], 0), "invc": np.stack([consts[w][1] for w in order], 0),
             "fgain": fgain}
        d.update(Wd)
        ins.append(d)
    rf = run_bass_kernel_spmd(pf, ins, core_ids=cores).results
    out = np.concatenate([np.asarray(rf[k]["y"]).T for k in cores], axis=0)
    return np.ascontiguousarray(out[None].astype(np.float32))


def kernel(x, c, ctx, c_ctx, ada_w, ada_b, norm1_g, norm2_g, even_w_in, pool_w, pool_scale, conv_w, even_w_out,
           mla_w_dq, mla_q_norm_g, mla_w_uq, mla_w_dkv, mla_kv_norm_g, mla_w_ukv, mla_w_o,
           ffn_w_gate, ffn_w_up, ffn_w_down, final_norm_g):
    f32 = lambda a: np.ascontiguousarray(np.asarray(a, dtype=np.float32))
    x, c, ctx, c_ctx = f32(x), f32(c), f32(ctx), f32(c_ctx)
    cores = list(range(NCORES))
    if FUSED:
        return _kernel_fused(x, c, ctx, c_ctx, ada_w, ada_b, norm1_g, norm2_g, even_w_in, pool_w, pool_scale, conv_w,
                             even_w_out, mla_w_dq, mla_q_norm_g, mla_w_uq, mla_w_dkv, mla_kv_norm_g, mla_w_ukv, mla_w_o,
                             ffn_w_gate, ffn_w_up, ffn_w_down, final_norm_g)
    NJ = 12
    cvec = np.stack([_fm(c[0]), _fm(c_ctx)], axis=-1).astype(np.float32)
    ada_w = np.asarray(ada_w)
    ada_b = np.asarray(ada_b)
    inA = []
    for k in cores:
        wk = np.ascontiguousarray(ada_w[:, :, k * 1536:(k + 1) * 1536]).reshape(4 * D, 1536)
        bk = np.ascontiguousarray(ada_b[:, k * 1536:(k + 1) * 1536].reshape(4, NJ, 128).transpose(2, 0, 1).reshape(128, 4 * NJ))
        inA.append({"ada_w": wk, "ada_b": bk, "cvec": cvec, "c_ones": np.ones((128, 128), NPBF16), "c_rot": _rot_lhsT()})
    pa = _prog("A")
    ra = run_bass_kernel_spmd(pa, [{k: v for k, v in m.items() if k in ("ada_w", "ada_b", "cvec")} for m in inA], core_ids=cores)
    modf = np.zeros((4, 12288, 2), np.float32)
    for k in cores:
        r = np.asarray(ra.results[k]["mod_out"]).reshape(128, 4, NJ, 2)
        modf[:, k * 1536:(k + 1) * 1536, :] = r.transpose(1, 2, 0, 3).reshape(4, 1536, 2)
    modt = np.ascontiguousarray(modf.reshape(4, 6, KC, 128, 2).transpose(3, 0, 1, 2, 4))
    ngt = np.stack([np.stack([_fm(norm1_g[l]), _fm(norm2_g[l])], 0) for l in range(4)], 0)
    ngt = np.ascontiguousarray(ngt.transpose(2, 0, 1, 3)).astype(np.float32)

    ones = np.ones((128, 128), NPBF16)
    rot = _rot_lhsT()
    consts = [_consts(k) for k in cores]
    xs = []
    xp = np.zeros((SEQ + 2 * HALO, D), np.float32)
    xp[HALO:HALO + SEQ] = x[0]
    for k in cores:
        st = np.concatenate([xp[k * OWN:k * OWN + WIN], ctx[0]], axis=0)
        xs.append(np.ascontiguousarray(st.T))

    def common(k, layers):
        mask3, invc, cos, sin = consts[k]
        return {"c_ones": ones, "c_rot": rot, "mod": np.ascontiguousarray(modt[:, layers]),
                "ng": np.ascontiguousarray(ngt[:, layers]), "x_in": xs[k], "rope_cos": cos, "rope_sin": sin,
                "mask3": mask3, "invc": invc}

    def lw(names_idx):
        d = {}
        for nm, arr, i in names_idx:
            a = f32(np.asarray(arr)[i])
            if nm in ("pool_scale", "mla_q_norm_g", "mla_kv_norm_g"):
                a = _fm(a)
            elif nm == "conv_w":
                a = np.ascontiguousarray(a.reshape(3, 8, 128).transpose(2, 0, 1))
            d["%s_%d" % (nm, i)] = a
        return d

    def gather_kv(res):
        lat = np.concatenate([np.asarray(res[k]["kv_out"])[:, :OWN] for k in cores], axis=1)
        return np.ascontiguousarray(np.concatenate([lat, np.asarray(res[0]["kv_out"])[:, OWN:]], axis=1))

    wB = lw([("even_w_in", even_w_in, 0), ("pool_w", pool_w, 0), ("pool_scale", pool_scale, 0), ("conv_w", conv_w, 0),
             ("even_w_out", even_w_out, 0), ("ffn_w_gate", ffn_w_gate, 0), ("ffn_w_up", ffn_w_up, 0),
             ("ffn_w_down", ffn_w_down, 0), ("mla_w_dq", mla_w_dq, 0), ("mla_q_norm_g", mla_q_norm_g, 0),
             ("mla_w_dkv", mla_w_dkv, 0), ("mla_kv_norm_g", mla_kv_norm_g, 0)])
    pb = _prog("B")
    rb = run_bass_kernel_spmd(pb, [dict(common(k, [0, 1]), **wB) for k in cores], core_ids=cores).results
    xs = [np.asarray(rb[k]["x_out"]) for k in cores]
    kv_all = gather_kv(rb)
    cqs = [np.asarray(rb[k]["cq_out"]) for k in cores]
    del wB
    wC = lw([("mla_w_uq", mla_w_uq, 0), ("mla_w_ukv", mla_w_ukv, 0), ("mla_w_o", mla_w_o, 0),
             ("ffn_w_gate", ffn_w_gate, 1), ("ffn_w_up", ffn_w_up, 1), ("ffn_w_down", ffn_w_down, 1),
             ("even_w_in", even_w_in, 1), ("pool_w", pool_w, 1), ("pool_scale", pool_scale, 1), ("conv_w", conv_w, 1),
             ("even_w_out", even_w_out, 1), ("ffn_w_gate", ffn_w_gate, 2), ("ffn_w_up", ffn_w_up, 2),
             ("ffn_w_down", ffn_w_down, 2), ("mla_w_dq", mla_w_dq, 1), ("mla_q_norm_g", mla_q_norm_g, 1),
             ("mla_w_dkv", mla_w_dkv, 1), ("mla_kv_norm_g", mla_kv_norm_g, 1)])
    pc = _prog("C")
    rc = run_bass_kernel_spmd(pc, [dict(common(k, [1, 2, 3]), kv_all=kv_all, cq_in=cqs[k], **wC) for k in cores],
                              core_ids=cores).results
    xs = [np.asarray(rc[k]["x_out"]) for k in cores]
    kv_all = gather_kv(rc)
    cqs = [np.asarray(rc[k]["cq_out"]) for k in cores]
    del wC
    wD = lw([("mla_w_uq", mla_w_uq, 1), ("mla_w_ukv", mla_w_ukv, 1), ("mla_w_o", mla_w_o, 1),
             ("ffn_w_gate", ffn_w_gate, 3), ("ffn_w_up", ffn_w_up, 3), ("ffn_w_down", ffn_w_down, 3)])
    fgain = np.ascontiguousarray(np.broadcast_to(_fm(final_norm_g)[:, :, None], (128, KC, 2))).astype(np.float32)
    pd = _prog("D")
    inD = []
    for k in cores:
        m = common(k, [3])
        for nm in ("mask3", "invc"):
            m.pop(nm)
        m.update(kv_all=kv_all, cq_in=cqs[k], fgain=fgain, **wD)
        inD.append(m)
    rd = run_bass_kernel_spmd(pd, inD, core_ids=cores).results
    out = np.concatenate([np.asarray(rd[k]["y"]).T for k in cores], axis=0)
    return np.ascontiguousarray(out[None].astype(np.float32))
```

```python
import numpy as np
import ml_dtypes
from contextlib import ExitStack
import concourse.bass as bass
import concourse.mybir as mybir
from concourse.bass_utils import run_bass_kernel_spmd

F32 = mybir.dt.float32
BF16 = mybir.dt.bfloat16
AF = mybir.ActivationFunctionType
ALU = mybir.AluOpType
NPBF16 = ml_dtypes.bfloat16

NCORES = 8
D = 2048
KC = 16
SEQ = 8192
CTX = 256
HALO = 16
OWN = SEQ // NCORES
WIN = OWN + 2 * HALO
T = WIN + CTX
BLOCKS = [(0, 352, 0), (352, 352, 0), (704, 352, 0), (1056, 256, 1)]
DFF = 5632
FC = DFF // 128
NKEYS = SEQ + CTX
NKC = NKEYS // 128
EPS = 1e-6
NH = 16
GRID_W = 64
ENGS = ("sp", "act", "dve", "pool", "pe")
KD = 6


class Op:
    __slots__ = ("eng", "fn", "deps", "flag", "sem", "val", "dma")


class Sched:
    def __init__(self):
        self.ops = {e: [] for e in ENGS}
        self.lastw = {}
        self.rd = {}
        self.dmal = {e: [] for e in ENGS}
        self.lastreal = {e: None for e in ENGS}

    def add(self, eng, fn, reads=(), writes=(), dma=False):
        op = Op()
        op.eng, op.fn, op.dma, op.flag = eng, fn, dma, dma
        op.sem = None
        op.val = 0
        deps = []
        for k in reads:
            w = self.lastw.get(k)
            if w is not None:
                deps.append(w)
        for k in writes:
            w = self.lastw.get(k)
            if w is not None:
                deps.append(w)
            deps.extend(self.rd.get(k, ()))
        f = []
        for d in deps:
            if d.eng == "pe" and eng == "pe" and not d.dma and not dma:
                continue
            d.flag = True
            f.append(d)
        op.deps = f
        for k in writes:
            self.lastw[k] = op
            self.rd[k] = []
        for k in reads:
            lst = self.rd.setdefault(k, [])
            if not dma:
                for i, o in enumerate(lst):
                    if o.eng == eng and not o.dma:
                        lst[i] = op
                        break
                else:
                    lst.append(op)
            else:
                lst.append(op)
        self.ops[eng].append(op)
        if dma:
            self.dmal[eng].append(op)
        elif fn is not None:
            self.lastreal[eng] = op
        return op

    def barrier(self):
        deps = []
        for e in ENGS:
            if self.lastreal[e] is not None:
                deps.append(self.lastreal[e])
            deps.extend(self.dmal[e][-KD:])
        for d in deps:
            d.flag = True
        for e in ENGS:
            op = Op()
            op.eng, op.fn, op.dma, op.flag, op.sem, op.val = e, None, False, False, None, 0
            op.deps = [d for d in deps if not (d.eng == e and not d.dma)]
            self.ops[e].append(op)
        self.lastw = {}
        self.rd = {}

    def finalize(self, nc, stack):
        self.sem = {e: stack.enter_context(nc.semaphore("s_" + e)) for e in ENGS}
        self.dsem = {e: [stack.enter_context(nc.semaphore("d_%s_%d" % (e, i))) for i in range(KD)] for e in ENGS}
        for e in ENGS:
            cnt = 0
            dl = self.dmal[e]
            nd = 0
            for op in self.ops[e]:
                if op.dma:
                    assert dl[nd] is op
                    op.sem = self.dsem[e][nd % KD]
                    op.val = 16 * (nd // KD + 1)
                    if nd >= KD:
                        op.deps.append(dl[nd - KD])
                    nd += 1
                elif op.flag:
                    cnt += 1
                    op.sem = self.sem[e]
                    op.val = cnt

    def emit(self, ename, eng):
        waited = {}
        for op in self.ops[ename]:
            need = {}
            for d in op.deps:
                key = d.sem
                if d.val > need.get(key, (None, 0))[1]:
                    need[key] = (d.sem, d.val)
            for key, (sem, val) in need.items():
                if waited.get(key, 0) < val:
                    eng.wait_ge(sem, val)
                    waited[key] = val
            ins = op.fn(eng) if op.fn is not None else None
            if op.flag:
                ins.then_inc(op.sem, 16 if op.dma else 1)


class Prog:
    def __init__(self):
        self.nc = bass.Bass("TRN2", target_bir_lowering=False)
        self.S = Sched()
        self.stack = ExitStack()
        self.AE = 51200
        self.arena = self.stack.enter_context(self.nc.sbuf_tensor("arena", [128, self.AE], F32))
        self.psall = self.stack.enter_context(self.nc.psum_tensor("psall", [128, 8, 512], F32))
        self.ps = [self.psall[:, i, :] for i in range(8)]
        self.off = 0
        self.uid = 0
        self.inputs = {}
        self.psrot = 0

    def din(self, name, shape, dt=F32):
        self.inputs[name] = (tuple(shape), dt)
        return self.nc.dram_tensor(name, list(shape), dt, kind="ExternalInput").ap()

    def dbg(self, name, view, key, shape, dt=F32):
        d = self.nc.dram_tensor(name, [128] + list(shape), dt, kind="ExternalOutput").ap()
        self.S.add("sp", (lambda e: e.dma_start(out=d, in_=view)), reads=(key,), writes=("dbg_" + name,), dma=True)

    def dout(self, name, shape, dt=F32):
        return self.nc.dram_tensor(name, list(shape), dt, kind="ExternalOutput").ap()

    def dint(self, name, shape, dt=F32):
        return self.nc.dram_tensor(name, list(shape), dt).ap()

    def mark(self):
        return self.off

    def release(self, m):
        self.S.barrier()
        self.off = m

    def alloc(self, dt, *free):
        n = 1
        for f in free:
            n *= f
        nbytes = n * (2 if dt is BF16 else 4)
        nbytes = (nbytes + 63) // 64 * 64
        o = self.off
        self.off += nbytes
        self.hw = max(getattr(self, "hw", 0), self.off)
        assert self.off <= self.AE * 4, "SBUF arena overflow %d" % self.off
        a = self.arena[:, o // 4:(o + nbytes) // 4]
        if dt is BF16:
            a = a.bitcast(BF16)
        a = a[:, 0:n]
        if len(free) == 2:
            a = a.rearrange("p (a b) -> p a b", a=free[0])
        elif len(free) == 3:
            a = a.rearrange("p (a b c) -> p a b c", a=free[0], b=free[1])
        self.uid += 1
        return a, "b%d" % self.uid

    def barrier(self):
        self.S.barrier()

    def psbank(self, banks):
        b = banks[self.psrot % len(banks)]
        self.psrot += 1
        return b


def A_(eng, fn, reads=(), writes=(), dma=False, P=None):
    return P.S.add(eng, fn, reads, writes, dma)


def cast_op(S, eng, out_ap, in_ap, reads, writes):
    if eng == "act":
        S.add("act", (lambda e: e.activation(out=out_ap, in_=in_ap, func=AF.Identity)), reads=reads, writes=writes)
    else:
        S.add(eng, (lambda e: e.tensor_copy(out=out_ap, in_=in_ap)), reads=reads, writes=writes)


def linear(P, in_v, in_key, kcn, panels, blocks, epi, pre_chunk=None, post_chunk=None, post_panel=None,
           banks=(0, 1, 2, 3), nbuf=2, tag="w", cast=("dve", "act")):
    S = P.S
    maxc = max(sum(s[1] for s in p) for p in panels)
    m0 = P.mark()
    st = [P.alloc(F32, kcn, maxc) for _ in range(nbuf)]
    wb = [P.alloc(BF16, kcn, maxc) for _ in range(nbuf)]

    def load(pi):
        sv, sk = st[pi % nbuf]
        bv, bk = wb[pi % nbuf]
        c0 = 0
        for (wap, ncol) in panels[pi]:
            src = wap.rearrange("(kc p) n -> p kc n", p=128)
            dst = sv[:, :, c0:c0 + ncol]
            S.add("sp", (lambda e, d=dst, s=src: e.dma_start(out=d, in_=s)), reads=(), writes=(sk,), dma=True)
            c0 += ncol
        cast_op(S, cast[pi % len(cast)], bv[:, :, 0:c0], sv[:, :, 0:c0], (sk,), (bk,))

    load(0)
    for pi in range(len(panels)):
        if pi + 1 < len(panels):
            load(pi + 1)
        bv, bk = wb[pi % nbuf]
        c0 = 0
        ci = 0
        for (wap, ncol) in panels[pi]:
            for cc in range(0, ncol, 128):
                M = min(128, ncol - cc)
                if pre_chunk:
                    pre_chunk(pi, ci)
                for bi, (o, n, v) in enumerate(blocks):
                    b = P.psbank(banks)
                    pk = ("ps", b)
                    psap = P.ps[b][0:M, 0:n]

                    def mm(e, psap=psap, bv=bv, c=c0 + cc, M=M, o=o, n=n):
                        ins = None
                        for kc in range(kcn):
                            ins = e.matmul(psap, lhsT=bv[:, kc, c:c + M], rhs=in_v[:, kc, o:o + n],
                                           start=(kc == 0), stop=(kc == kcn - 1))
                        return ins
                    S.add("pe", mm, reads=(bk, in_key), writes=(pk,))
                    epi(pi, ci, bi, psap, pk, M)
                if post_chunk:
                    post_chunk(pi, ci)
                ci += 1
            c0 += ncol
        if post_panel:
            post_panel(pi)
    P.release(m0)


def split_cols(w, c0, c1, step):
    return [[(w[:, c:min(c + step, c1)], min(c + step, c1) - c)] for c in range(c0, c1, step)]


def load_small(P, dst, key, src):
    P.S.add("sp", (lambda e: e.dma_start(out=dst, in_=src)), writes=(key,), dma=True)


def rms_stats(P, src3, src_key, nchunk, o, n, sq, sqk, rstd, rk, ones, bank, inv_n):
    S = P.S
    S.add("act", (lambda e: e.activation(out=sq[:, 0:nchunk, 0:n], in_=src3[:, 0:nchunk, o:o + n], func=AF.Square)),
          reads=(src_key,), writes=(sqk,))
    pk = ("ps", bank)
    psap = P.ps[bank][:, 0:n]

    def mm(e):
        ins = None
        for c in range(nchunk):
            ins = e.matmul(psap, lhsT=ones, rhs=sq[:, c, 0:n], start=(c == 0), stop=(c == nchunk - 1))
        return ins
    S.add("pe", mm, reads=(sqk, "const"), writes=(pk,))
    S.add("dve", (lambda e: e.tensor_scalar(out=rstd[:, 0:n], in0=psap, scalar1=inv_n, scalar2=EPS,
                                            op0=ALU.mult, op1=ALU.add)), reads=(pk,), writes=(rk,))
    S.add("act", (lambda e: e.sqrt(out=rstd[:, 0:n], in_=rstd[:, 0:n])), reads=(rk,), writes=(rk,))
    S.add("dve", (lambda e: e.reciprocal(out=rstd[:, 0:n], in_=rstd[:, 0:n])), reads=(rk,), writes=(rk,))


def norm_phase(P, C, xd, Amod, Bmod, h, hk, blocks, mask3=None):
    S = P.S
    m0 = P.mark()
    xs = [P.alloc(F32, KC, 352) for _ in range(2)]
    sq, sqk = P.alloc(BF16, KC, 352)
    rs = [P.alloc(F32, 352) for _ in range(2)]
    tmp = [P.alloc(F32, 352) for _ in range(4)]
    xr = xd.rearrange("(c p) t -> p c t", p=128)
    ti = 0
    for bi, (o, n, v) in enumerate(blocks):
        xv, xk = xs[bi % 2]
        rv, rk = rs[bi % 2]
        S.add("sp", (lambda e, xv=xv, o=o, n=n: e.dma_start(out=xv[:, :, 0:n], in_=xr[:, :, o:o + n])),
              reads=("xd",), writes=(xk,), dma=True)
        rms_stats(P, xv, xk, KC, 0, n, sq, sqk, rv, rk, C["ones"], 7, 1.0 / D)
        for c in range(KC):
            tv, tk = tmp[ti % 4]
            ti += 1
            S.add("dve", (lambda e, tv=tv, xv=xv, c=c, n=n, v=v, rv=rv: e.scalar_tensor_tensor(
                out=tv[:, 0:n], in0=xv[:, c, 0:n], scalar=Amod[:, c, v:v + 1], in1=rv[:, 0:n],
                op0=ALU.mult, op1=ALU.mult)), reads=(xk, rk, "mod"), writes=(tk,))
            if Bmod is not None:
                S.add("act", (lambda e, tv=tv, c=c, o=o, n=n, v=v: e.activation(
                    out=h[:, c, o:o + n], in_=tv[:, 0:n], func=AF.Identity, bias=Bmod[:, c, v:v + 1], scale=1.0)),
                    reads=(tk, "mod"), writes=(hk,))
            else:
                S.add("act", (lambda e, tv=tv, c=c, o=o, n=n: e.activation(
                    out=h[:, c, o:o + n], in_=tv[:, 0:n], func=AF.Identity)), reads=(tk,), writes=(hk,))
    if mask3 is not None:
        for (a, b, ma) in ((0, HALO, 0), (WIN - HALO, WIN, HALO)):
            S.add("dve", (lambda e, a=a, b=b, ma=ma: e.tensor_tensor(
                out=h[:, :, a:b], in0=h[:, :, a:b], in1=mask3[:, :, ma:ma + HALO], op=ALU.mult)),
                reads=(hk, "const"), writes=(hk,))
    P.release(m0)


class Resid:
    def __init__(self, P, xd, gmod, blocks):
        self.P, self.xd, self.g, self.blocks = P, xd, gmod, blocks
        self.xt = [P.alloc(F32, T) for _ in range(2)]
        self.n = 0
        self.lo = min(b[0] for b in blocks)
        self.hi = max(b[0] + b[1] for b in blocks)

    def pre(self, pi, ci):
        self.cur = self.xt[self.n % 2]
        self.c = self.n
        self.n += 1
        xv, xk = self.cur
        c = self.c
        self.P.S.add("sp", (lambda e: e.dma_start(out=xv[:, self.lo:self.hi],
                                                  in_=self.xd[c * 128:(c + 1) * 128, self.lo:self.hi])),
                     reads=("xd",), writes=(xk,), dma=True)

    def epi(self, pi, ci, bi, ps, pk, M):
        xv, xk = self.cur
        o, n, v = self.blocks[bi]
        c = self.c
        self.P.S.add("dve", (lambda e: e.scalar_tensor_tensor(
            out=xv[:, o:o + n], in0=ps, scalar=self.g[:, c, v:v + 1], in1=xv[:, o:o + n],
            op0=ALU.mult, op1=ALU.add)), reads=(pk, xk, "mod"), writes=(xk,))

    def post(self, pi, ci):
        xv, xk = self.cur
        c = self.c
        self.P.S.add("sp", (lambda e: e.dma_start(out=self.xd[c * 128:(c + 1) * 128, self.lo:self.hi],
                                                  in_=xv[:, self.lo:self.hi])),
                     reads=(xk,), writes=("xd",), dma=True)


def ffn_phase(P, C, xd, w_gate, w_up, w_down, ud, Amod, Bmod, gmod, blocks):
    S = P.S
    m0 = P.mark()
    h, hk = P.alloc(BF16, KC, T)
    norm_phase(P, C, xd, Amod, Bmod, h, hk, blocks)
    sg = [P.alloc(F32, T) for _ in range(2)]
    ub = [P.alloc(BF16, T) for _ in range(3)]
    lo = min(b[0] for b in blocks)
    hi = max(b[0] + b[1] for b in blocks)
    panels = [[(w_gate[:, f * 128:(f + 1) * 128], 128), (w_up[:, f * 128:(f + 1) * 128], 128)] for f in range(FC)]

    def epi(pi, ci, bi, ps, pk, M):
        o, n, v = blocks[bi]
        sv, sk = sg[pi % 2]
        uv, uk = ub[pi % 3]
        if ci == 0:
            S.add("act", (lambda e: e.activation(out=sv[:, o:o + n], in_=ps, func=AF.Silu)), reads=(pk,), writes=(sk,))
        else:
            S.add("dve", (lambda e: e.tensor_tensor(out=uv[:, o:o + n], in0=sv[:, o:o + n], in1=ps, op=ALU.mult)),
                  reads=(pk, sk), writes=(uk,))

    def post_panel(pi):
        uv, uk = ub[pi % 3]
        S.add("sp", (lambda e: e.dma_start(out=ud[pi * 128:(pi + 1) * 128, lo:hi], in_=uv[:, lo:hi])),
              reads=(uk,), writes=("ud",), dma=True)
    linear(P, h, hk, KC, panels, blocks, epi, post_panel=post_panel, banks=(0, 1, 2, 3, 4, 5))
    P.release(m0)
    P.barrier()
    m0 = P.mark()
    u, ukk = P.alloc(BF16, FC, T)
    ur = ud.rearrange("(c p) t -> p c t", p=128)
    for q in range(4):
        S.add("sp", (lambda e, q=q: e.dma_start(out=u[:, q * 11:(q + 1) * 11, lo:hi], in_=ur[:, q * 11:(q + 1) * 11, lo:hi])),
              reads=("ud",), writes=(ukk,), dma=True)
    R = Resid(P, xd, gmod, blocks)
    linear(P, u, ukk, FC, split_cols(w_down, 0, D, 128), blocks, R.epi, pre_chunk=R.pre, post_chunk=R.post,
           banks=(0, 1, 2, 3, 4, 5))
    P.release(m0)
    P.barrier()


LATP = 16
TP = (WIN + 2 * LATP) + (CTX + 2 * LATP)
SEQS = ((LATP, WIN, 0), (WIN + 3 * LATP, CTX, WIN))


def even_mixer(P, C, xd, h, hk, w_in, pool_w, pool_scale, conv_w, w_out, gmod, blocks, invc_d):
    S = P.S
    m0 = P.mark()
    yab, yk = P.alloc(BF16, KC, T)
    pwb, pwbk = P.alloc(BF16, 4, 2, 256)
    psc, psck = P.alloc(F32, 8)
    cw, cwk = P.alloc(F32, 3, 8)
    mA = P.mark()
    pw32, pw32k = P.alloc(F32, 4, 2, 256)
    load_small(P, pw32, pw32k, pool_w.rearrange("g (kc p) n -> p g kc n", p=128))
    S.add("pool", (lambda e: e.tensor_copy(out=pwb, in_=pw32)), reads=(pw32k,), writes=(pwbk,))
    load_small(P, psc, psck, pool_scale)
    load_small(P, cw, cwk, conv_w)
    P.barrier()
    P.release(mA)
    invg = [P.alloc(F32, T) for _ in range(2)]
    U, Uk = P.alloc(F32, 2, TP)
    LA, LAk = P.alloc(F32, 2, TP)
    LB, LBk = P.alloc(F32, 2, TP)
    pb, pbk = P.alloc(BF16, 2, T)
    for (buf, k) in ((U, Uk), (LA, LAk), (LB, LBk)):
        S.add("pool", (lambda e, buf=buf: e.memset(buf, 0.0)), writes=(k,))
    for g in range(4):
        iv, ik = invg[g % 2]
        if g < 2:
            load_small(P, iv, ik, invc_d[:, g, :])

    def epi_pool(pi, ci, bi, ps, pk, M):
        o, n, v = blocks[bi]
        po = (LATP + o) if v == 0 else (WIN + 3 * LATP + o - WIN)
        S.add("act", (lambda e: e.activation(out=U[:, ci, po:po + n], in_=ps, func=AF.Identity)),
              reads=(pk,), writes=(Uk,))

    def post_pool(g):
        iv, ik = invg[g % 2]
        nlev = g + 1
        src, srck = U, Uk
        dsts = [(LA, LAk), (LB, LBk)]
        sh = [(1, 0), (1, 1), (2, 2), (4, 4)]
        for l in range(nlev):
            dst, dstk = dsts[l % 2]
            a, b = sh[l]
            for (so, sn, to) in SEQS:
                lo_, hi_ = so - 8, so + sn + 8
                S.add("dve", (lambda e, dst=dst, src=src, lo_=lo_, hi_=hi_, a=a, b=b: e.tensor_tensor(
                    out=dst[:, :, lo_:hi_], in0=src[:, :, lo_ - a:hi_ - a], in1=src[:, :, lo_ + b:hi_ + b], op=ALU.add)),
                    reads=(srck,), writes=(dstk,))
            src, srck = dst, dstk
        for (so, sn, to) in SEQS:
            for j in range(2):
                S.add("dve", (lambda e, src=src, so=so, sn=sn, to=to, j=j: e.tensor_tensor(
                    out=src[:, j, so:so + sn], in0=src[:, j, so:so + sn], in1=iv[:, to:to + sn], op=ALU.mult)),
                    reads=(srck, ik), writes=(srck,))
            S.add("dve", (lambda e, src=src, so=so, sn=sn, to=to: e.tensor_tensor(
                out=pb[:, :, to:to + sn], in0=src[:, :, so:so + sn], in1=U[:, :, so:so + sn], op=ALU.subtract)),
                reads=(srck, Uk), writes=(pbk,))
        if g + 2 < 4:
            load_small(P, iv, ik, invc_d[:, g + 2, :])
        for nn in range(2):
            for bi, (o, n, v) in enumerate(blocks):
                b = P.psbank((4, 5))
                pk = ("ps", b)
                psap = P.ps[b][:, 0:n]

                def mm(e, psap=psap, nn=nn, o=o, n=n):
                    ins = None
                    for kc in range(2):
                        ins = e.matmul(psap, lhsT=pwb[:, g, kc, nn * 128:(nn + 1) * 128], rhs=pb[:, kc, o:o + n],
                                       start=(kc == 0), stop=(kc == 1))
                    return ins
                S.add("pe", mm, reads=(pwbk, pbk), writes=(pk,))
                S.add("act", (lambda e, psap=psap, nn=nn, o=o, n=n: e.activation(
                    out=yab[:, 2 * g + nn, o:o + n], in_=psap, func=AF.Identity, scale=psc[:, 2 * g + nn:2 * g + nn + 1])),
                    reads=(pk, psck), writes=(yk,))

    linear(P, h, hk, KC, split_cols(w_in, 0, 1024, 256), blocks, epi_pool, post_panel=post_pool, banks=(0, 1, 2, 3))
    if "yab" in DEBUG["dump"]:
        P.dbg("dbg_U", U, Uk, [2, TP])
        P.dbg("dbg_LB", LB, LBk, [2, TP])
        P.dbg("dbg_pb", pb, pbk, [2, T], BF16)
    P.barrier()
    P.release(mA)

    TPC = T + 4
    CSEQ = ((1, WIN, 0), (WIN + 3, CTX, WIN))
    gbs, gbk = P.alloc(F32, T)
    gcf, gck = P.alloc(F32, T)
    uc, uck = P.alloc(F32, TPC)
    cv, cvk = P.alloc(F32, T)
    S.add("pool", (lambda e: e.memset(uc, 0.0)), writes=(uck,))
    panels = [[(w_in[:, 1024 + c * 128:1024 + (c + 1) * 128], 128), (w_in[:, 2048 + c * 128:2048 + (c + 1) * 128], 128),
               (w_in[:, 3072 + c * 128:3072 + (c + 1) * 128], 128)] for c in range(8)]

    def epi_conv(pi, ci, bi, ps, pk, M):
        o, n, v = blocks[bi]
        if ci == 0:
            S.add("act", (lambda e: e.activation(out=gbs[:, o:o + n], in_=ps, func=AF.Identity)), reads=(pk,), writes=(gbk,))
        elif ci == 1:
            S.add("act", (lambda e: e.activation(out=gcf[:, o:o + n], in_=ps, func=AF.Identity)), reads=(pk,), writes=(gck,))
        else:
            po = (1 + o) if v == 0 else (WIN + 3 + o - WIN)
            S.add("dve", (lambda e: e.tensor_tensor(out=uc[:, po:po + n], in0=gcf[:, o:o + n], in1=ps, op=ALU.mult)),
                  reads=(pk, gck), writes=(uck,))

    def post_conv(c):
        for (so, sn, to) in CSEQ:
            S.add("dve", (lambda e, so=so, sn=sn, to=to: e.tensor_scalar(
                out=cv[:, to:to + sn], in0=uc[:, so - 1:so - 1 + sn], scalar1=cw[:, 0, c:c + 1], scalar2=None, op0=ALU.mult)),
                reads=(uck, cwk), writes=(cvk,))
            for k in (1, 2):
                S.add("dve", (lambda e, so=so, sn=sn, to=to, k=k: e.scalar_tensor_tensor(
                    out=cv[:, to:to + sn], in0=uc[:, so - 1 + k:so - 1 + k + sn], scalar=cw[:, k, c:c + 1],
                    in1=cv[:, to:to + sn], op0=ALU.mult, op1=ALU.add)), reads=(uck, cwk, cvk), writes=(cvk,))
        S.add("dve", (lambda e: e.tensor_tensor(out=yab[:, 8 + c, :], in0=gbs, in1=cv, op=ALU.mult)),
              reads=(gbk, cvk), writes=(yk,))

    linear(P, h, hk, KC, panels, blocks, epi_conv, post_panel=post_conv, banks=(0, 1, 2, 3, 4, 5))
    if "yab" in DEBUG["dump"]:
        P.dbg("dbg_yab", yab, yk, [KC, T], BF16)
        P.dbg("dbg_gbs", gbs, gbk, [T])
        P.dbg("dbg_uc", uc, uck, [TPC])
        P.dbg("dbg_cv", cv, cvk, [T])
    if DEBUG["stop"] == "conv":
        raise StopBuild(P)
    P.barrier()
    P.release(mA)
    R = Resid(P, xd, gmod, blocks)
    linear(P, yab, yk, KC, split_cols(w_out, 0, D, 256), blocks, R.epi, pre_chunk=R.pre, post_chunk=R.post,
           banks=(0, 1, 2, 3, 4, 5))
    P.release(m0)
    P.barrier()


def odd_pre(P, C, h, hk, w_dq, qg_d, w_dkv, kvg_d, rope_cos_d, rope_sin_d, kv_lat, kv_ctx, cq_out, blocks):
    S = P.S
    m0 = P.mark()
    qg, qgk = P.alloc(F32, 4)
    kvg, kvgk = P.alloc(F32, 4)
    load_small(P, qg, qgk, qg_d)
    load_small(P, kvg, kvgk, kvg_d)
    cos, cosk = P.alloc(F32, T)
    sin, sink = P.alloc(F32, T)
    load_small(P, cos[0:64, :], cosk, rope_cos_d)
    load_small(P, sin[0:64, :], sink, rope_sin_d)
    c32, c32k = P.alloc(F32, 4, T)
    kr32, kr32k = P.alloc(F32, T)
    krb, krbk = P.alloc(BF16, T)
    kro, krok = P.alloc(BF16, T)
    cn, cnk = P.alloc(BF16, 4, T)
    sq, sqk = P.alloc(BF16, 4, 352)
    rs = [P.alloc(F32, 352) for _ in range(2)]
    t1 = [P.alloc(F32, 352) for _ in range(2)]
    t2 = [P.alloc(F32, 352) for _ in range(2)]

    def epi(pi, ci, bi, ps, pk, M):
        o, n, v = blocks[bi]
        ch = pi * 2 + ci
        if M == 128:
            S.add("act", (lambda e: e.activation(out=c32[:, ch, o:o + n], in_=ps, func=AF.Identity)),
                  reads=(pk,), writes=(c32k,))
        else:
            S.add("act", (lambda e: e.activation(out=kr32[0:64, o:o + n], in_=ps, func=AF.Identity)),
                  reads=(pk,), writes=(kr32k,))

    def normalize(g):
        for bi, (o, n, v) in enumerate(blocks):
            rv, rk = rs[bi % 2]
            rms_stats(P, c32, c32k, 4, o, n, sq, sqk, rv, rk, C["ones"], 7, 1.0 / 512)
            for c in range(4):
                S.add("dve", (lambda e, c=c, o=o, n=n, rv=rv: e.scalar_tensor_tensor(
                    out=cn[:, c, o:o + n], in0=c32[:, c, o:o + n], scalar=g[:, c:c + 1], in1=rv[:, 0:n],
                    op0=ALU.mult, op1=ALU.mult)), reads=(c32k, rk, qgk, kvgk), writes=(cnk,))

    lo = min(b[0] for b in blocks)
    hi = max(b[0] + b[1] for b in blocks)
    linear(P, h, hk, KC, split_cols(w_dq, 0, 512, 256), blocks, epi, banks=(0, 1, 2, 3))
    normalize(qg)
    S.add("sp", (lambda e: e.dma_start(out=cq_out.rearrange("(c p) t -> p c t", p=128)[:, :, lo:hi], in_=cn[:, :, lo:hi])),
          reads=(cnk,), writes=("cq_out",), dma=True)
    linear(P, h, hk, KC, split_cols(w_dkv, 0, 576, 256), blocks, epi, banks=(0, 1, 2, 3))
    normalize(kvg)
    S.add("sp", (lambda e: e.dma_start(out=kv_lat[0:512, :].rearrange("(c p) t -> p c t", p=128), in_=cn[:, :, HALO:HALO + OWN])),
          reads=(cnk,), writes=("kv_out",), dma=True)
    if kv_ctx is not None:
        S.add("sp", (lambda e: e.dma_start(out=kv_ctx[0:512, :].rearrange("(c p) t -> p c t", p=128), in_=cn[:, :, WIN:WIN + CTX])),
              reads=(cnk,), writes=("kv_out",), dma=True)
    S.add("act", (lambda e: e.activation(out=krb[0:64, lo:hi], in_=kr32[0:64, lo:hi], func=AF.Identity)),
          reads=(kr32k,), writes=(krbk,))
    for bi, (o, n, v) in enumerate(blocks):
        b = P.psbank((4, 5))
        pk = ("ps", b)
        psap = P.ps[b][0:64, 0:n]
        S.add("pe", (lambda e, psap=psap, o=o, n=n: e.matmul(psap, lhsT=C["rot"], rhs=krb[0:64, o:o + n], start=True, stop=True)),
              reads=(krbk, "const"), writes=(pk,))
        av, ak = t1[bi % 2]
        bv, bk = t2[bi % 2]
        S.add("dve", (lambda e, av=av, o=o, n=n: e.tensor_tensor(out=av[0:64, 0:n], in0=kr32[0:64, o:o + n],
                                                               in1=cos[0:64, o:o + n], op=ALU.mult)),
              reads=(kr32k, cosk), writes=(ak,))
        S.add("dve", (lambda e, bv=bv, psap=psap, o=o, n=n: e.tensor_tensor(out=bv[0:64, 0:n], in0=psap,
                                                                          in1=sin[0:64, o:o + n], op=ALU.mult)),
              reads=(pk, sink), writes=(bk,))
        S.add("dve", (lambda e, av=av, bv=bv, o=o, n=n: e.tensor_tensor(out=kro[0:64, o:o + n], in0=av[0:64, 0:n],
                                                                      in1=bv[0:64, 0:n], op=ALU.add)),
              reads=(ak, bk), writes=(krok,))
    S.add("sp", (lambda e: e.dma_start(out=kv_lat[512:576, :], in_=kro[0:64, HALO:HALO + OWN])),
          reads=(krok,), writes=("kv_out",), dma=True)
    if kv_ctx is not None:
        S.add("sp", (lambda e: e.dma_start(out=kv_ctx[512:576, :], in_=kro[0:64, WIN:WIN + CTX])),
              reads=(krok,), writes=("kv_out",), dma=True)
    P.release(m0)
    P.barrier()


def attention(P, C, xd, od, kv_all, cq_in, w_uq, w_ukv, w_o, rope_cos_d, rope_sin_d, gmod, blocks):
    S = P.S
    scale = float((128 + 64) ** -0.5)
    m0 = P.mark()
    ckv, ckvk = P.alloc(BF16, 4, NKEYS)
    kr, krk = P.alloc(BF16, NKEYS)
    cq, cqk = P.alloc(BF16, 4, T)
    cos, cosk = P.alloc(F32, T)
    sin, sink = P.alloc(F32, T)
    load_small(P, cos[0:64, :], cosk, rope_cos_d)
    load_small(P, sin[0:64, :], sink, rope_sin_d)
    kvr = kv_all[0:512, :].rearrange("(c p) t -> p c t", p=128)
    for c in range(4):
        S.add("sp", (lambda e, c=c: e.dma_start(out=ckv[:, c, :], in_=kvr[:, c, :])), writes=(ckvk,), dma=True)
    S.add("sp", (lambda e: e.dma_start(out=kr[0:64, :], in_=kv_all[512:576, :])), writes=(krk,), dma=True)
    S.add("sp", (lambda e: e.dma_start(out=cq, in_=cq_in.rearrange("(c p) t -> p c t", p=128))), writes=(cqk,), dma=True)
    wq32 = [P.alloc(F32, 4, 192) for _ in range(2)]
    wqb = [P.alloc(BF16, 4, 192) for _ in range(2)]
    wk32 = [P.alloc(F32, 4, 256) for _ in range(2)]
    wkb = [P.alloc(BF16, 4, 256) for _ in range(2)]
    qn, qnk = P.alloc(BF16, T)
    qr32, qr32k = P.alloc(F32, T)
    qrb, qrbk = P.alloc(BF16, T)
    qro, qrok = P.alloc(BF16, T)
    kT, kTk = P.alloc(BF16, NKEYS)
    vh, vhk = P.alloc(BF16, NKC, 128)
    pT = [P.alloc(BF16, 2, 352) for _ in range(3)]
    t1 = [P.alloc(F32, 352) for _ in range(2)]
    t2 = [P.alloc(F32, 352) for _ in range(2)]
    rsum = [P.alloc(F32, 352) for _ in range(2)]
    sacc = [P.alloc(F32, 2, 352) for _ in range(2)]
    ones32, ones32k = P.alloc(F32, 128)
    S.add("pool", (lambda e: e.memset(ones32, 1.0)), writes=(ones32k,))
    obuf = [P.alloc(BF16, T) for _ in range(2)]
    wqr = w_uq.rearrange("(kc p) n -> p kc n", p=128)
    wkr = w_ukv.rearrange("(kc p) n -> p kc n", p=128)
    lo = min(b[0] for b in blocks)
    hi = max(b[0] + b[1] for b in blocks)
    GB = (0, 1, 2, 3)
    evac = [0]
    att_rot = [0]

    def loadw(hd):
        a, ak = wq32[hd % 2]
        ab, abk = wqb[hd % 2]
        b, bk = wk32[hd % 2]
        bb, bbk = wkb[hd % 2]
        S.add("sp", (lambda e: e.dma_start(out=a, in_=wqr[:, :, hd * 192:(hd + 1) * 192])), writes=(ak,), dma=True)
        S.add("sp", (lambda e: e.dma_start(out=b, in_=wkr[:, :, hd * 256:(hd + 1) * 256])), writes=(bk,), dma=True)
        S.add("pool", (lambda e: e.tensor_copy(out=ab, in_=a)), reads=(ak,), writes=(abk,))
        S.add("pool", (lambda e: e.tensor_copy(out=bb, in_=b)), reads=(bk,), writes=(bbk,))

    def evacuate(out_ap, ps_ap, pk, wkey):
        evac[0] += 1
        if evac[0] % 2:
            S.add("act", (lambda e: e.activation(out=out_ap, in_=ps_ap, func=AF.Identity)), reads=(pk,), writes=(wkey,))
        else:
            S.add("dve", (lambda e: e.tensor_copy(out=out_ap, in_=ps_ap)), reads=(pk,), writes=(wkey,))

    def group4(psap, lhs_fn, rhs_fn, reads, pk):
        def mm(e):
            ins = None
            for kc in range(4):
                ins = e.matmul(psap, lhsT=lhs_fn(kc), rhs=rhs_fn(kc), start=(kc == 0), stop=(kc == 3))
            return ins
        S.add("pe", mm, reads=reads, writes=(pk,))

    def q_block(hd, bi, o, n, v, wq, wqk_):
        b = P.psbank(GB)
        pk = ("ps", b)
        psap = P.ps[b][:, 0:n]
        group4(psap, (lambda kc: wq[:, kc, 0:128]), (lambda kc: cq[:, kc, o:o + n]), (wqk_, cqk), pk)
        evacuate(qn[:, o:o + n], psap, pk, qnk)
        qs = DEBUG.get("q_steps", 9)
        if qs < 2:
            return
        b = P.psbank(GB)
        pk2 = ("ps", b)
        psr = P.ps[b][0:64, 0:n]
        group4(psr, (lambda kc: wq[:, kc, 128:192]), (lambda kc: cq[:, kc, o:o + n]), (wqk_, cqk), pk2)
        S.add("act", (lambda e: e.activation(out=qr32[0:64, o:o + n], in_=psr, func=AF.Identity)), reads=(pk2,), writes=(qr32k,))
        S.add("dve", (lambda e: e.tensor_copy(out=qrb[0:64, o:o + n], in_=qr32[0:64, o:o + n])), reads=(qr32k,), writes=(qrbk,))
        if qs < 3:
            return
        b = P.psbank(GB)
        pk3 = ("ps", b)
        psq = P.ps[b][0:64, 0:n]
        S.add("pe", (lambda e: e.matmul(psq, lhsT=C["rot"], rhs=qrb[0:64, o:o + n], start=True, stop=True)),
              reads=(qrbk, "const"), writes=(pk3,))
        if qs < 4:
            return
        av, ak = t1[bi % 2]
        bv, bk = t2[bi % 2]
        S.add("dve", (lambda e: e.tensor_tensor(out=av[0:64, 0:n], in0=qr32[0:64, o:o + n], in1=cos[0:64, o:o + n], op=ALU.mult)),
              reads=(qr32k, cosk), writes=(ak,))
        S.add("dve", (lambda e: e.tensor_tensor(out=bv[0:64, 0:n], in0=psq, in1=sin[0:64, o:o + n], op=ALU.mult)),
              reads=(pk3, sink), writes=(bk,))
        S.add("dve", (lambda e: e.tensor_tensor(out=qro[0:64, o:o + n], in0=av[0:64, 0:n], in1=bv[0:64, 0:n], op=ALU.add)),
              reads=(ak, bk), writes=(qrok,))

    def k_block(kb, n, wk, wkk_):
        b = P.psbank(GB)
        pk = ("ps", b)
        psap = P.ps[b][:, 0:n]
        group4(psap, (lambda kc: wk[:, kc, 0:128]), (lambda kc: ckv[:, kc, kb:kb + n]), (wkk_, ckvk), pk)
        evacuate(kT[:, kb:kb + n], psap, pk, kTk)

    def v_block(kc0, nk, wk, wkk_):
        b = P.psbank(GB)
        pk = ("ps", b)

        def mmv(e):
            ins = None
            for j in range(nk):
                for kc in range(4):
                    ins = e.matmul(P.ps[b][:, j * 128:(j + 1) * 128], lhsT=ckv[:, kc, (kc0 + j) * 128:(kc0 + j + 1) * 128],
                                   rhs=wk[:, kc, 128:256], start=(kc == 0), stop=(kc == 3))
            return ins
        S.add("pe", mmv, reads=(wkk_, ckvk), writes=(pk,))
        evacuate(vh[:, kc0:kc0 + nk, :], P.ps[b][:, 0:nk * 128].rearrange("p (a b) -> p a b", a=nk), pk, vhk)

    def att_block(hd, bi, o, n, v, ov, ovk):
        kcs = list(range(NKC)) if v == 0 else list(range(SEQ // 128, NKC))
        pairs = [(kcs[i], kcs[i + 1]) for i in range(0, len(kcs), 2)]
        ob, sb = 6, 7
        opk, spk = ("ps", ob), ("ps", sb)
        ops_ap, sps_ap = P.ps[ob][:, 0:n], P.ps[sb][:, 0:n]
        av, avk = sacc[bi % 2]
        cnt = [0]

        def qk(pi, pr):
            b = 2 * (att_rot[0] % 3)
            att_rot[0] += 1
            pk0, pk1 = ("ps", b), ("ps", b + 1)

            def mm(e):
                ins = None
                for j, kc in enumerate(pr):
                    psap = P.ps[b + j][:, 0:n]
                    e.matmul(psap, lhsT=kT[:, kc * 128:(kc + 1) * 128], rhs=qn[:, o:o + n], start=True, stop=False)
                    ins = e.matmul(psap, lhsT=kr[0:64, kc * 128:(kc + 1) * 128], rhs=qro[0:64, o:o + n], start=False, stop=True)
                return ins
            S.add("pe", mm, reads=(kTk, krk, qnk, qrok), writes=(pk0, pk1))
            pv, pvk = pT[pi % 3]
            S.add("act", (lambda e: e.activation(out=pv[:, :, 0:n], in_=P.psall[:, b:b + 2, 0:n], func=AF.Exp, scale=scale)),
                  reads=(pk0, pk1), writes=(pvk,))
            return (pr, pv, pvk)

        def pvs(item, first, last):
            pr, pv, pvk = item

            def mm(e):
                e.matmul(ops_ap, lhsT=vh[:, pr[0], :], rhs=pv[:, 0, 0:n], start=first, stop=False)
                return e.matmul(ops_ap, lhsT=vh[:, pr[1], :], rhs=pv[:, 1, 0:n], start=False, stop=last)
            S.add("pe", mm, reads=(pvk, vhk), writes=(opk,))
            if first:
                S.add("dve", (lambda e: e.tensor_copy(out=av[:, :, 0:n], in_=pv[:, :, 0:n])), reads=(pvk,), writes=(avk,))
            else:
                S.add("dve", (lambda e: e.tensor_tensor(out=av[:, :, 0:n], in0=av[:, :, 0:n], in1=pv[:, :, 0:n], op=ALU.add)),
                      reads=(pvk, avk), writes=(avk,))
            if last:
                def mms(e):
                    e.matmul(sps_ap, lhsT=ones32, rhs=av[:, 0, 0:n], start=True, stop=False)
                    return e.matmul(sps_ap, lhsT=ones32, rhs=av[:, 1, 0:n], start=False, stop=True)
                S.add("pe", mms, reads=(avk, ones32k), writes=(spk,))

        pend = []
        done = 0
        for pi, pr in enumerate(pairs):
            pend.append(qk(pi, pr))
            if len(pend) > 1:
                pvs(pend.pop(0), done == 0, False)
                done += 1
        while pend:
            it = pend.pop(0)
            pvs(it, done == 0, len(pend) == 0)
            done += 1
        rv, rk = rsum[bi % 2]
        S.add("dve", (lambda e: e.reciprocal(out=rv[:, 0:n], in_=sps_ap)), reads=(spk,), writes=(rk,))
        S.add("dve", (lambda e: e.tensor_tensor(out=ov[:, o:o + n], in0=ops_ap, in1=rv[:, 0:n], op=ALU.mult)),
              reads=(opk, rk), writes=(ovk,))

    parts = DEBUG.get("att_parts", "qkva")

    def do_head(hd):
        if hd + 1 < NH:
            loadw(hd + 1)
        wq, wqk_ = wqb[hd % 2]
        wk, wkk_ = wkb[hd % 2]
        if "q" in parts:
            for bi, (o, n, v) in enumerate(blocks):
                q_block(hd, bi, o, n, v, wq, wqk_)
        if "k" in parts:
            for kb in range(0, NKEYS, 512):
                k_block(kb, min(512, NKEYS - kb), wk, wkk_)
        if "v" in parts:
            for kc0 in range(0, NKC, 4):
                v_block(kc0, min(4, NKC - kc0), wk, wkk_)
        ov, ovk = obuf[hd % 2]
        if "a" in parts:
            for bi, (o, n, v) in enumerate(blocks):
                att_block(hd, bi, o, n, v, ov, ovk)
            S.add("sp", (lambda e: e.dma_start(out=od[hd * 128:(hd + 1) * 128, lo:hi], in_=ov[:, lo:hi])),
                  reads=(ovk,), writes=("od",), dma=True)

    loadw(0)
    for hd in range(DEBUG.get("att_heads", NH)):
        do_head(hd)
    if "att_dump" in DEBUG["dump"]:
        P.dbg("dbg_qn", qn, qnk, [T], BF16)
        P.dbg("dbg_qro", qro, qrok, [T], BF16)
        P.dbg("dbg_kT", kT, kTk, [NKEYS], BF16)
        P.dbg("dbg_vh", vh, vhk, [NKC, 128], BF16)
        P.dbg("dbg_ov", obuf[0][0], obuf[0][1], [T], BF16)
    if DEBUG["stop"] == "attcore":
        raise StopBuild(P)
    P.barrier()
    P.release(m0)
    m0 = P.mark()
    oT, oTk = P.alloc(BF16, KC, T)
    odr = od.rearrange("(c p) t -> p c t", p=128)
    for q in range(4):
        S.add("sp", (lambda e, q=q: e.dma_start(out=oT[:, q * 4:(q + 1) * 4, lo:hi], in_=odr[:, q * 4:(q + 1) * 4, lo:hi])),
              reads=("od",), writes=(oTk,), dma=True)
    R = Resid(P, xd, gmod, blocks)
    linear(P, oT, oTk, KC, split_cols(w_o, 0, D, 256), blocks, R.epi, pre_chunk=R.pre, post_chunk=R.post,
           banks=(0, 1, 2, 3, 4, 5))
    P.release(m0)
    P.barrier()


def load_consts(P):
    S = P.S
    C = {}
    ones, _ = P.alloc(BF16, 128)
    rot, _ = P.alloc(BF16, 64)
    o_d = P.din("c_ones", [128, 128], BF16)
    r_d = P.din("c_rot", [64, 64], BF16)
    S.add("sp", (lambda e: e.dma_start(out=ones, in_=o_d)), writes=("const",), dma=True)
    S.add("sp", (lambda e: e.dma_start(out=rot[0:64, :], in_=r_d)), writes=("const",), dma=True)
    C["ones"] = ones
    C["rot"] = rot[0:64, :]
    return C


def load_mod(P, C, layers, mod_ap=None):
    S = P.S
    nl = len(layers)
    mod_d = mod_ap if mod_ap is not None else P.din("mod", [128, nl, 6, KC, 2])
    ng_d = P.din("ng", [128, nl, 2, KC])
    mod, _ = P.alloc(F32, nl * 6, KC, 2)
    ng, _ = P.alloc(F32, nl * 2, KC)
    S.add("sp", (lambda e: e.dma_start(out=mod, in_=mod_d.rearrange("p l s c v -> p (l s) c v"))), writes=("mod",), dma=True)
    S.add("sp", (lambda e: e.dma_start(out=ng, in_=ng_d.rearrange("p l s c -> p (l s) c"))), writes=("mod",), dma=True)
    out = {}
    for li, L in enumerate(layers):
        d = {}
        for w, (si, ni) in enumerate(((1, 0), (4, 1))):
            a, _ = P.alloc(F32, KC, 2)
            for v in range(2):
                S.add("dve", (lambda e, a=a, v=v, li=li, si=si, ni=ni: e.scalar_tensor_tensor(
                    out=a[:, :, v], in0=mod[:, li * 6 + si, :, v], scalar=1.0, in1=ng[:, li * 2 + ni, :],
                    op0=ALU.add, op1=ALU.mult)), reads=("mod",), writes=("mod",))
            d["A%d" % (w + 1)] = a
        d["B1"] = mod[:, li * 6 + 0]
        d["G1"] = mod[:, li * 6 + 2]
        d["B2"] = mod[:, li * 6 + 3]
        d["G2"] = mod[:, li * 6 + 5]
        out[L] = d
    return out


class StopBuild(Exception):
    pass


DEBUG = {"stop": None, "dump": ()}


def build_stage(stage):
    try:
        return _build_stage(stage)
    except StopBuild as e:
        P = e.args[0]
        P.barrier()
        return P


def _build_stage(stage):
    P = Prog()
    S = P.S
    C = load_consts(P)
    layers = {"B": [0, 1], "C": [1, 2, 3], "D": [3]}[stage]
    M = load_mod(P, C, layers)
    x_in = P.din("x_in", [D, T])
    if stage == "D":
        xd = P.dint("x_state", [D, T])
    else:
        xd = P.dout("x_out", [D, T])
    ud = P.dint("u_spill", [DFF, T], BF16)
    od = P.dint("o_spill", [D, T], BF16)
    S.add("sp", (lambda e: e.dma_start(out=xd, in_=x_in)), writes=("xd",), dma=True)
    rope_cos = P.din("rope_cos", [64, T])
    rope_sin = P.din("rope_sin", [64, T])
    W = {}

    def win(name, shape):
        W[name] = P.din(name, shape)
        return W[name]

    def ffn_w(L):
        return (win("ffn_w_gate_%d" % L, [D, DFF]), win("ffn_w_up_%d" % L, [D, DFF]), win("ffn_w_down_%d" % L, [DFF, D]))

    def even_layer(L):
        i = L // 2
        mask_d = P.din("mask3", [128, KC, 2 * HALO])
        invc_d = P.din("invc", [128, 4, T])
        w_in = win("even_w_in_%d" % i, [D, 4096])
        pool_w = win("pool_w_%d" % i, [4, 256, 256])
        pool_scale = win("pool_scale_%d" % i, [128, 8])
        conv_w = win("conv_w_%d" % i, [128, 3, 8])
        w_out = win("even_w_out_%d" % i, [D, D])
        m0 = P.mark()
        mask3, _ = P.alloc(F32, KC, 2 * HALO)
        S.add("sp", (lambda e: e.dma_start(out=mask3, in_=mask_d)), writes=("const",), dma=True)
        h, hk = P.alloc(BF16, KC, T)
        norm_phase(P, C, xd, M[L]["A1"], M[L]["B1"], h, hk, BLOCKS, mask3=mask3)
        if "h" in DEBUG["dump"]:
            P.dbg("dbg_h", h, hk, [KC, T], BF16)
        if DEBUG["stop"] == "norm%d" % L:
            raise StopBuild(P)
        even_mixer(P, C, xd, h, hk, w_in, pool_w, pool_scale, conv_w, w_out, M[L]["G1"], BLOCKS, invc_d)
        P.release(m0)
        if DEBUG["stop"] == "mixer%d" % L:
            raise StopBuild(P)
        wg, wu, wd = ffn_w(L)
        ffn_phase(P, C, xd, wg, wu, wd, ud, M[L]["A2"], M[L]["B2"], M[L]["G2"], BLOCKS)
        if DEBUG["stop"] == "ffn%d" % L:
            raise StopBuild(P)

    def odd_layer_pre(L):
        i = L // 2
        w_dq = win("mla_w_dq_%d" % i, [D, 512])
        qg = win("mla_q_norm_g_%d" % i, [128, 4])
        w_dkv = win("mla_w_dkv_%d" % i, [D, 576])
        kvg = win("mla_kv_norm_g_%d" % i, [128, 4])
        kv_out = P.dout("kv_out", [576, OWN + CTX], BF16)
        cq_out = P.dout("cq_out", [512, T], BF16)
        m0 = P.mark()
        h, hk = P.alloc(BF16, KC, T)
        norm_phase(P, C, xd, M[L]["A1"], M[L]["B1"], h, hk, BLOCKS)
        odd_pre(P, C, h, hk, w_dq, qg, w_dkv, kvg, rope_cos, rope_sin, kv_out[:, 0:OWN], kv_out[:, OWN:OWN + CTX], cq_out, BLOCKS)
        P.release(m0)

    def odd_layer_post(L, blocks):
        i = L // 2
        kv_all = P.din("kv_all", [576, NKEYS], BF16)
        cq_in = P.din("cq_in", [512, T], BF16)
        w_uq = win("mla_w_uq_%d" % i, [512, 3072])
        w_ukv = win("mla_w_ukv_%d" % i, [512, 4096])
        w_o = win("mla_w_o_%d" % i, [D, D])
        attention(P, C, xd, od, kv_all, cq_in, w_uq, w_ukv, w_o, rope_cos, rope_sin, M[L]["G1"], blocks)
        if DEBUG["stop"] == "att%d" % L:
            raise StopBuild(P)
        wg, wu, wd = ffn_w(L)
        ffn_phase(P, C, xd, wg, wu, wd, ud, M[L]["A2"], M[L]["B2"], M[L]["G2"], blocks)
        if DEBUG["stop"] == "ffn%d" % L:
            raise StopBuild(P)

    P.barrier()
    if stage == "B":
        even_layer(0)
        odd_layer_pre(1)
    elif stage == "C":
        odd_layer_post(1, BLOCKS)
        even_layer(2)
        odd_layer_pre(3)
    else:
        odd_layer_post(3, BLOCKS[:3])
        fg_d = P.din("fgain", [128, KC, 2])
        y = P.dout("y", [D, OWN])
        m0 = P.mark()
        fg, _ = P.alloc(F32, KC, 2)
        S.add("sp", (lambda e: e.dma_start(out=fg, in_=fg_d)), writes=("mod",), dma=True)
        hf, hfk = P.alloc(F32, KC, T)
        norm_phase(P, C, xd, fg, None, hf, hfk, BLOCKS[:3])
        yr = y.rearrange("(c p) t -> p c t", p=128)
        for q in range(4):
            S.add("sp", (lambda e, q=q: e.dma_start(out=yr[:, q * 4:(q + 1) * 4, :], in_=hf[:, q * 4:(q + 1) * 4, HALO:HALO + OWN])),
                  reads=(hfk,), writes=("y",), dma=True)
        P.release(m0)
    P.barrier()
    return P


def build_fused():
    P = Prog()
    S = P.S
    C = load_consts(P)
    NS = NCORES
    NJ = 96
    ada_w = P.din("ada_w", [4 * D, 12288])
    ada_b = P.din("ada_b", [128, 4 * NJ])
    cvec = P.din("cvec", [128, KC, 2])
    mod_d = P.dint("mod_d", [128, 4 * NJ, 2])
    m0 = P.mark()
    cs, csk = P.alloc(F32, KC, 2)
    ss, ssk = P.alloc(F32, KC, 2)
    bb, bbk = P.alloc(F32, 4 * NJ)
    res, resk = P.alloc(F32, 4 * NJ, 2)
    load_small(P, cs, csk, cvec)
    load_small(P, bb, bbk, ada_b)
    S.add("act", (lambda e: e.activation(out=ss, in_=cs, func=AF.Silu)), reads=(csk,), writes=(ssk,))
    st = [P.alloc(F32, KC, 128) for _ in range(3)]
    items = [(l, j) for l in range(4) for j in range(NJ)]

    def aload(idx):
        l, j = items[idx]
        sv, sk = st[idx % 3]
        S.add("sp", (lambda e: e.dma_start(out=sv, in_=ada_w[l * D:(l + 1) * D, j * 128:(j + 1) * 128].rearrange("(kc p) n -> p kc n", p=128))),
              writes=(sk,), dma=True)
    aload(0)
    aload(1)
    for idx in range(len(items)):
        if idx + 2 < len(items):
            aload(idx + 2)
        sv, sk = st[idx % 3]
        b = idx % 4
        pk = ("ps", b)
        psap = P.ps[b][:, 0:2]

        def mm(e, sv=sv, psap=psap):
            ins = None
            for kc in range(KC):
                ins = e.matmul(psap, lhsT=sv[:, kc, :], rhs=ss[:, kc, :], start=(kc == 0), stop=(kc == KC - 1))
            return ins
        S.add("pe", mm, reads=(sk, ssk), writes=(pk,))
        S.add("act", (lambda e, psap=psap, idx=idx: e.activation(out=res[:, idx, :], in_=psap, func=AF.Identity,
                                                                 bias=bb[:, idx:idx + 1], scale=1.0)),
              reads=(pk, bbk), writes=(resk,))
    S.add("sp", (lambda e: e.dma_start(out=mod_d, in_=res)), reads=(resk,), writes=("mod_d",), dma=True)
    P.release(m0)
    M = load_mod(P, C, [0, 1, 2, 3], mod_ap=mod_d.rearrange("p (l s c) v -> p l s c v", l=4, s=6))
    x_in = P.din("x_in", [NS, D, T])
    xds = [P.dint("x_state%d" % sl, [D, T]) for sl in range(NS)]
    for sl in range(NS):
        S.add("sp", (lambda e, sl=sl: e.dma_start(out=xds[sl], in_=x_in[sl])), writes=("xd",), dma=True)
    ud = P.dint("u_spill", [DFF, T], BF16)
    od = P.dint("o_spill", [D, T], BF16)
    kvs = {1: P.dint("kv_all_l1", [576, NKEYS], BF16), 3: P.dint("kv_all_l3", [576, NKEYS], BF16)}
    cqd = [P.dint("cq_%d" % sl, [512, T], BF16) for sl in range(NS)]
    rope_cos = P.din("rope_cos", [NS, 64, T])
    rope_sin = P.din("rope_sin", [NS, 64, T])
    mask_d = P.din("mask3", [NS, 128, KC, 2 * HALO])
    invc_d = P.din("invc", [NS, 128, 4, T])
    W = {}

    def win(name, shape):
        if name not in W:
            W[name] = P.din(name, shape)
        return W[name]

    def ffn_w(L):
        return (win("ffn_w_gate_%d" % L, [D, DFF]), win("ffn_w_up_%d" % L, [D, DFF]), win("ffn_w_down_%d" % L, [DFF, D]))

    def blocks_of(sl):
        return BLOCKS if sl == 0 else BLOCKS[:3]

    def even_layer(L, sl):
        i = L // 2
        blocks = blocks_of(sl)
        w_in = win("even_w_in_%d" % i, [D, 4096])
        pool_w = win("pool_w_%d" % i, [4, 256, 256])
        pool_scale = win("pool_scale_%d" % i, [128, 8])
        conv_w = win("conv_w_%d" % i, [128, 3, 8])
        w_out = win("even_w_out_%d" % i, [D, D])
        m0 = P.mark()
        mask3, _ = P.alloc(F32, KC, 2 * HALO)
        S.add("sp", (lambda e: e.dma_start(out=mask3, in_=mask_d[sl])), writes=("const",), dma=True)
        h, hk = P.alloc(BF16, KC, T)
        norm_phase(P, C, xds[sl], M[L]["A1"], M[L]["B1"], h, hk, blocks, mask3=mask3)
        even_mixer(P, C, xds[sl], h, hk, w_in, pool_w, pool_scale, conv_w, w_out, M[L]["G1"], blocks, invc_d[sl])
        P.release(m0)
        wg, wu, wd = ffn_w(L)
        ffn_phase(P, C, xds[sl], wg, wu, wd, ud, M[L]["A2"], M[L]["B2"], M[L]["G2"], blocks)

    def odd_layer_pre(L, sl):
        i = L // 2
        blocks = blocks_of(sl)
        w_dq = win("mla_w_dq_%d" % i, [D, 512])
        qg = win("mla_q_norm_g_%d" % i, [128, 4])
        w_dkv = win("mla_w_dkv_%d" % i, [D, 576])
        kvg = win("mla_kv_norm_g_%d" % i, [128, 4])
        m0 = P.mark()
        h, hk = P.alloc(BF16, KC, T)
        norm_phase(P, C, xds[sl], M[L]["A1"], M[L]["B1"], h, hk, blocks)
        kv_all = kvs[L]
        odd_pre(P, C, h, hk, w_dq, qg, w_dkv, kvg, rope_cos[sl], rope_sin[sl], kv_all[:, sl * OWN:(sl + 1) * OWN],
                kv_all[:, SEQ:SEQ + CTX] if sl == 0 else None, cqd[sl], blocks)
        P.release(m0)

    def odd_layer_post(L, sl, blocks):
        i = L // 2
        w_uq = win("mla_w_uq_%d" % i, [512, 3072])
        w_ukv = win("mla_w_ukv_%d" % i, [512, 4096])
        w_o = win("mla_w_o_%d" % i, [D, D])
        attention(P, C, xds[sl], od, kvs[L], cqd[sl], w_uq, w_ukv, w_o, rope_cos[sl], rope_sin[sl], M[L]["G1"], blocks)
        wg, wu, wd = ffn_w(L)
        ffn_phase(P, C, xds[sl], wg, wu, wd, ud, M[L]["A2"], M[L]["B2"], M[L]["G2"], blocks)

    P.barrier()
    nsl = DEBUG.get("fused_slots", NS)
    for sl in range(nsl):
        even_layer(0, sl)
        odd_layer_pre(1, sl)
    for sl in range(nsl):
        odd_layer_post(1, sl, blocks_of(sl))
        even_layer(2, sl)
        odd_layer_pre(3, sl)
    odd_layer_post(3, 0, BLOCKS[:3])
    fg_d = P.din("fgain", [128, KC, 2])
    y = P.dout("y", [D, OWN])
    m0 = P.mark()
    fg, _ = P.alloc(F32, KC, 2)
    S.add("sp", (lambda e: e.dma_start(out=fg, in_=fg_d)), writes=("mod",), dma=True)
    hf, hfk = P.alloc(F32, KC, T)
    norm_phase(P, C, xds[0], fg, None, hf, hfk, BLOCKS[:3])
    yr = y.rearrange("(c p) t -> p c t", p=128)
    for q in range(4):
        S.add("sp", (lambda e, q=q: e.dma_start(out=yr[:, q * 4:(q + 1) * 4, :], in_=hf[:, q * 4:(q + 1) * 4, HALO:HALO + OWN])),
              reads=(hfk,), writes=("y",), dma=True)
    P.release(m0)
    P.barrier()
    return P


def finish(P):
    nc, S = P.nc, P.S
    with ExitStack() as st:
        S.finalize(nc, st)
        with nc.Block() as block:
            @block.sync
            def _(e):
                S.emit("sp", e)

            @block.scalar
            def _(e):
                S.emit("act", e)

            @block.vector
            def _(e):
                S.emit("dve", e)

            @block.gpsimd
            def _(e):
                S.emit("pool", e)

            @block.tensor
            def _(e):
                S.emit("pe", e)
    P.stack.close()
    return nc


def build_ada():
    P = Prog()
    S = P.S
    NJ = 12288 // NCORES // 128
    w = P.din("ada_w", [4 * D, NJ * 128])
    b_d = P.din("ada_b", [128, 4 * NJ])
    cv = P.din("cvec", [128, KC, 2])
    out_d = P.dout("mod_out", [128, 4 * NJ, 2])
    cs, csk = P.alloc(F32, KC, 2)
    ss, ssk = P.alloc(F32, KC, 2)
    bb, bbk = P.alloc(F32, 4 * NJ)
    res, resk = P.alloc(F32, 4 * NJ, 2)
    load_small(P, cs, csk, cv)
    load_small(P, bb, bbk, b_d)
    S.add("act", (lambda e: e.activation(out=ss, in_=cs, func=AF.Silu)), reads=(csk,), writes=(ssk,))
    st = [P.alloc(F32, KC, 128) for _ in range(3)]
    n = 0
    pend = []

    def load(l, j):
        sv, sk = st[(l * NJ + j) % 3]
        S.add("sp", (lambda e: e.dma_start(out=sv, in_=w[l * D:(l + 1) * D, j * 128:(j + 1) * 128].rearrange("(kc p) n -> p kc n", p=128))),
              writes=(sk,), dma=True)
    items = [(l, j) for l in range(4) for j in range(NJ)]
    load(*items[0])
    load(*items[1])
    for idx, (l, j) in enumerate(items):
        if idx + 2 < len(items):
            load(*items[idx + 2])
        sv, sk = st[idx % 3]
        b = idx % 4
        pk = ("ps", b)
        psap = P.ps[b][:, 0:2]

        def mm(e, sv=sv, psap=psap):
            ins = None
            for kc in range(KC):
                ins = e.matmul(psap, lhsT=sv[:, kc, :], rhs=ss[:, kc, :], start=(kc == 0), stop=(kc == KC - 1))
            return ins
        S.add("pe", mm, reads=(sk, ssk), writes=(pk,))
        S.add("act", (lambda e, psap=psap, idx=idx: e.activation(out=res[:, idx, :], in_=psap, func=AF.Identity,
                                                                 bias=bb[:, idx:idx + 1], scale=1.0)),
              reads=(pk, bbk), writes=(resk,))
    S.add("sp", (lambda e: e.dma_start(out=out_d, in_=res)), reads=(resk,), writes=("out",), dma=True)
    P.barrier()
    return P


_CACHE = {}


def _prog(name):
    if name not in _CACHE:
        P = build_ada() if name == "A" else (build_fused() if name == "F" else build_stage(name))
        _CACHE[name] = finish(P)
    return _CACHE[name]


def _fm(v):
    return np.ascontiguousarray(np.asarray(v).reshape(-1, 128).T)


def _consts(core):
    s = core * OWN - HALO
    pos = np.arange(s, s + WIN)
    valid = (pos >= 0) & (pos < SEQ)
    mask = np.concatenate([valid[:HALO], valid[-HALO:]]).astype(np.float32)
    mask3 = np.ascontiguousarray(np.broadcast_to(mask[None, None, :], (128, KC, 2 * HALO))).astype(np.float32)
    invc = np.ones((4, T), np.float32)
    for g, w in enumerate((2, 4, 8, 16)):
        lo = np.clip(pos - w // 2, 0, SEQ)
        hi = np.clip(pos + (w - w // 2), 0, SEQ)
        cnt = np.maximum(hi - lo, 1).astype(np.float32)
        invc[g, :WIN] = np.float32(1.0) / cnt
        t = np.arange(CTX)
        lo = np.clip(t - w // 2, 0, CTX)
        hi = np.clip(t + (w - w // 2), 0, CTX)
        invc[g, WIN:] = np.float32(1.0) / (hi - lo).astype(np.float32)
    invc = np.ascontiguousarray(np.broadcast_to(invc[None], (128, 4, T))).astype(np.float32)
    p = np.clip(pos, 0, SEQ - 1)
    row = (p // GRID_W).astype(np.float32)
    col = (p % GRID_W).astype(np.float32)
    nf = 16
    inv = np.power(np.float32(10000.0), -np.arange(nf, dtype=np.float32) / np.float32(nf)).astype(np.float32)
    ar = (row[:, None] * inv).astype(np.float32)
    ac = (col[:, None] * inv).astype(np.float32)
    cos = np.ones((64, T), np.float32)
    sin = np.zeros((64, T), np.float32)
    cos[0:16, :WIN] = np.cos(ar).T
    cos[16:32, :WIN] = np.cos(ar).T
    cos[32:48, :WIN] = np.cos(ac).T
    cos[48:64, :WIN] = np.cos(ac).T
    sin[0:16, :WIN] = np.sin(ar).T
    sin[16:32, :WIN] = np.sin(ar).T
    sin[32:48, :WIN] = np.sin(ac).T
    sin[48:64, :WIN] = np.sin(ac).T
    return mask3, invc, cos, sin


def _rot_lhsT():
    Pm = np.zeros((64, 64), np.float32)
    for base in (0, 32):
        for i in range(16):
            Pm[base + i, base + 16 + i] = -1.0
            Pm[base + 16 + i, base + i] = 1.0
    return np.ascontiguousarray(Pm.T).astype(NPBF16)


FUSED = False


def _kernel_fused(x, c, ctx, c_ctx, ada_w, ada_b, norm1_g, norm2_g, even_w_in, pool_w, pool_scale, conv_w, even_w_out,
                  mla_w_dq, mla_q_norm_g, mla_w_uq, mla_w_dkv, mla_kv_norm_g, mla_w_ukv, mla_w_o,
                  ffn_w_gate, ffn_w_up, ffn_w_down, final_norm_g):
    f32 = lambda a: np.ascontiguousarray(np.asarray(a, dtype=np.float32))
    cores = list(range(NCORES))
    NJ = 96
    cvec = np.stack([_fm(c[0]), _fm(c_ctx)], axis=-1).astype(np.float32)
    ada_w2 = f32(np.asarray(ada_w)).reshape(4 * D, 12288)
    ada_b2 = np.ascontiguousarray(np.asarray(ada_b, dtype=np.float32).reshape(4, NJ, 128).transpose(2, 0, 1).reshape(128, 4 * NJ))
    ngt = np.stack([np.stack([_fm(norm1_g[l]), _fm(norm2_g[l])], 0) for l in range(4)], 0)
    ngt = np.ascontiguousarray(ngt.transpose(2, 0, 1, 3)).astype(np.float32)
    ones = np.ones((128, 128), NPBF16)
    rot = _rot_lhsT()
    consts = [_consts(k) for k in cores]
    xp = np.zeros((SEQ + 2 * HALO, D), np.float32)
    xp[HALO:HALO + SEQ] = x[0]
    wins = [np.ascontiguousarray(np.concatenate([xp[k * OWN:k * OWN + WIN], ctx[0]], axis=0).T) for k in cores]
    Wd = {}
    for nm, arr in (("even_w_in", even_w_in), ("pool_w", pool_w), ("pool_scale", pool_scale), ("conv_w", conv_w),
                    ("even_w_out", even_w_out), ("mla_w_dq", mla_w_dq), ("mla_q_norm_g", mla_q_norm_g), ("mla_w_uq", mla_w_uq),
                    ("mla_w_dkv", mla_w_dkv), ("mla_kv_norm_g", mla_kv_norm_g), ("mla_w_ukv", mla_w_ukv), ("mla_w_o", mla_w_o),
                    ("ffn_w_gate", ffn_w_gate), ("ffn_w_up", ffn_w_up), ("ffn_w_down", ffn_w_down)):
        arr = np.asarray(arr)
        for i in range(arr.shape[0]):
            a = f32(arr[i])
            if nm in ("pool_scale", "mla_q_norm_g", "mla_kv_norm_g"):
                a = _fm(a)
            elif nm == "conv_w":
                a = np.ascontiguousarray(a.reshape(3, 8, 128).transpose(2, 0, 1))
            Wd["%s_%d" % (nm, i)] = a
    fgain = np.ascontiguousarray(np.broadcast_to(_fm(final_norm_g)[:, :, None], (128, KC, 2))).astype(np.float32)
    pf = _prog("F")
    ins = []
    for k in cores:
        order = [(k + sl) % NCORES for sl in range(NCORES)]
        d = {"c_ones": ones, "c_rot": rot, "ada_w": ada_w2, "ada_b": ada_b2, "cvec": cvec, "ng": ngt,
             "x_in": np.stack([wins[w] for w in order], 0),
             "rope_cos": np.stack([consts[w][2] for w in order], 0), "rope_sin": np.stack([consts[w][3] for w in order], 0),
             "mask3": np.stack([consts[w][0] for w in order], 0), "invc": np.stack([consts[w][1] for w in order], 0),
             "fgain": fgain}
        d.update(Wd)
        ins.append(d)
    rf = run_bass_kernel_spmd(pf, ins, core_ids=cores).results
    out = np.concatenate([np.asarray(rf[k]["y"]).T for k in cores], axis=0)
    return np.ascontiguousarray(out[None].astype(np.float32))


def kernel(x, c, ctx, c_ctx, ada_w, ada_b, norm1_g, norm2_g, even_w_in, pool_w, pool_scale, conv_w, even_w_out,
           mla_w_dq, mla_q_norm_g, mla_w_uq, mla_w_dkv, mla_kv_norm_g, mla_w_ukv, mla_w_o,
           ffn_w_gate, ffn_w_up, ffn_w_down, final_norm_g):
    f32 = lambda a: np.ascontiguousarray(np.asarray(a, dtype=np.float32))
    x, c, ctx, c_ctx = f32(x), f32(c), f32(ctx), f32(c_ctx)
    cores = list(range(NCORES))
    if FUSED:
        return _kernel_fused(x, c, ctx, c_ctx, ada_w, ada_b, norm1_g, norm2_g, even_w_in, pool_w, pool_scale, conv_w,
                             even_w_out, mla_w_dq, mla_q_norm_g, mla_w_uq, mla_w_dkv, mla_kv_norm_g, mla_w_ukv, mla_w_o,
                             ffn_w_gate, ffn_w_up, ffn_w_down, final_norm_g)
    NJ = 12
    cvec = np.stack([_fm(c[0]), _fm(c_ctx)], axis=-1).astype(np.float32)
    ada_w = np.asarray(ada_w)
    ada_b = np.asarray(ada_b)
    inA = []
    for k in cores:
        wk = np.ascontiguousarray(ada_w[:, :, k * 1536:(k + 1) * 1536]).reshape(4 * D, 1536)
        bk = np.ascontiguousarray(ada_b[:, k * 1536:(k + 1) * 1536].reshape(4, NJ, 128).transpose(2, 0, 1).reshape(128, 4 * NJ))
        inA.append({"ada_w": wk, "ada_b": bk, "cvec": cvec, "c_ones": np.ones((128, 128), NPBF16), "c_rot": _rot_lhsT()})
    pa = _prog("A")
    ra = run_bass_kernel_spmd(pa, [{k: v for k, v in m.items() if k in ("ada_w", "ada_b", "cvec")} for m in inA], core_ids=cores)
    modf = np.zeros((4, 12288, 2), np.float32)
    for k in cores:
        r = np.asarray(ra.results[k]["mod_out"]).reshape(128, 4, NJ, 2)
        modf[:, k * 1536:(k + 1) * 1536, :] = r.transpose(1, 2, 0, 3).reshape(4, 1536, 2)
    modt = np.ascontiguousarray(modf.reshape(4, 6, KC, 128, 2).transpose(3, 0, 1, 2, 4))
    ngt = np.stack([np.stack([_fm(norm1_g[l]), _fm(norm2_g[l])], 0) for l in range(4)], 0)
    ngt = np.ascontiguousarray(ngt.transpose(2, 0, 1, 3)).astype(np.float32)

    ones = np.ones((128, 128), NPBF16)
    rot = _rot_lhsT()
    consts = [_consts(k) for k in cores]
    xs = []
    xp = np.zeros((SEQ + 2 * HALO, D), np.float32)
    xp[HALO:HALO + SEQ] = x[0]
    for k in cores:
        st = np.concatenate([xp[k * OWN:k * OWN + WIN], ctx[0]], axis=0)
        xs.append(np.ascontiguousarray(st.T))

    def common(k, layers):
        mask3, invc, cos, sin = consts[k]
        return {"c_ones": ones, "c_rot": rot, "mod": np.ascontiguousarray(modt[:, layers]),
                "ng": np.ascontiguousarray(ngt[:, layers]), "x_in": xs[k], "rope_cos": cos, "rope_sin": sin,
                "mask3": mask3, "invc": invc}

    def lw(names_idx):
        d = {}
        for nm, arr, i in names_idx:
            a = f32(np.asarray(arr)[i])
            if nm in ("pool_scale", "mla_q_norm_g", "mla_kv_norm_g"):
                a = _fm(a)
            elif nm == "conv_w":
                a = np.ascontiguousarray(a.reshape(3, 8, 128).transpose(2, 0, 1))
            d["%s_%d" % (nm, i)] = a
        return d

    def gather_kv(res):
        lat = np.concatenate([np.asarray(res[k]["kv_out"])[:, :OWN] for k in cores], axis=1)
        return np.ascontiguousarray(np.concatenate([lat, np.asarray(res[0]["kv_out"])[:, OWN:]], axis=1))

    wB = lw([("even_w_in", even_w_in, 0), ("pool_w", pool_w, 0), ("pool_scale", pool_scale, 0), ("conv_w", conv_w, 0),
             ("even_w_out", even_w_out, 0), ("ffn_w_gate", ffn_w_gate, 0), ("ffn_w_up", ffn_w_up, 0),
             ("ffn_w_down", ffn_w_down, 0), ("mla_w_dq", mla_w_dq, 0), ("mla_q_norm_g", mla_q_norm_g, 0),
             ("mla_w_dkv", mla_w_dkv, 0), ("mla_kv_norm_g", mla_kv_norm_g, 0)])
    pb = _prog("B")
    rb = run_bass_kernel_spmd(pb, [dict(common(k, [0, 1]), **wB) for k in cores], core_ids=cores).results
    xs = [np.asarray(rb[k]["x_out"]) for k in cores]
    kv_all = gather_kv(rb)
    cqs = [np.asarray(rb[k]["cq_out"]) for k in cores]
    del wB
    wC = lw([("mla_w_uq", mla_w_uq, 0), ("mla_w_ukv", mla_w_ukv, 0), ("mla_w_o", mla_w_o, 0),
             ("ffn_w_gate", ffn_w_gate, 1), ("ffn_w_up", ffn_w_up, 1), ("ffn_w_down", ffn_w_down, 1),
             ("even_w_in", even_w_in, 1), ("pool_w", pool_w, 1), ("pool_scale", pool_scale, 1), ("conv_w", conv_w, 1),
             ("even_w_out", even_w_out, 1), ("ffn_w_gate", ffn_w_gate, 2), ("ffn_w_up", ffn_w_up, 2),
             ("ffn_w_down", ffn_w_down, 2), ("mla_w_dq", mla_w_dq, 1), ("mla_q_norm_g", mla_q_norm_g, 1),
             ("mla_w_dkv", mla_w_dkv, 1), ("mla_kv_norm_g", mla_kv_norm_g, 1)])
    pc = _prog("C")
    rc = run_bass_kernel_spmd(pc, [dict(common(k, [1, 2, 3]), kv_all=kv_all, cq_in=cqs[k], **wC) for k in cores],
                              core_ids=cores).results
    xs = [np.asarray(rc[k]["x_out"]) for k in cores]
    kv_all = gather_kv(rc)
    cqs = [np.asarray(rc[k]["cq_out"]) for k in cores]
    del wC
    wD = lw([("mla_w_uq", mla_w_uq, 1), ("mla_w_ukv", mla_w_ukv, 1), ("mla_w_o", mla_w_o, 1),
             ("ffn_w_gate", ffn_w_gate, 3), ("ffn_w_up", ffn_w_up, 3), ("ffn_w_down", ffn_w_down, 3)])
    fgain = np.ascontiguousarray(np.broadcast_to(_fm(final_norm_g)[:, :, None], (128, KC, 2))).astype(np.float32)
    pd = _prog("D")
    inD = []
    for k in cores:
        m = common(k, [3])
        for nm in ("mask3", "invc"):
            m.pop(nm)
        m.update(kv_all=kv_all, cq_in=cqs[k], fgain=fgain, **wD)
        inD.append(m)
    rd = run_bass_kernel_spmd(pd, inD, core_ids=cores).results
    out = np.concatenate([np.asarray(rd[k]["y"]).T for k in cores], axis=0)
    return np.ascontiguousarray(out[None].astype(np.float32))
```

```python
import numpy as np
import ml_dtypes
from contextlib import ExitStack
import concourse.bass as bass
import concourse.mybir as mybir
from concourse.bass_utils import run_bass_kernel_spmd

F32 = mybir.dt.float32
BF16 = mybir.dt.bfloat16
AF = mybir.ActivationFunctionType
ALU = mybir.AluOpType
NPBF16 = ml_dtypes.bfloat16

NCORES = 8
D = 2048
KC = 16
SEQ = 8192
CTX = 256
HALO = 16
OWN = SEQ // NCORES
WIN = OWN + 2 * HALO
T = WIN + CTX
BLOCKS = [(0, 352, 0), (352, 352, 0), (704, 352, 0), (1056, 256, 1)]
DFF = 5632
FC = DFF // 128
NKEYS = SEQ + CTX
NKC = NKEYS // 128
EPS = 1e-6
NH = 16
GRID_W = 64
ENGS = ("sp", "act", "dve", "pool", "pe")
KD = 6


class Op:
    __slots__ = ("eng", "fn", "deps", "flag", "sem", "val", "dma")


class Sched:
    def __init__(self):
        self.ops = {e: [] for e in ENGS}
        self.lastw = {}
        self.rd = {}
        self.dmal = {e: [] for e in ENGS}
        self.lastreal = {e: None for e in ENGS}

    def add(self, eng, fn, reads=(), writes=(), dma=False):
        op = Op()
        op.eng, op.fn, op.dma, op.flag = eng, fn, dma, dma
        op.sem = None
        op.val = 0
        deps = []
        for k in reads:
            w = self.lastw.get(k)
            if w is not None:
                deps.append(w)
        for k in writes:
            w = self.lastw.get(k)
            if w is not None:
                deps.append(w)
            deps.extend(self.rd.get(k, ()))
        f = []
        for d in deps:
            if d.eng == "pe" and eng == "pe" and not d.dma and not dma:
                continue
            d.flag = True
            f.append(d)
        op.deps = f
        for k in writes:
            self.lastw[k] = op
            self.rd[k] = []
        for k in reads:
            lst = self.rd.setdefault(k, [])
            if not dma:
                for i, o in enumerate(lst):
                    if o.eng == eng and not o.dma:
                        lst[i] = op
                        break
                else:
                    lst.append(op)
            else:
                lst.append(op)
        self.ops[eng].append(op)
        if dma:
            self.dmal[eng].append(op)
        elif fn is not None:
            self.lastreal[eng] = op
        return op

    def barrier(self):
        deps = []
        for e in ENGS:
            if self.lastreal[e] is not None:
                deps.append(self.lastreal[e])
            deps.extend(self.dmal[e][-KD:])
        for d in deps:
            d.flag = True
        for e in ENGS:
            op = Op()
            op.eng, op.fn, op.dma, op.flag, op.sem, op.val = e, None, False, False, None, 0
            op.deps = [d for d in deps if not (d.eng == e and not d.dma)]
            self.ops[e].append(op)
        self.lastw = {}
        self.rd = {}

    def finalize(self, nc, stack):
        self.sem = {e: stack.enter_context(nc.semaphore("s_" + e)) for e in ENGS}
        self.dsem = {e: [stack.enter_context(nc.semaphore("d_%s_%d" % (e, i))) for i in range(KD)] for e in ENGS}
        for e in ENGS:
            cnt = 0
            dl = self.dmal[e]
            nd = 0
            for op in self.ops[e]:
                if op.dma:
                    assert dl[nd] is op
                    op.sem = self.dsem[e][nd % KD]
                    op.val = 16 * (nd // KD + 1)
                    if nd >= KD:
                        op.deps.append(dl[nd - KD])
                    nd += 1
                elif op.flag:
                    cnt += 1
                    op.sem = self.sem[e]
                    op.val = cnt

    def emit(self, ename, eng):
        waited = {}
        for op in self.ops[ename]:
            need = {}
            for d in op.deps:
                key = d.sem
                if d.val > need.get(key, (None, 0))[1]:
                    need[key] = (d.sem, d.val)
            for key, (sem, val) in need.items():
                if waited.get(key, 0) < val:
                    eng.wait_ge(sem, val)
                    waited[key] = val
            ins = op.fn(eng) if op.fn is not None else None
            if op.flag:
                ins.then_inc(op.sem, 16 if op.dma else 1)


class Prog:
    def __init__(self):
        self.nc = bass.Bass("TRN2", target_bir_lowering=False)
        self.S = Sched()
        self.stack = ExitStack()
        self.AE = 51200
        self.arena = self.stack.enter_context(self.nc.sbuf_tensor("arena", [128, self.AE], F32))
        self.psall = self.stack.enter_context(self.nc.psum_tensor("psall", [128, 8, 512], F32))
        self.ps = [self.psall[:, i, :] for i in range(8)]
        self.off = 0
        self.uid = 0
        self.inputs = {}
        self.psrot = 0

    def din(self, name, shape, dt=F32):
        self.inputs[name] = (tuple(shape), dt)
        return self.nc.dram_tensor(name, list(shape), dt, kind="ExternalInput").ap()

    def dbg(self, name, view, key, shape, dt=F32):
        d = self.nc.dram_tensor(name, [128] + list(shape), dt, kind="ExternalOutput").ap()
        self.S.add("sp", (lambda e: e.dma_start(out=d, in_=view)), reads=(key,), writes=("dbg_" + name,), dma=True)

    def dout(self, name, shape, dt=F32):
        return self.nc.dram_tensor(name, list(shape), dt, kind="ExternalOutput").ap()

    def dint(self, name, shape, dt=F32):
        return self.nc.dram_tensor(name, list(shape), dt).ap()

    def mark(self):
        return self.off

    def release(self, m):
        self.S.barrier()
        self.off = m

    def alloc(self, dt, *free):
        n = 1
        for f in free:
            n *= f
        nbytes = n * (2 if dt is BF16 else 4)
        nbytes = (nbytes + 63) // 64 * 64
        o = self.off
        self.off += nbytes
        self.hw = max(getattr(self, "hw", 0), self.off)
        assert self.off <= self.AE * 4, "SBUF arena overflow %d" % self.off
        a = self.arena[:, o // 4:(o + nbytes) // 4]
        if dt is BF16:
            a = a.bitcast(BF16)
        a = a[:, 0:n]
        if len(free) == 2:
            a = a.rearrange("p (a b) -> p a b", a=free[0])
        elif len(free) == 3:
            a = a.rearrange("p (a b c) -> p a b c", a=free[0], b=free[1])
        self.uid += 1
        return a, "b%d" % self.uid

    def barrier(self):
        self.S.barrier()

    def psbank(self, banks):
        b = banks[self.psrot % len(banks)]
        self.psrot += 1
        return b


def A_(eng, fn, reads=(), writes=(), dma=False, P=None):
    return P.S.add(eng, fn, reads, writes, dma)


def cast_op(S, eng, out_ap, in_ap, reads, writes):
    if eng == "act":
        S.add("act", (lambda e: e.activation(out=out_ap, in_=in_ap, func=AF.Identity)), reads=reads, writes=writes)
    else:
        S.add(eng, (lambda e: e.tensor_copy(out=out_ap, in_=in_ap)), reads=reads, writes=writes)


def linear(P, in_v, in_key, kcn, panels, blocks, epi, pre_chunk=None, post_chunk=None, post_panel=None,
           banks=(0, 1, 2, 3), nbuf=2, tag="w", cast=("dve", "act")):
    S = P.S
    maxc = max(sum(s[1] for s in p) for p in panels)
    m0 = P.mark()
    st = [P.alloc(F32, kcn, maxc) for _ in range(nbuf)]
    wb = [P.alloc(BF16, kcn, maxc) for _ in range(nbuf)]

    def load(pi):
        sv, sk = st[pi % nbuf]
        bv, bk = wb[pi % nbuf]
        c0 = 0
        for (wap, ncol) in panels[pi]:
            src = wap.rearrange("(kc p) n -> p kc n", p=128)
            dst = sv[:, :, c0:c0 + ncol]
            S.add("sp", (lambda e, d=dst, s=src: e.dma_start(out=d, in_=s)), reads=(), writes=(sk,), dma=True)
            c0 += ncol
        cast_op(S, cast[pi % len(cast)], bv[:, :, 0:c0], sv[:, :, 0:c0], (sk,), (bk,))

    load(0)
    for pi in range(len(panels)):
        if pi + 1 < len(panels):
            load(pi + 1)
        bv, bk = wb[pi % nbuf]
        c0 = 0
        ci = 0
        for (wap, ncol) in panels[pi]:
            for cc in range(0, ncol, 128):
                M = min(128, ncol - cc)
                if pre_chunk:
                    pre_chunk(pi, ci)
                for bi, (o, n, v) in enumerate(blocks):
                    b = P.psbank(banks)
                    pk = ("ps", b)
                    psap = P.ps[b][0:M, 0:n]

                    def mm(e, psap=psap, bv=bv, c=c0 + cc, M=M, o=o, n=n):
                        ins = None
                        for kc in range(kcn):
                            ins = e.matmul(psap, lhsT=bv[:, kc, c:c + M], rhs=in_v[:, kc, o:o + n],
                                           start=(kc == 0), stop=(kc == kcn - 1))
                        return ins
                    S.add("pe", mm, reads=(bk, in_key), writes=(pk,))
                    epi(pi, ci, bi, psap, pk, M)
                if post_chunk:
                    post_chunk(pi, ci)
                ci += 1
            c0 += ncol
        if post_panel:
            post_panel(pi)
    P.release(m0)


def split_cols(w, c0, c1, step):
    return [[(w[:, c:min(c + step, c1)], min(c + step, c1) - c)] for c in range(c0, c1, step)]


def load_small(P, dst, key, src):
    P.S.add("sp", (lambda e: e.dma_start(out=dst, in_=src)), writes=(key,), dma=True)


def rms_stats(P, src3, src_key, nchunk, o, n, sq, sqk, rstd, rk, ones, bank, inv_n):
    S = P.S
    S.add("act", (lambda e: e.activation(out=sq[:, 0:nchunk, 0:n], in_=src3[:, 0:nchunk, o:o + n], func=AF.Square)),
          reads=(src_key,), writes=(sqk,))
    pk = ("ps", bank)
    psap = P.ps[bank][:, 0:n]

    def mm(e):
        ins = None
        for c in range(nchunk):
            ins = e.matmul(psap, lhsT=ones, rhs=sq[:, c, 0:n], start=(c == 0), stop=(c == nchunk - 1))
        return ins
    S.add("pe", mm, reads=(sqk, "const"), writes=(pk,))
    S.add("dve", (lambda e: e.tensor_scalar(out=rstd[:, 0:n], in0=psap, scalar1=inv_n, scalar2=EPS,
                                            op0=ALU.mult, op1=ALU.add)), reads=(pk,), writes=(rk,))
    S.add("act", (lambda e: e.sqrt(out=rstd[:, 0:n], in_=rstd[:, 0:n])), reads=(rk,), writes=(rk,))
    S.add("dve", (lambda e: e.reciprocal(out=rstd[:, 0:n], in_=rstd[:, 0:n])), reads=(rk,), writes=(rk,))


def norm_phase(P, C, xd, Amod, Bmod, h, hk, blocks, mask3=None):
    S = P.S
    m0 = P.mark()
    xs = [P.alloc(F32, KC, 352) for _ in range(2)]
    sq, sqk = P.alloc(BF16, KC, 352)
    rs = [P.alloc(F32, 352) for _ in range(2)]
    tmp = [P.alloc(F32, 352) for _ in range(4)]
    xr = xd.rearrange("(c p) t -> p c t", p=128)
    ti = 0
    for bi, (o, n, v) in enumerate(blocks):
        xv, xk = xs[bi % 2]
        rv, rk = rs[bi % 2]
        S.add("sp", (lambda e, xv=xv, o=o, n=n: e.dma_start(out=xv[:, :, 0:n], in_=xr[:, :, o:o + n])),
              reads=("xd",), writes=(xk,), dma=True)
        rms_stats(P, xv, xk, KC, 0, n, sq, sqk, rv, rk, C["ones"], 7, 1.0 / D)
        for c in range(KC):
            tv, tk = tmp[ti % 4]
            ti += 1
            S.add("dve", (lambda e, tv=tv, xv=xv, c=c, n=n, v=v, rv=rv: e.scalar_tensor_tensor(
                out=tv[:, 0:n], in0=xv[:, c, 0:n], scalar=Amod[:, c, v:v + 1], in1=rv[:, 0:n],
                op0=ALU.mult, op1=ALU.mult)), reads=(xk, rk, "mod"), writes=(tk,))
            if Bmod is not None:
                S.add("act", (lambda e, tv=tv, c=c, o=o, n=n, v=v: e.activation(
                    out=h[:, c, o:o + n], in_=tv[:, 0:n], func=AF.Identity, bias=Bmod[:, c, v:v + 1], scale=1.0)),
                    reads=(tk, "mod"), writes=(hk,))
            else:
                S.add("act", (lambda e, tv=tv, c=c, o=o, n=n: e.activation(
                    out=h[:, c, o:o + n], in_=tv[:, 0:n], func=AF.Identity)), reads=(tk,), writes=(hk,))
    if mask3 is not None:
        for (a, b, ma) in ((0, HALO, 0), (WIN - HALO, WIN, HALO)):
            S.add("dve", (lambda e, a=a, b=b, ma=ma: e.tensor_tensor(
                out=h[:, :, a:b], in0=h[:, :, a:b], in1=mask3[:, :, ma:ma + HALO], op=ALU.mult)),
                reads=(hk, "const"), writes=(hk,))
    P.release(m0)


class Resid:
    def __init__(self, P, xd, gmod, blocks):
        self.P, self.xd, self.g, self.blocks = P, xd, gmod, blocks
        self.xt = [P.alloc(F32, T) for _ in range(2)]
        self.n = 0
        self.lo = min(b[0] for b in blocks)
        self.hi = max(b[0] + b[1] for b in blocks)

    def pre(self, pi, ci):
        self.cur = self.xt[self.n % 2]
        self.c = self.n
        self.n += 1
        xv, xk = self.cur
        c = self.c
        self.P.S.add("sp", (lambda e: e.dma_start(out=xv[:, self.lo:self.hi],
                                                  in_=self.xd[c * 128:(c + 1) * 128, self.lo:self.hi])),
                     reads=("xd",), writes=(xk,), dma=True)

    def epi(self, pi, ci, bi, ps, pk, M):
        xv, xk = self.cur
        o, n, v = self.blocks[bi]
        c = self.c
        self.P.S.add("dve", (lambda e: e.scalar_tensor_tensor(
            out=xv[:, o:o + n], in0=ps, scalar=self.g[:, c, v:v + 1], in1=xv[:, o:o + n],
            op0=ALU.mult, op1=ALU.add)), reads=(pk, xk, "mod"), writes=(xk,))

    def post(self, pi, ci):
        xv, xk = self.cur
        c = self.c
        self.P.S.add("sp", (lambda e: e.dma_start(out=self.xd[c * 128:(c + 1) * 128, self.lo:self.hi],
                                                  in_=xv[:, self.lo:self.hi])),
                     reads=(xk,), writes=("xd",), dma=True)


def ffn_phase(P, C, xd, w_gate, w_up, w_down, ud, Amod, Bmod, gmod, blocks):
    S = P.S
    m0 = P.mark()
    h, hk = P.alloc(BF16, KC, T)
    norm_phase(P, C, xd, Amod, Bmod, h, hk, blocks)
    sg = [P.alloc(F32, T) for _ in range(2)]
    ub = [P.alloc(BF16, T) for _ in range(3)]
    lo = min(b[0] for b in blocks)
    hi = max(b[0] + b[1] for b in blocks)
    panels = [[(w_gate[:, f * 128:(f + 1) * 128], 128), (w_up[:, f * 128:(f + 1) * 128], 128)] for f in range(FC)]

    def epi(pi, ci, bi, ps, pk, M):
        o, n, v = blocks[bi]
        sv, sk = sg[pi % 2]
        uv, uk = ub[pi % 3]
        if ci == 0:
            S.add("act", (lambda e: e.activation(out=sv[:, o:o + n], in_=ps, func=AF.Silu)), reads=(pk,), writes=(sk,))
        else:
            S.add("dve", (lambda e: e.tensor_tensor(out=uv[:, o:o + n], in0=sv[:, o:o + n], in1=ps, op=ALU.mult)),
                  reads=(pk, sk), writes=(uk,))

    def post_panel(pi):
        uv, uk = ub[pi % 3]
        S.add("sp", (lambda e: e.dma_start(out=ud[pi * 128:(pi + 1) * 128, lo:hi], in_=uv[:, lo:hi])),
              reads=(uk,), writes=("ud",), dma=True)
    linear(P, h, hk, KC, panels, blocks, epi, post_panel=post_panel, banks=(0, 1, 2, 3, 4, 5))
    P.release(m0)
    P.barrier()
    m0 = P.mark()
    u, ukk = P.alloc(BF16, FC, T)
    ur = ud.rearrange("(c p) t -> p c t", p=128)
    for q in range(4):
        S.add("sp", (lambda e, q=q: e.dma_start(out=u[:, q * 11:(q + 1) * 11, lo:hi], in_=ur[:, q * 11:(q + 1) * 11, lo:hi])),
              reads=("ud",), writes=(ukk,), dma=True)
    R = Resid(P, xd, gmod, blocks)
    linear(P, u, ukk, FC, split_cols(w_down, 0, D, 128), blocks, R.epi, pre_chunk=R.pre, post_chunk=R.post,
           banks=(0, 1, 2, 3, 4, 5))
    P.release(m0)
    P.barrier()


LATP = 16
TP = (WIN + 2 * LATP) + (CTX + 2 * LATP)
SEQS = ((LATP, WIN, 0), (WIN + 3 * LATP, CTX, WIN))


def even_mixer(P, C, xd, h, hk, w_in, pool_w, pool_scale, conv_w, w_out, gmod, blocks, invc_d):
    S = P.S
    m0 = P.mark()
    yab, yk = P.alloc(BF16, KC, T)
    pwb, pwbk = P.alloc(BF16, 4, 2, 256)
    psc, psck = P.alloc(F32, 8)
    cw, cwk = P.alloc(F32, 3, 8)
    mA = P.mark()
    pw32, pw32k = P.alloc(F32, 4, 2, 256)
    load_small(P, pw32, pw32k, pool_w.rearrange("g (kc p) n -> p g kc n", p=128))
    S.add("pool", (lambda e: e.tensor_copy(out=pwb, in_=pw32)), reads=(pw32k,), writes=(pwbk,))
    load_small(P, psc, psck, pool_scale)
    load_small(P, cw, cwk, conv_w)
    P.barrier()
    P.release(mA)
    invg = [P.alloc(F32, T) for _ in range(2)]
    U, Uk = P.alloc(F32, 2, TP)
    LA, LAk = P.alloc(F32, 2, TP)
    LB, LBk = P.alloc(F32, 2, TP)
    pb, pbk = P.alloc(BF16, 2, T)
    for (buf, k) in ((U, Uk), (LA, LAk), (LB, LBk)):
        S.add("pool", (lambda e, buf=buf: e.memset(buf, 0.0)), writes=(k,))
    for g in range(4):
        iv, ik = invg[g % 2]
        if g < 2:
            load_small(P, iv, ik, invc_d[:, g, :])

    def epi_pool(pi, ci, bi, ps, pk, M):
        o, n, v = blocks[bi]
        po = (LATP + o) if v == 0 else (WIN + 3 * LATP + o - WIN)
        S.add("act", (lambda e: e.activation(out=U[:, ci, po:po + n], in_=ps, func=AF.Identity)),
              reads=(pk,), writes=(Uk,))

    def post_pool(g):
        iv, ik = invg[g % 2]
        nlev = g + 1
        src, srck = U, Uk
        dsts = [(LA, LAk), (LB, LBk)]
        sh = [(1, 0), (1, 1), (2, 2), (4, 4)]
        for l in range(nlev):
            dst, dstk = dsts[l % 2]
            a, b = sh[l]
            for (so, sn, to) in SEQS:
                lo_, hi_ = so - 8, so + sn + 8
                S.add("dve", (lambda e, dst=dst, src=src, lo_=lo_, hi_=hi_, a=a, b=b: e.tensor_tensor(
                    out=dst[:, :, lo_:hi_], in0=src[:, :, lo_ - a:hi_ - a], in1=src[:, :, lo_ + b:hi_ + b], op=ALU.add)),
                    reads=(srck,), writes=(dstk,))
            src, srck = dst, dstk
        for (so, sn, to) in SEQS:
            for j in range(2):
                S.add("dve", (lambda e, src=src, so=so, sn=sn, to=to, j=j: e.tensor_tensor(
                    out=src[:, j, so:so + sn], in0=src[:, j, so:so + sn], in1=iv[:, to:to + sn], op=ALU.mult)),
                    reads=(srck, ik), writes=(srck,))
            S.add("dve", (lambda e, src=src, so=so, sn=sn, to=to: e.tensor_tensor(
                out=pb[:, :, to:to + sn], in0=src[:, :, so:so + sn], in1=U[:, :, so:so + sn], op=ALU.subtract)),
                reads=(srck, Uk), writes=(pbk,))
        if g + 2 < 4:
            load_small(P, iv, ik, invc_d[:, g + 2, :])
        for nn in range(2):
            for bi, (o, n, v) in enumerate(blocks):
                b = P.psbank((4, 5))
                pk = ("ps", b)
                psap = P.ps[b][:, 0:n]

                def mm(e, psap=psap, nn=nn, o=o, n=n):
                    ins = None
                    for kc in range(2):
                        ins = e.matmul(psap, lhsT=pwb[:, g, kc, nn * 128:(nn + 1) * 128], rhs=pb[:, kc, o:o + n],
                                       start=(kc == 0), stop=(kc == 1))
                    return ins
                S.add("pe", mm, reads=(pwbk, pbk), writes=(pk,))
                S.add("act", (lambda e, psap=psap, nn=nn, o=o, n=n: e.activation(
                    out=yab[:, 2 * g + nn, o:o + n], in_=psap, func=AF.Identity, scale=psc[:, 2 * g + nn:2 * g + nn + 1])),
                    reads=(pk, psck), writes=(yk,))

    linear(P, h, hk, KC, split_cols(w_in, 0, 1024, 256), blocks, epi_pool, post_panel=post_pool, banks=(0, 1, 2, 3))
    if "yab" in DEBUG["dump"]:
        P.dbg("dbg_U", U, Uk, [2, TP])
        P.dbg("dbg_LB", LB, LBk, [2, TP])
        P.dbg("dbg_pb", pb, pbk, [2, T], BF16)
    P.barrier()
    P.release(mA)

    TPC = T + 4
    CSEQ = ((1, WIN, 0), (WIN + 3, CTX, WIN))
    gbs, gbk = P.alloc(F32, T)
    gcf, gck = P.alloc(F32, T)
    uc, uck = P.alloc(F32, TPC)
    cv, cvk = P.alloc(F32, T)
    S.add("pool", (lambda e: e.memset(uc, 0.0)), writes=(uck,))
    panels = [[(w_in[:, 1024 + c * 128:1024 + (c + 1) * 128], 128), (w_in[:, 2048 + c * 128:2048 + (c + 1) * 128], 128),
               (w_in[:, 3072 + c * 128:3072 + (c + 1) * 128], 128)] for c in range(8)]

    def epi_conv(pi, ci, bi, ps, pk, M):
        o, n, v = blocks[bi]
        if ci == 0:
            S.add("act", (lambda e: e.activation(out=gbs[:, o:o + n], in_=ps, func=AF.Identity)), reads=(pk,), writes=(gbk,))
        elif ci == 1:
            S.add("act", (lambda e: e.activation(out=gcf[:, o:o + n], in_=ps, func=AF.Identity)), reads=(pk,), writes=(gck,))
        else:
            po = (1 + o) if v == 0 else (WIN + 3 + o - WIN)
            S.add("dve", (lambda e: e.tensor_tensor(out=uc[:, po:po + n], in0=gcf[:, o:o + n], in1=ps, op=ALU.mult)),
                  reads=(pk, gck), writes=(uck,))

    def post_conv(c):
        for (so, sn, to) in CSEQ:
            S.add("dve", (lambda e, so=so, sn=sn, to=to: e.tensor_scalar(
                out=cv[:, to:to + sn], in0=uc[:, so - 1:so - 1 + sn], scalar1=cw[:, 0, c:c + 1], scalar2=None, op0=ALU.mult)),
                reads=(uck, cwk), writes=(cvk,))
            for k in (1, 2):
                S.add("dve", (lambda e, so=so, sn=sn, to=to, k=k: e.scalar_tensor_tensor(
                    out=cv[:, to:to + sn], in0=uc[:, so - 1 + k:so - 1 + k + sn], scalar=cw[:, k, c:c + 1],
                    in1=cv[:, to:to + sn], op0=ALU.mult, op1=ALU.add)), reads=(uck, cwk, cvk), writes=(cvk,))
        S.add("dve", (lambda e: e.tensor_tensor(out=yab[:, 8 + c, :], in0=gbs, in1=cv, op=ALU.mult)),
              reads=(gbk, cvk), writes=(yk,))

    linear(P, h, hk, KC, panels, blocks, epi_conv, post_panel=post_conv, banks=(0, 1, 2, 3, 4, 5))
    if "yab" in DEBUG["dump"]:
        P.dbg("dbg_yab", yab, yk, [KC, T], BF16)
        P.dbg("dbg_gbs", gbs, gbk, [T])
        P.dbg("dbg_uc", uc, uck, [TPC])
        P.dbg("dbg_cv", cv, cvk, [T])
    if DEBUG["stop"] == "conv":
        raise StopBuild(P)
    P.barrier()
    P.release(mA)
    R = Resid(P, xd, gmod, blocks)
    linear(P, yab, yk, KC, split_cols(w_out, 0, D, 256), blocks, R.epi, pre_chunk=R.pre, post_chunk=R.post,
           banks=(0, 1, 2, 3, 4, 5))
    P.release(m0)
    P.barrier()


def odd_pre(P, C, h, hk, w_dq, qg_d, w_dkv, kvg_d, rope_cos_d, rope_sin_d, kv_lat, kv_ctx, cq_out, blocks):
    S = P.S
    m0 = P.mark()
    qg, qgk = P.alloc(F32, 4)
    kvg, kvgk = P.alloc(F32, 4)
    load_small(P, qg, qgk, qg_d)
    load_small(P, kvg, kvgk, kvg_d)
    cos, cosk = P.alloc(F32, T)
    sin, sink = P.alloc(F32, T)
    load_small(P, cos[0:64, :], cosk, rope_cos_d)
    load_small(P, sin[0:64, :], sink, rope_sin_d)
    c32, c32k = P.alloc(F32, 4, T)
    kr32, kr32k = P.alloc(F32, T)
    krb, krbk = P.alloc(BF16, T)
    kro, krok = P.alloc(BF16, T)
    cn, cnk = P.alloc(BF16, 4, T)
    sq, sqk = P.alloc(BF16, 4, 352)
    rs = [P.alloc(F32, 352) for _ in range(2)]
    t1 = [P.alloc(F32, 352) for _ in range(2)]
    t2 = [P.alloc(F32, 352) for _ in range(2)]

    def epi(pi, ci, bi, ps, pk, M):
        o, n, v = blocks[bi]
        ch = pi * 2 + ci
        if M == 128:
            S.add("act", (lambda e: e.activation(out=c32[:, ch, o:o + n], in_=ps, func=AF.Identity)),
                  reads=(pk,), writes=(c32k,))
        else:
            S.add("act", (lambda e: e.activation(out=kr32[0:64, o:o + n], in_=ps, func=AF.Identity)),
                  reads=(pk,), writes=(kr32k,))

    def normalize(g):
        for bi, (o, n, v) in enumerate(blocks):
            rv, rk = rs[bi % 2]
            rms_stats(P, c32, c32k, 4, o, n, sq, sqk, rv, rk, C["ones"], 7, 1.0 / 512)
            for c in range(4):
                S.add("dve", (lambda e, c=c, o=o, n=n, rv=rv: e.scalar_tensor_tensor(
                    out=cn[:, c, o:o + n], in0=c32[:, c, o:o + n], scalar=g[:, c:c + 1], in1=rv[:, 0:n],
                    op0=ALU.mult, op1=ALU.mult)), reads=(c32k, rk, qgk, kvgk), writes=(cnk,))

    lo = min(b[0] for b in blocks)
    hi = max(b[0] + b[1] for b in blocks)
    linear(P, h, hk, KC, split_cols(w_dq, 0, 512, 256), blocks, epi, banks=(0, 1, 2, 3))
    normalize(qg)
    S.add("sp", (lambda e: e.dma_start(out=cq_out.rearrange("(c p) t -> p c t", p=128)[:, :, lo:hi], in_=cn[:, :, lo:hi])),
          reads=(cnk,), writes=("cq_out",), dma=True)
    linear(P, h, hk, KC, split_cols(w_dkv, 0, 576, 256), blocks, epi, banks=(0, 1, 2, 3))
    normalize(kvg)
    S.add("sp", (lambda e: e.dma_start(out=kv_lat[0:512, :].rearrange("(c p) t -> p c t", p=128), in_=cn[:, :, HALO:HALO + OWN])),
          reads=(cnk,), writes=("kv_out",), dma=True)
    if kv_ctx is not None:
        S.add("sp", (lambda e: e.dma_start(out=kv_ctx[0:512, :].rearrange("(c p) t -> p c t", p=128), in_=cn[:, :, WIN:WIN + CTX])),
              reads=(cnk,), writes=("kv_out",), dma=True)
    S.add("act", (lambda e: e.activation(out=krb[0:64, lo:hi], in_=kr32[0:64, lo:hi], func=AF.Identity)),
          reads=(kr32k,), writes=(krbk,))
    for bi, (o, n, v) in enumerate(blocks):
        b = P.psbank((4, 5))
        pk = ("ps", b)
        psap = P.ps[b][0:64, 0:n]
        S.add("pe", (lambda e, psap=psap, o=o, n=n: e.matmul(psap, lhsT=C["rot"], rhs=krb[0:64, o:o + n], start=True, stop=True)),
              reads=(krbk, "const"), writes=(pk,))
        av, ak = t1[bi % 2]
        bv, bk = t2[bi % 2]
        S.add("dve", (lambda e, av=av, o=o, n=n: e.tensor_tensor(out=av[0:64, 0:n], in0=kr32[0:64, o:o + n],
                                                               in1=cos[0:64, o:o + n], op=ALU.mult)),
              reads=(kr32k, cosk), writes=(ak,))
        S.add("dve", (lambda e, bv=bv, psap=psap, o=o, n=n: e.tensor_tensor(out=bv[0:64, 0:n], in0=psap,
                                                                          in1=sin[0:64, o:o + n], op=ALU.mult)),
              reads=(pk, sink), writes=(bk,))
        S.add("dve", (lambda e, av=av, bv=bv, o=o, n=n: e.tensor_tensor(out=kro[0:64, o:o + n], in0=av[0:64, 0:n],
                                                                      in1=bv[0:64, 0:n], op=ALU.add)),
              reads=(ak, bk), writes=(krok,))
    S.add("sp", (lambda e: e.dma_start(out=kv_lat[512:576, :], in_=kro[0:64, HALO:HALO + OWN])),
          reads=(krok,), writes=("kv_out",), dma=True)
    if kv_ctx is not None:
        S.add("sp", (lambda e: e.dma_start(out=kv_ctx[512:576, :], in_=kro[0:64, WIN:WIN + CTX])),
              reads=(krok,), writes=("kv_out",), dma=True)
    P.release(m0)
    P.barrier()


def attention(P, C, xd, od, kv_all, cq_in, w_uq, w_ukv, w_o, rope_cos_d, rope_sin_d, gmod, blocks):
    S = P.S
    scale = float((128 + 64) ** -0.5)
    m0 = P.mark()
    ckv, ckvk = P.alloc(BF16, 4, NKEYS)
    kr, krk = P.alloc(BF16, NKEYS)
    cq, cqk = P.alloc(BF16, 4, T)
    cos, cosk = P.alloc(F32, T)
    sin, sink = P.alloc(F32, T)
    load_small(P, cos[0:64, :], cosk, rope_cos_d)
    load_small(P, sin[0:64, :], sink, rope_sin_d)
    kvr = kv_all[0:512, :].rearrange("(c p) t -> p c t", p=128)
    for c in range(4):
        S.add("sp", (lambda e, c=c: e.dma_start(out=ckv[:, c, :], in_=kvr[:, c, :])), writes=(ckvk,), dma=True)
    S.add("pool", (lambda e: e.memset(kr, 0.0)), writes=(krk,))
    S.add("sp", (lambda e: e.dma_start(out=kr[0:64, :], in_=kv_all[512:576, :])), writes=(krk,), dma=True)
    S.add("sp", (lambda e: e.dma_start(out=cq, in_=cq_in.rearrange("(c p) t -> p c t", p=128))), writes=(cqk,), dma=True)
    wq32 = [P.alloc(F32, 4, 192) for _ in range(2)]
    wqb = [P.alloc(BF16, 4, 192) for _ in range(2)]
    wk32 = [P.alloc(F32, 4, 256) for _ in range(2)]
    wkb = [P.alloc(BF16, 4, 256) for _ in range(2)]
    qn, qnk = P.alloc(BF16, T)
    qr32, qr32k = P.alloc(F32, T)
    qrb, qrbk = P.alloc(BF16, T)
    qro, qrok = P.alloc(BF16, T)
    S.add("pool", (lambda e: e.memset(qro, 0.0)), writes=(qrok,))
    kT, kTk = P.alloc(BF16, NKEYS)
    vh, vhk = P.alloc(BF16, NKC, 128)
    pT = [P.alloc(BF16, 2, 352) for _ in range(3)]
    t1 = [P.alloc(F32, 352) for _ in range(2)]
    t2 = [P.alloc(F32, 352) for _ in range(2)]
    rsum = [P.alloc(F32, 352) for _ in range(2)]
    sacc = [P.alloc(F32, 2, 352) for _ in range(2)]
    ones32, ones32k = P.alloc(F32, 128)
    S.add("pool", (lambda e: e.memset(ones32, 1.0)), writes=(ones32k,))
    obuf = [P.alloc(BF16, T) for _ in range(2)]
    wqr = w_uq.rearrange("(kc p) n -> p kc n", p=128)
    wkr = w_ukv.rearrange("(kc p) n -> p kc n", p=128)
    lo = min(b[0] for b in blocks)
    hi = max(b[0] + b[1] for b in blocks)
    GB = (0, 1, 2, 3)
    evac = [0]
    att_rot = [0]

    def loadw(hd):
        a, ak = wq32[hd % 2]
        ab, abk = wqb[hd % 2]
        b, bk = wk32[hd % 2]
        bb, bbk = wkb[hd % 2]
        S.add("sp", (lambda e: e.dma_start(out=a, in_=wqr[:, :, hd * 192:(hd + 1) * 192])), writes=(ak,), dma=True)
        S.add("sp", (lambda e: e.dma_start(out=b, in_=wkr[:, :, hd * 256:(hd + 1) * 256])), writes=(bk,), dma=True)
        S.add("pool", (lambda e: e.tensor_copy(out=ab, in_=a)), reads=(ak,), writes=(abk,))
        S.add("pool", (lambda e: e.tensor_copy(out=bb, in_=b)), reads=(bk,), writes=(bbk,))

    def evacuate(out_ap, ps_ap, pk, wkey):
        evac[0] += 1
        if evac[0] % 2:
            S.add("act", (lambda e: e.activation(out=out_ap, in_=ps_ap, func=AF.Identity)), reads=(pk,), writes=(wkey,))
        else:
            S.add("dve", (lambda e: e.tensor_copy(out=out_ap, in_=ps_ap)), reads=(pk,), writes=(wkey,))

    def group4(psap, lhs_fn, rhs_fn, reads, pk):
        def mm(e):
            ins = None
            for kc in range(4):
                ins = e.matmul(psap, lhsT=lhs_fn(kc), rhs=rhs_fn(kc), start=(kc == 0), stop=(kc == 3))
            return ins
        S.add("pe", mm, reads=reads, writes=(pk,))

    def q_block(hd, bi, o, n, v, wq, wqk_):
        b = P.psbank(GB)
        pk = ("ps", b)
        psap = P.ps[b][:, 0:n]
        group4(psap, (lambda kc: wq[:, kc, 0:128]), (lambda kc: cq[:, kc, o:o + n]), (wqk_, cqk), pk)
        evacuate(qn[:, o:o + n], psap, pk, qnk)
        qs = DEBUG.get("q_steps", 9)
        if qs < 2:
            return
        b = P.psbank(GB)
        pk2 = ("ps", b)
        psr = P.ps[b][0:64, 0:n]
        group4(psr, (lambda kc: wq[:, kc, 128:192]), (lambda kc: cq[:, kc, o:o + n]), (wqk_, cqk), pk2)
        S.add("act", (lambda e: e.activation(out=qr32[0:64, o:o + n], in_=psr, func=AF.Identity)), reads=(pk2,), writes=(qr32k,))
        S.add("dve", (lambda e: e.tensor_copy(out=qrb[0:64, o:o + n], in_=qr32[0:64, o:o + n])), reads=(qr32k,), writes=(qrbk,))
        if qs < 3:
            return
        b = P.psbank(GB)
        pk3 = ("ps", b)
        psq = P.ps[b][0:64, 0:n]
        S.add("pe", (lambda e: e.matmul(psq, lhsT=C["rot"], rhs=qrb[0:64, o:o + n], start=True, stop=True)),
              reads=(qrbk, "const"), writes=(pk3,))
        if qs < 4:
            return
        av, ak = t1[bi % 2]
        bv, bk = t2[bi % 2]
        S.add("dve", (lambda e: e.tensor_tensor(out=av[0:64, 0:n], in0=qr32[0:64, o:o + n], in1=cos[0:64, o:o + n], op=ALU.mult)),
              reads=(qr32k, cosk), writes=(ak,))
        S.add("dve", (lambda e: e.tensor_tensor(out=bv[0:64, 0:n], in0=psq, in1=sin[0:64, o:o + n], op=ALU.mult)),
              reads=(pk3, sink), writes=(bk,))
        S.add("dve", (lambda e: e.tensor_tensor(out=qro[0:64, o:o + n], in0=av[0:64, 0:n], in1=bv[0:64, 0:n], op=ALU.add)),
              reads=(ak, bk), writes=(qrok,))

    def k_block(kb, n, wk, wkk_):
        b = P.psbank(GB)
        pk = ("ps", b)
        psap = P.ps[b][:, 0:n]
        group4(psap, (lambda kc: wk[:, kc, 0:128]), (lambda kc: ckv[:, kc, kb:kb + n]), (wkk_, ckvk), pk)
        evacuate(kT[:, kb:kb + n], psap, pk, kTk)

    def v_block(kc0, nk, wk, wkk_):
        b = P.psbank(GB)
        pk = ("ps", b)

        def mmv(e):
            ins = None
            for j in range(nk):
                for kc in range(4):
                    ins = e.matmul(P.ps[b][:, j * 128:(j + 1) * 128], lhsT=ckv[:, kc, (kc0 + j) * 128:(kc0 + j + 1) * 128],
                                   rhs=wk[:, kc, 128:256], start=(kc == 0), stop=(kc == 3))
            return ins
        S.add("pe", mmv, reads=(wkk_, ckvk), writes=(pk,))
        evacuate(vh[:, kc0:kc0 + nk, :], P.ps[b][:, 0:nk * 128].rearrange("p (a b) -> p a b", a=nk), pk, vhk)

    def att_block(hd, bi, o, n, v, ov, ovk):
        kcs = list(range(NKC)) if v == 0 else list(range(SEQ // 128, NKC))
        pairs = [(kcs[i], kcs[i + 1]) for i in range(0, len(kcs), 2)]
        ob, sb = 6, 7
        opk, spk = ("ps", ob), ("ps", sb)
        ops_ap, sps_ap = P.ps[ob][:, 0:n], P.ps[sb][:, 0:n]
        av, avk = sacc[bi % 2]
        cnt = [0]

        def qk(pi, pr):
            b = 2 * (att_rot[0] % 3)
            att_rot[0] += 1
            pk0, pk1 = ("ps", b), ("ps", b + 1)

            def mm(e):
                ins = None
                for j, kc in enumerate(pr):
                    psap = P.ps[b + j][:, 0:n]
                    e.matmul(psap, lhsT=kT[:, kc * 128:(kc + 1) * 128], rhs=qn[:, o:o + n], start=True, stop=False)
                    ins = e.matmul(psap, lhsT=kr[:, kc * 128:(kc + 1) * 128], rhs=qro[:, o:o + n], start=False, stop=True)
                return ins
            S.add("pe", mm, reads=(kTk, krk, qnk, qrok), writes=(pk0, pk1))
            pv, pvk = pT[pi % 3]
            S.add("act", (lambda e: e.activation(out=pv[:, :, 0:n], in_=P.psall[:, b:b + 2, 0:n], func=AF.Exp, scale=scale)),
                  reads=(pk0, pk1), writes=(pvk,))
            return (pr, pv, pvk)

        def pvs(item, first, last):
            pr, pv, pvk = item

            def mm(e):
                e.matmul(ops_ap, lhsT=vh[:, pr[0], :], rhs=pv[:, 0, 0:n], start=first, stop=False)
                return e.matmul(ops_ap, lhsT=vh[:, pr[1], :], rhs=pv[:, 1, 0:n], start=False, stop=last)
            S.add("pe", mm, reads=(pvk, vhk), writes=(opk,))
            if first:
                S.add("dve", (lambda e: e.tensor_copy(out=av[:, :, 0:n], in_=pv[:, :, 0:n])), reads=(pvk,), writes=(avk,))
            else:
                S.add("dve", (lambda e: e.tensor_tensor(out=av[:, :, 0:n], in0=av[:, :, 0:n], in1=pv[:, :, 0:n], op=ALU.add)),
                      reads=(pvk, avk), writes=(avk,))
            if last:
                def mms(e):
                    e.matmul(sps_ap, lhsT=ones32, rhs=av[:, 0, 0:n], start=True, stop=False)
                    return e.matmul(sps_ap, lhsT=ones32, rhs=av[:, 1, 0:n], start=False, stop=True)
                S.add("pe", mms, reads=(avk, ones32k), writes=(spk,))

        pend = []
        done = 0
        for pi, pr in enumerate(pairs):
            pend.append(qk(pi, pr))
            if len(pend) > 1:
                pvs(pend.pop(0), done == 0, False)
                done += 1
        while pend:
            it = pend.pop(0)
            pvs(it, done == 0, len(pend) == 0)
            done += 1
        rv, rk = rsum[bi % 2]
        S.add("dve", (lambda e: e.reciprocal(out=rv[:, 0:n], in_=sps_ap)), reads=(spk,), writes=(rk,))
        S.add("dve", (lambda e: e.tensor_tensor(out=ov[:, o:o + n], in0=ops_ap, in1=rv[:, 0:n], op=ALU.mult)),
              reads=(opk, rk), writes=(ovk,))

    parts = DEBUG.get("att_parts", "qkva")

    def do_head(hd):
        if hd + 1 < NH:
            loadw(hd + 1)
        wq, wqk_ = wqb[hd % 2]
        wk, wkk_ = wkb[hd % 2]
        if "q" in parts:
            for bi, (o, n, v) in enumerate(blocks):
                q_block(hd, bi, o, n, v, wq, wqk_)
        if "k" in parts:
            for kb in range(0, NKEYS, 512):
                k_block(kb, min(512, NKEYS - kb), wk, wkk_)
        if "v" in parts:
            for kc0 in range(0, NKC, 4):
                v_block(kc0, min(4, NKC - kc0), wk, wkk_)
        ov, ovk = obuf[hd % 2]
        if "a" in parts:
            for bi, (o, n, v) in enumerate(blocks):
                att_block(hd, bi, o, n, v, ov, ovk)
            S.add("sp", (lambda e: e.dma_start(out=od[hd * 128:(hd + 1) * 128, lo:hi], in_=ov[:, lo:hi])),
                  reads=(ovk,), writes=("od",), dma=True)

    loadw(0)
    for hd in range(DEBUG.get("att_heads", NH)):
        do_head(hd)
    if "att_dump" in DEBUG["dump"]:
        P.dbg("dbg_qn", qn, qnk, [T], BF16)
        P.dbg("dbg_qro", qro, qrok, [T], BF16)
        P.dbg("dbg_kT", kT, kTk, [NKEYS], BF16)
        P.dbg("dbg_vh", vh, vhk, [NKC, 128], BF16)
        P.dbg("dbg_ov", obuf[0][0], obuf[0][1], [T], BF16)
    if DEBUG["stop"] == "attcore":
        raise StopBuild(P)
    P.barrier()
    P.release(m0)
    m0 = P.mark()
    oT, oTk = P.alloc(BF16, KC, T)
    odr = od.rearrange("(c p) t -> p c t", p=128)
    for q in range(4):
        S.add("sp", (lambda e, q=q: e.dma_start(out=oT[:, q * 4:(q + 1) * 4, lo:hi], in_=odr[:, q * 4:(q + 1) * 4, lo:hi])),
              reads=("od",), writes=(oTk,), dma=True)
    R = Resid(P, xd, gmod, blocks)
    linear(P, oT, oTk, KC, split_cols(w_o, 0, D, 256), blocks, R.epi, pre_chunk=R.pre, post_chunk=R.post,
           banks=(0, 1, 2, 3, 4, 5))
    P.release(m0)
    P.barrier()


def load_consts(P):
    S = P.S
    C = {}
    ones, _ = P.alloc(BF16, 128)
    rot, _ = P.alloc(BF16, 64)
    o_d = P.din("c_ones", [128, 128], BF16)
    r_d = P.din("c_rot", [64, 64], BF16)
    S.add("sp", (lambda e: e.dma_start(out=ones, in_=o_d)), writes=("const",), dma=True)
    S.add("sp", (lambda e: e.dma_start(out=rot[0:64, :], in_=r_d)), writes=("const",), dma=True)
    C["ones"] = ones
    C["rot"] = rot[0:64, :]
    return C


def load_mod(P, C, layers, mod_ap=None):
    S = P.S
    nl = len(layers)
    mod_d = mod_ap if mod_ap is not None else P.din("mod", [128, nl, 6, KC, 2])
    ng_d = P.din("ng", [128, nl, 2, KC])
    mod, _ = P.alloc(F32, nl * 6, KC, 2)
    ng, _ = P.alloc(F32, nl * 2, KC)
    S.add("sp", (lambda e: e.dma_start(out=mod, in_=mod_d.rearrange("p l s c v -> p (l s) c v"))), writes=("mod",), dma=True)
    S.add("sp", (lambda e: e.dma_start(out=ng, in_=ng_d.rearrange("p l s c -> p (l s) c"))), writes=("mod",), dma=True)
    out = {}
    for li, L in enumerate(layers):
        d = {}
        for w, (si, ni) in enumerate(((1, 0), (4, 1))):
            a, _ = P.alloc(F32, KC, 2)
            for v in range(2):
                S.add("dve", (lambda e, a=a, v=v, li=li, si=si, ni=ni: e.scalar_tensor_tensor(
                    out=a[:, :, v], in0=mod[:, li * 6 + si, :, v], scalar=1.0, in1=ng[:, li * 2 + ni, :],
                    op0=ALU.add, op1=ALU.mult)), reads=("mod",), writes=("mod",))
            d["A%d" % (w + 1)] = a
        d["B1"] = mod[:, li * 6 + 0]
        d["G1"] = mod[:, li * 6 + 2]
        d["B2"] = mod[:, li * 6 + 3]
        d["G2"] = mod[:, li * 6 + 5]
        out[L] = d
    return out


class StopBuild(Exception):
    pass


DEBUG = {"stop": None, "dump": ()}


def build_stage(stage):
    try:
        return _build_stage(stage)
    except StopBuild as e:
        P = e.args[0]
        P.barrier()
        return P


def _build_stage(stage):
    P = Prog()
    S = P.S
    C = load_consts(P)
    layers = {"B": [0, 1], "C": [1, 2, 3], "D": [3]}[stage]
    M = load_mod(P, C, layers)
    x_in = P.din("x_in", [D, T])
    if stage == "D":
        xd = P.dint("x_state", [D, T])
    else:
        xd = P.dout("x_out", [D, T])
    ud = P.dint("u_spill", [DFF, T], BF16)
    od = P.dint("o_spill", [D, T], BF16)
    S.add("sp", (lambda e: e.dma_start(out=xd, in_=x_in)), writes=("xd",), dma=True)
    rope_cos = P.din("rope_cos", [64, T])
    rope_sin = P.din("rope_sin", [64, T])
    W = {}

    def win(name, shape):
        W[name] = P.din(name, shape)
        return W[name]

    def ffn_w(L):
        return (win("ffn_w_gate_%d" % L, [D, DFF]), win("ffn_w_up_%d" % L, [D, DFF]), win("ffn_w_down_%d" % L, [DFF, D]))

    def even_layer(L):
        i = L // 2
        mask_d = P.din("mask3", [128, KC, 2 * HALO])
        invc_d = P.din("invc", [128, 4, T])
        w_in = win("even_w_in_%d" % i, [D, 4096])
        pool_w = win("pool_w_%d" % i, [4, 256, 256])
        pool_scale = win("pool_scale_%d" % i, [128, 8])
        conv_w = win("conv_w_%d" % i, [128, 3, 8])
        w_out = win("even_w_out_%d" % i, [D, D])
        m0 = P.mark()
        mask3, _ = P.alloc(F32, KC, 2 * HALO)
        S.add("sp", (lambda e: e.dma_start(out=mask3, in_=mask_d)), writes=("const",), dma=True)
        h, hk = P.alloc(BF16, KC, T)
        norm_phase(P, C, xd, M[L]["A1"], M[L]["B1"], h, hk, BLOCKS, mask3=mask3)
        if "h" in DEBUG["dump"]:
            P.dbg("dbg_h", h, hk, [KC, T], BF16)
        if DEBUG["stop"] == "norm%d" % L:
            raise StopBuild(P)
        even_mixer(P, C, xd, h, hk, w_in, pool_w, pool_scale, conv_w, w_out, M[L]["G1"], BLOCKS, invc_d)
        P.release(m0)
        if DEBUG["stop"] == "mixer%d" % L:
            raise StopBuild(P)
        wg, wu, wd = ffn_w(L)
        ffn_phase(P, C, xd, wg, wu, wd, ud, M[L]["A2"], M[L]["B2"], M[L]["G2"], BLOCKS)
        if DEBUG["stop"] == "ffn%d" % L:
            raise StopBuild(P)

    def odd_layer_pre(L):
        i = L // 2
        w_dq = win("mla_w_dq_%d" % i, [D, 512])
        qg = win("mla_q_norm_g_%d" % i, [128, 4])
        w_dkv = win("mla_w_dkv_%d" % i, [D, 576])
        kvg = win("mla_kv_norm_g_%d" % i, [128, 4])
        kv_out = P.dout("kv_out", [576, OWN + CTX], BF16)
        cq_out = P.dout("cq_out", [512, T], BF16)
        m0 = P.mark()
        h, hk = P.alloc(BF16, KC, T)
        norm_phase(P, C, xd, M[L]["A1"], M[L]["B1"], h, hk, BLOCKS)
        odd_pre(P, C, h, hk, w_dq, qg, w_dkv, kvg, rope_cos, rope_sin, kv_out[:, 0:OWN], kv_out[:, OWN:OWN + CTX], cq_out, BLOCKS)
        P.release(m0)

    def odd_layer_post(L, blocks):
        i = L // 2
        kv_all = P.din("kv_all", [576, NKEYS], BF16)
        cq_in = P.din("cq_in", [512, T], BF16)
        w_uq = win("mla_w_uq_%d" % i, [512, 3072])
        w_ukv = win("mla_w_ukv_%d" % i, [512, 4096])
        w_o = win("mla_w_o_%d" % i, [D, D])
        attention(P, C, xd, od, kv_all, cq_in, w_uq, w_ukv, w_o, rope_cos, rope_sin, M[L]["G1"], blocks)
        if DEBUG["stop"] == "att%d" % L:
            raise StopBuild(P)
        wg, wu, wd = ffn_w(L)
        ffn_phase(P, C, xd, wg, wu, wd, ud, M[L]["A2"], M[L]["B2"], M[L]["G2"], blocks)
        if DEBUG["stop"] == "ffn%d" % L:
            raise StopBuild(P)

    P.barrier()
    if stage == "B":
        even_layer(0)
        odd_layer_pre(1)
    elif stage == "C":
        odd_layer_post(1, BLOCKS)
        even_layer(2)
        odd_layer_pre(3)
    else:
        odd_layer_post(3, BLOCKS[:3])
        fg_d = P.din("fgain", [128, KC, 2])
        y = P.dout("y", [D, OWN])
        m0 = P.mark()
        fg, _ = P.alloc(F32, KC, 2)
        S.add("sp", (lambda e: e.dma_start(out=fg, in_=fg_d)), writes=("mod",), dma=True)
        hf, hfk = P.alloc(F32, KC, T)
        norm_phase(P, C, xd, fg, None, hf, hfk, BLOCKS[:3])
        yr = y.rearrange("(c p) t -> p c t", p=128)
        for q in range(4):
            S.add("sp", (lambda e, q=q: e.dma_start(out=yr[:, q * 4:(q + 1) * 4, :], in_=hf[:, q * 4:(q + 1) * 4, HALO:HALO + OWN])),
                  reads=(hfk,), writes=("y",), dma=True)
        P.release(m0)
    P.barrier()
    return P


def build_fused():
    P = Prog()
    S = P.S
    C = load_consts(P)
    NS = NCORES
    NJ = 96
    ada_w = P.din("ada_w", [4 * D, 12288])
    ada_b = P.din("ada_b", [128, 4 * NJ])
    cvec = P.din("cvec", [128, KC, 2])
    mod_d = P.dint("mod_d", [128, 4 * NJ, 2])
    m0 = P.mark()
    cs, csk = P.alloc(F32, KC, 2)
    ss, ssk = P.alloc(F32, KC, 2)
    bb, bbk = P.alloc(F32, 4 * NJ)
    res, resk = P.alloc(F32, 4 * NJ, 2)
    load_small(P, cs, csk, cvec)
    load_small(P, bb, bbk, ada_b)
    S.add("act", (lambda e: e.activation(out=ss, in_=cs, func=AF.Silu)), reads=(csk,), writes=(ssk,))
    st = [P.alloc(F32, KC, 128) for _ in range(3)]
    items = [(l, j) for l in range(4) for j in range(NJ)]

    def aload(idx):
        l, j = items[idx]
        sv, sk = st[idx % 3]
        S.add("sp", (lambda e: e.dma_start(out=sv, in_=ada_w[l * D:(l + 1) * D, j * 128:(j + 1) * 128].rearrange("(kc p) n -> p kc n", p=128))),
              writes=(sk,), dma=True)
    aload(0)
    aload(1)
    for idx in range(len(items)):
        if idx + 2 < len(items):
            aload(idx + 2)
        sv, sk = st[idx % 3]
        b = idx % 4
        pk = ("ps", b)
        psap = P.ps[b][:, 0:2]

        def mm(e, sv=sv, psap=psap):
            ins = None
            for kc in range(KC):
                ins = e.matmul(psap, lhsT=sv[:, kc, :], rhs=ss[:, kc, :], start=(kc == 0), stop=(kc == KC - 1))
            return ins
        S.add("pe", mm, reads=(sk, ssk), writes=(pk,))
        S.add("act", (lambda e, psap=psap, idx=idx: e.activation(out=res[:, idx, :], in_=psap, func=AF.Identity,
                                                                 bias=bb[:, idx:idx + 1], scale=1.0)),
              reads=(pk, bbk), writes=(resk,))
    S.add("sp", (lambda e: e.dma_start(out=mod_d, in_=res)), reads=(resk,), writes=("mod_d",), dma=True)
    P.release(m0)
    M = load_mod(P, C, [0, 1, 2, 3], mod_ap=mod_d.rearrange("p (l s c) v -> p l s c v", l=4, s=6))
    x_in = P.din("x_in", [NS, D, T])
    xds = [P.dint("x_state%d" % sl, [D, T]) for sl in range(NS)]
    for sl in range(NS):
        S.add("sp", (lambda e, sl=sl: e.dma_start(out=xds[sl], in_=x_in[sl])), writes=("xd",), dma=True)
    ud = P.dint("u_spill", [DFF, T], BF16)
    od = P.dint("o_spill", [D, T], BF16)
    kvs = {1: P.dint("kv_all_l1", [576, NKEYS], BF16), 3: P.dint("kv_all_l3", [576, NKEYS], BF16)}
    cqd = [P.dint("cq_%d" % sl, [512, T], BF16) for sl in range(NS)]
    rope_cos = P.din("rope_cos", [NS, 64, T])
    rope_sin = P.din("rope_sin", [NS, 64, T])
    mask_d = P.din("mask3", [NS, 128, KC, 2 * HALO])
    invc_d = P.din("invc", [NS, 128, 4, T])
    W = {}

    def win(name, shape):
        if name not in W:
            W[name] = P.din(name, shape)
        return W[name]

    def ffn_w(L):
        return (win("ffn_w_gate_%d" % L, [D, DFF]), win("ffn_w_up_%d" % L, [D, DFF]), win("ffn_w_down_%d" % L, [DFF, D]))

    def blocks_of(sl):
        return BLOCKS if sl == 0 else BLOCKS[:3]

    def even_layer(L, sl):
        i = L // 2
        blocks = blocks_of(sl)
        w_in = win("even_w_in_%d" % i, [D, 4096])
        pool_w = win("pool_w_%d" % i, [4, 256, 256])
        pool_scale = win("pool_scale_%d" % i, [128, 8])
        conv_w = win("conv_w_%d" % i, [128, 3, 8])
        w_out = win("even_w_out_%d" % i, [D, D])
        m0 = P.mark()
        mask3, _ = P.alloc(F32, KC, 2 * HALO)
        S.add("sp", (lambda e: e.dma_start(out=mask3, in_=mask_d[sl])), writes=("const",), dma=True)
        h, hk = P.alloc(BF16, KC, T)
        norm_phase(P, C, xds[sl], M[L]["A1"], M[L]["B1"], h, hk, blocks, mask3=mask3)
        even_mixer(P, C, xds[sl], h, hk, w_in, pool_w, pool_scale, conv_w, w_out, M[L]["G1"], blocks, invc_d[sl])
        P.release(m0)
        wg, wu, wd = ffn_w(L)
        ffn_phase(P, C, xds[sl], wg, wu, wd, ud, M[L]["A2"], M[L]["B2"], M[L]["G2"], blocks)

    def odd_layer_pre(L, sl):
        i = L // 2
        blocks = blocks_of(sl)
        w_dq = win("mla_w_dq_%d" % i, [D, 512])
        qg = win("mla_q_norm_g_%d" % i, [128, 4])
        w_dkv = win("mla_w_dkv_%d" % i, [D, 576])
        kvg = win("mla_kv_norm_g_%d" % i, [128, 4])
        m0 = P.mark()
        h, hk = P.alloc(BF16, KC, T)
        norm_phase(P, C, xds[sl], M[L]["A1"], M[L]["B1"], h, hk, blocks)
        kv_all = kvs[L]
        odd_pre(P, C, h, hk, w_dq, qg, w_dkv, kvg, rope_cos[sl], rope_sin[sl], kv_all[:, sl * OWN:(sl + 1) * OWN],
                kv_all[:, SEQ:SEQ + CTX] if sl == 0 else None, cqd[sl], blocks)
        P.release(m0)

    def odd_layer_post(L, sl, blocks):
        i = L // 2
        w_uq = win("mla_w_uq_%d" % i, [512, 3072])
        w_ukv = win("mla_w_ukv_%d" % i, [512, 4096])
        w_o = win("mla_w_o_%d" % i, [D, D])
        attention(P, C, xds[sl], od, kvs[L], cqd[sl], w_uq, w_ukv, w_o, rope_cos[sl], rope_sin[sl], M[L]["G1"], blocks)
        wg, wu, wd = ffn_w(L)
        ffn_phase(P, C, xds[sl], wg, wu, wd, ud, M[L]["A2"], M[L]["B2"], M[L]["G2"], blocks)

    P.barrier()
    nsl = DEBUG.get("fused_slots", NS)
    for sl in range(nsl):
        even_layer(0, sl)
        odd_layer_pre(1, sl)
    for sl in range(nsl):
        odd_layer_post(1, sl, blocks_of(sl))
        even_layer(2, sl)
        odd_layer_pre(3, sl)
    odd_layer_post(3, 0, BLOCKS[:3])
    fg_d = P.din("fgain", [128, KC, 2])
    y = P.dout("y", [D, OWN])
    m0 = P.mark()
    fg, _ = P.alloc(F32, KC, 2)
    S.add("sp", (lambda e: e.dma_start(out=fg, in_=fg_d)), writes=("mod",), dma=True)
    hf, hfk = P.alloc(F32, KC, T)
    norm_phase(P, C, xds[0], fg, None, hf, hfk, BLOCKS[:3])
    yr = y.rearrange("(c p) t -> p c t", p=128)
    for q in range(4):
        S.add("sp", (lambda e, q=q: e.dma_start(out=yr[:, q * 4:(q + 1) * 4, :], in_=hf[:, q * 4:(q + 1) * 4, HALO:HALO + OWN])),
              reads=(hfk,), writes=("y",), dma=True)
    P.release(m0)
    P.barrier()
    return P


def finish(P):
    nc, S = P.nc, P.S
    with ExitStack() as st:
        S.finalize(nc, st)
        with nc.Block() as block:
            @block.sync
            def _(e):
                S.emit("sp", e)

            @block.scalar
            def _(e):
                S.emit("act", e)

            @block.vector
            def _(e):
                S.emit("dve", e)

            @block.gpsimd
            def _(e):
                S.emit("pool", e)

            @block.tensor
            def _(e):
                S.emit("pe", e)
    P.stack.close()
    return nc


def build_ada():
    P = Prog()
    S = P.S
    NJ = 12288 // NCORES // 128
    w = P.din("ada_w", [4 * D, NJ * 128])
    b_d = P.din("ada_b", [128, 4 * NJ])
    cv = P.din("cvec", [128, KC, 2])
    out_d = P.dout("mod_out", [128, 4 * NJ, 2])
    cs, csk = P.alloc(F32, KC, 2)
    ss, ssk = P.alloc(F32, KC, 2)
    bb, bbk = P.alloc(F32, 4 * NJ)
    res, resk = P.alloc(F32, 4 * NJ, 2)
    load_small(P, cs, csk, cv)
    load_small(P, bb, bbk, b_d)
    S.add("act", (lambda e: e.activation(out=ss, in_=cs, func=AF.Silu)), reads=(csk,), writes=(ssk,))
    st = [P.alloc(F32, KC, 128) for _ in range(3)]
    n = 0
    pend = []

    def load(l, j):
        sv, sk = st[(l * NJ + j) % 3]
        S.add("sp", (lambda e: e.dma_start(out=sv, in_=w[l * D:(l + 1) * D, j * 128:(j + 1) * 128].rearrange("(kc p) n -> p kc n", p=128))),
              writes=(sk,), dma=True)
    items = [(l, j) for l in range(4) for j in range(NJ)]
    load(*items[0])
    load(*items[1])
    for idx, (l, j) in enumerate(items):
        if idx + 2 < len(items):
            load(*items[idx + 2])
        sv, sk = st[idx % 3]
        b = idx % 4
        pk = ("ps", b)
        psap = P.ps[b][:, 0:2]

        def mm(e, sv=sv, psap=psap):
            ins = None
            for kc in range(KC):
                ins = e.matmul(psap, lhsT=sv[:, kc, :], rhs=ss[:, kc, :], start=(kc == 0), stop=(kc == KC - 1))
            return ins
        S.add("pe", mm, reads=(sk, ssk), writes=(pk,))
        S.add("act", (lambda e, psap=psap, idx=idx: e.activation(out=res[:, idx, :], in_=psap, func=AF.Identity,
                                                                 bias=bb[:, idx:idx + 1], scale=1.0)),
              reads=(pk, bbk), writes=(resk,))
    S.add("sp", (lambda e: e.dma_start(out=out_d, in_=res)), reads=(resk,), writes=("out",), dma=True)
    P.barrier()
    return P


_CACHE = {}


def _prog(name):
    if name not in _CACHE:
        P = build_ada() if name == "A" else (build_fused() if name == "F" else build_stage(name))
        _CACHE[name] = finish(P)
    return _CACHE[name]


def _fm(v):
    return np.ascontiguousarray(np.asarray(v).reshape(-1, 128).T)


def _consts(core):
    s = core * OWN - HALO
    pos = np.arange(s, s + WIN)
    valid = (pos >= 0) & (pos < SEQ)
    mask = np.concatenate([valid[:HALO], valid[-HALO:]]).astype(np.float32)
    mask3 = np.ascontiguousarray(np.broadcast_to(mask[None, None, :], (128, KC, 2 * HALO))).astype(np.float32)
    invc = np.ones((4, T), np.float32)
    for g, w in enumerate((2, 4, 8, 16)):
        lo = np.clip(pos - w // 2, 0, SEQ)
        hi = np.clip(pos + (w - w // 2), 0, SEQ)
        cnt = np.maximum(hi - lo, 1).astype(np.float32)
        invc[g, :WIN] = np.float32(1.0) / cnt
        t = np.arange(CTX)
        lo = np.clip(t - w // 2, 0, CTX)
        hi = np.clip(t + (w - w // 2), 0, CTX)
        invc[g, WIN:] = np.float32(1.0) / (hi - lo).astype(np.float32)
    invc = np.ascontiguousarray(np.broadcast_to(invc[None], (128, 4, T))).astype(np.float32)
    p = np.clip(pos, 0, SEQ - 1)
    row = (p // GRID_W).astype(np.float32)
    col = (p % GRID_W).astype(np.float32)
    nf = 16
    inv = np.power(np.float32(10000.0), -np.arange(nf, dtype=np.float32) / np.float32(nf)).astype(np.float32)
    ar = (row[:, None] * inv).astype(np.float32)
    ac = (col[:, None] * inv).astype(np.float32)
    cos = np.ones((64, T), np.float32)
    sin = np.zeros((64, T), np.float32)
    cos[0:16, :WIN] = np.cos(ar).T
    cos[16:32, :WIN] = np.cos(ar).T
    cos[32:48, :WIN] = np.cos(ac).T
    cos[48:64, :WIN] = np.cos(ac).T
    sin[0:16, :WIN] = np.sin(ar).T
    sin[16:32, :WIN] = np.sin(ar).T
    sin[32:48, :WIN] = np.sin(ac).T
    sin[48:64, :WIN] = np.sin(ac).T
    return mask3, invc, cos, sin


def _rot_lhsT():
    Pm = np.zeros((64, 64), np.float32)
    for base in (0, 32):
        for i in range(16):
            Pm[base + i, base + 16 + i] = -1.0
            Pm[base + 16 + i, base + i] = 1.0
    return np.ascontiguousarray(Pm.T).astype(NPBF16)


FUSED = False


def _kernel_fused(x, c, ctx, c_ctx, ada_w, ada_b, norm1_g, norm2_g, even_w_in, pool_w, pool_scale, conv_w, even_w_out,
                  mla_w_dq, mla_q_norm_g, mla_w_uq, mla_w_dkv, mla_kv_norm_g, mla_w_ukv, mla_w_o,
                  ffn_w_gate, ffn_w_up, ffn_w_down, final_norm_g):
    f32 = lambda a: np.ascontiguousarray(np.asarray(a, dtype=np.float32))
    cores = list(range(NCORES))
    NJ = 96
    cvec = np.stack([_fm(c[0]), _fm(c_ctx)], axis=-1).astype(np.float32)
    ada_w2 = f32(np.asarray(ada_w)).reshape(4 * D, 12288)
    ada_b2 = np.ascontiguousarray(np.asarray(ada_b, dtype=np.float32).reshape(4, NJ, 128).transpose(2, 0, 1).reshape(128, 4 * NJ))
    ngt = np.stack([np.stack([_fm(norm1_g[l]), _fm(norm2_g[l])], 0) for l in range(4)], 0)
    ngt = np.ascontiguousarray(ngt.transpose(2, 0, 1, 3)).astype(np.float32)
    ones = np.ones((128, 128), NPBF16)
    rot = _rot_lhsT()
    consts = [_consts(k) for k in cores]
    xp = np.zeros((SEQ + 2 * HALO, D), np.float32)
    xp[HALO:HALO + SEQ] = x[0]
    wins = [np.ascontiguousarray(np.concatenate([xp[k * OWN:k * OWN + WIN], ctx[0]], axis=0).T) for k in cores]
    Wd = {}
    for nm, arr in (("even_w_in", even_w_in), ("pool_w", pool_w), ("pool_scale", pool_scale), ("conv_w", conv_w),
                    ("even_w_out", even_w_out), ("mla_w_dq", mla_w_dq), ("mla_q_norm_g", mla_q_norm_g), ("mla_w_uq", mla_w_uq),
                    ("mla_w_dkv", mla_w_dkv), ("mla_kv_norm_g", mla_kv_norm_g), ("mla_w_ukv", mla_w_ukv), ("mla_w_o", mla_w_o),
                    ("ffn_w_gate", ffn_w_gate), ("ffn_w_up", ffn_w_up), ("ffn_w_down", ffn_w_down)):
        arr = np.asarray(arr)
        for i in range(arr.shape[0]):
            a = f32(arr[i])
            if nm in ("pool_scale", "mla_q_norm_g", "mla_kv_norm_g"):
                a = _fm(a)
            elif nm == "conv_w":
                a = np.ascontiguousarray(a.reshape(3, 8, 128).transpose(2, 0, 1))
            Wd["%s_%d" % (nm, i)] = a
    fgain = np.ascontiguousarray(np.broadcast_to(_fm(final_norm_g)[:, :, None], (128, KC, 2))).astype(np.float32)
    pf = _prog("F")
    ins = []
    for k in cores:
        order = [(k + sl) % NCORES for sl in range(NCORES)]
        d = {"c_ones": ones, "c_rot": rot, "ada_w": ada_w2, "ada_b": ada_b2, "cvec": cvec, "ng": ngt,
             "x_in": np.stack([wins[w] for w in order], 0),
             "rope_cos": np.stack([consts[w][2] for w in order], 0), "rope_sin": np.stack([consts[w][3] for w in order], 0),
             "mask3": np.stack([consts[w][0] for w in order], 0), "invc": np.stack([consts[w][1] for w in order], 0),
             "fgain": fgain}
        d.update(Wd)
        ins.append(d)
    rf = run_bass_kernel_spmd(pf, ins, core_ids=cores).results
    out = np.concatenate([np.asarray(rf[k]["y"]).T for k in cores], axis=0)
    return np.ascontiguousarray(out[None].astype(np.float32))


def kernel(x, c, ctx, c_ctx, ada_w, ada_b, norm1_g, norm2_g, even_w_in, pool_w, pool_scale, conv_w, even_w_out,
           mla_w_dq, mla_q_norm_g, mla_w_uq, mla_w_dkv, mla_kv_norm_g, mla_w_ukv, mla_w_o,
           ffn_w_gate, ffn_w_up, ffn_w_down, final_norm_g):
    f32 = lambda a: np.ascontiguousarray(np.asarray(a, dtype=np.float32))
    x, c, ctx, c_ctx = f32(x), f32(c), f32(ctx), f32(c_ctx)
    cores = list(range(NCORES))
    if FUSED:
        return _kernel_fused(x, c, ctx, c_ctx, ada_w, ada_b, norm1_g, norm2_g, even_w_in, pool_w, pool_scale, conv_w,
                             even_w_out, mla_w_dq, mla_q_norm_g, mla_w_uq, mla_w_dkv, mla_kv_norm_g, mla_w_ukv, mla_w_o,
                             ffn_w_gate, ffn_w_up, ffn_w_down, final_norm_g)
    NJ = 12
    cvec = np.stack([_fm(c[0]), _fm(c_ctx)], axis=-1).astype(np.float32)
    ada_w = np.asarray(ada_w)
    ada_b = np.asarray(ada_b)
    inA = []
    for k in cores:
        wk = np.ascontiguousarray(ada_w[:, :, k * 1536:(k + 1) * 1536]).reshape(4 * D, 1536)
        bk = np.ascontiguousarray(ada_b[:, k * 1536:(k + 1) * 1536].reshape(4, NJ, 128).transpose(2, 0, 1).reshape(128, 4 * NJ))
        inA.append({"ada_w": wk, "ada_b": bk, "cvec": cvec, "c_ones": np.ones((128, 128), NPBF16), "c_rot": _rot_lhsT()})
    pa = _prog("A")
    ra = run_bass_kernel_spmd(pa, [{k: v for k, v in m.items() if k in ("ada_w", "ada_b", "cvec")} for m in inA], core_ids=cores)
    modf = np.zeros((4, 12288, 2), np.float32)
    for k in cores:
        r = np.asarray(ra.results[k]["mod_out"]).reshape(128, 4, NJ, 2)
        modf[:, k * 1536:(k + 1) * 1536, :] = r.transpose(1, 2, 0, 3).reshape(4, 1536, 2)
    modt = np.ascontiguousarray(modf.reshape(4, 6, KC, 128, 2).transpose(3, 0, 1, 2, 4))
    ngt = np.stack([np.stack([_fm(norm1_g[l]), _fm(norm2_g[l])], 0) for l in range(4)], 0)
    ngt = np.ascontiguousarray(ngt.transpose(2, 0, 1, 3)).astype(np.float32)

    ones = np.ones((128, 128), NPBF16)
    rot = _rot_lhsT()
    consts = [_consts(k) for k in cores]
    xs = []
    xp = np.zeros((SEQ + 2 * HALO, D), np.float32)
    xp[HALO:HALO + SEQ] = x[0]
    for k in cores:
        st = np.concatenate([xp[k * OWN:k * OWN + WIN], ctx[0]], axis=0)
        xs.append(np.ascontiguousarray(st.T))

    def common(k, layers):
        mask3, invc, cos, sin = consts[k]
        return {"c_ones": ones, "c_rot": rot, "mod": np.ascontiguousarray(modt[:, layers]),
                "ng": np.ascontiguousarray(ngt[:, layers]), "x_in": xs[k], "rope_cos": cos, "rope_sin": sin,
                "mask3": mask3, "invc": invc}

    def lw(names_idx):
        d = {}
        for nm, arr, i in names_idx:
            a = f32(np.asarray(arr)[i])
            if nm in ("pool_scale", "mla_q_norm_g", "mla_kv_norm_g"):
                a = _fm(a)
            elif nm == "conv_w":
                a = np.ascontiguousarray(a.reshape(3, 8, 128).transpose(2, 0, 1))
            d["%s_%d" % (nm, i)] = a
        return d

    def gather_kv(res):
        lat = np.concatenate([np.asarray(res[k]["kv_out"])[:, :OWN] for k in cores], axis=1)
        return np.ascontiguousarray(np.concatenate([lat, np.asarray(res[0]["kv_out"])[:, OWN:]], axis=1))

    wB = lw([("even_w_in", even_w_in, 0), ("pool_w", pool_w, 0), ("pool_scale", pool_scale, 0), ("conv_w", conv_w, 0),
             ("even_w_out", even_w_out, 0), ("ffn_w_gate", ffn_w_gate, 0), ("ffn_w_up", ffn_w_up, 0),
             ("ffn_w_down", ffn_w_down, 0), ("mla_w_dq", mla_w_dq, 0), ("mla_q_norm_g", mla_q_norm_g, 0),
             ("mla_w_dkv", mla_w_dkv, 0), ("mla_kv_norm_g", mla_kv_norm_g, 0)])
    pb = _prog("B")
    rb = run_bass_kernel_spmd(pb, [dict(common(k, [0, 1]), **wB) for k in cores], core_ids=cores).results
    xs = [np.asarray(rb[k]["x_out"]) for k in cores]
    kv_all = gather_kv(rb)
    cqs = [np.asarray(rb[k]["cq_out"]) for k in cores]
    del wB
    wC = lw([("mla_w_uq", mla_w_uq, 0), ("mla_w_ukv", mla_w_ukv, 0), ("mla_w_o", mla_w_o, 0),
             ("ffn_w_gate", ffn_w_gate, 1), ("ffn_w_up", ffn_w_up, 1), ("ffn_w_down", ffn_w_down, 1),
             ("even_w_in", even_w_in, 1), ("pool_w", pool_w, 1), ("pool_scale", pool_scale, 1), ("conv_w", conv_w, 1),
             ("even_w_out", even_w_out, 1), ("ffn_w_gate", ffn_w_gate, 2), ("ffn_w_up", ffn_w_up, 2),
             ("ffn_w_down", ffn_w_down, 2), ("mla_w_dq", mla_w_dq, 1), ("mla_q_norm_g", mla_q_norm_g, 1),
             ("mla_w_dkv", mla_w_dkv, 1), ("mla_kv_norm_g", mla_kv_norm_g, 1)])
    pc = _prog("C")
    rc = run_bass_kernel_spmd(pc, [dict(common(k, [1, 2, 3]), kv_all=kv_all, cq_in=cqs[k], **wC) for k in cores],
                              core_ids=cores).results
    xs = [np.asarray(rc[k]["x_out"]) for k in cores]
    kv_all = gather_kv(rc)
    cqs = [np.asarray(rc[k]["cq_out"]) for k in cores]
    del wC
    wD = lw([("mla_w_uq", mla_w_uq, 1), ("mla_w_ukv", mla_w_ukv, 1), ("mla_w_o", mla_w_o, 1),
             ("ffn_w_gate", ffn_w_gate, 3), ("ffn_w_up", ffn_w_up, 3), ("ffn_w_down", ffn_w_down, 3)])
    fgain = np.ascontiguousarray(np.broadcast_to(_fm(final_norm_g)[:, :, None], (128, KC, 2))).astype(np.float32)
    pd = _prog("D")
    inD = []
    for k in cores:
        m = common(k, [3])
        for nm in ("mask3", "invc"):
            m.pop(nm)
        m.update(kv_all=kv_all, cq_in=cqs[k], fgain=fgain, **wD)
        inD.append(m)
    rd = run_bass_kernel_spmd(pd, inD, core_ids=cores).results
    out = np.concatenate([np.asarray(rd[k]["y"]).T for k in cores], axis=0)
    return np.ascontiguousarray(out[None].astype(np.float32))
```
